# Optimizing a Trainium2 kernel written in Bass

```python
import jax
import jax.numpy as jnp
from jax import lax
import numpy as np

D_MODEL = 1024
BATCH = 16
SEQ = 2048
DEPTH = 1

CTX_LEN = 256
GRID_W = 64
N_HEADS_A = 8
HEAD_DIM_K = 128
HEAD_DIM_V = 128
KEY_DIM = N_HEADS_A * HEAD_DIM_K
VAL_DIM = N_HEADS_A * HEAD_DIM_V
QKV_DIM = 2 * KEY_DIM + VAL_DIM
CHUNK = 64
N_DIR = 2
CONV_K = 3
CONV_PAD = CONV_K // 2
CONV_DIM = D_MODEL
D_FF = 4 * D_MODEL
DN_ALPHA = (2.0 * DEPTH) ** 0.25
DN_BETA = (8.0 * DEPTH) ** -0.25
LN_EPS = 1e-5
RMS_EPS = 1e-6
L2_EPS = 1e-6
SPLIT_SIZES = (QKV_DIM, N_DIR * N_HEADS_A, N_DIR * N_HEADS_A, VAL_DIM,
               CONV_DIM, CONV_DIM, CONV_DIM, D_MODEL, D_MODEL)
STATE_COLS = QKV_DIM + 2 * N_DIR * N_HEADS_A
P_TOTAL = QKV_DIM + 2 * N_DIR * N_HEADS_A + VAL_DIM + 3 * CONV_DIM + 2 * D_MODEL

kernel_name = "hybrid_gdn_shortconv_dit_block"


def split_proj(p, sizes):
    idx, acc = [], 0
    for s in sizes[:-1]:
        acc += s
        idx.append(acc)
    return jnp.split(p, idx, axis=-1)


def layer_norm(t, g, b):
    t32 = t.astype(jnp.float32)
    mu = jnp.mean(t32, axis=-1, keepdims=True)
    var = jnp.mean(jnp.square(t32 - mu), axis=-1, keepdims=True)
    out = (t32 - mu) * lax.rsqrt(var + LN_EPS) * g.astype(jnp.float32) + b.astype(jnp.float32)
    return out.astype(t.dtype)


def l2norm(t):
    return t * lax.rsqrt(jnp.sum(t * t, axis=-1, keepdims=True) + L2_EPS)


def conv_seq(h, w):
    length = h.shape[1]
    hp = jnp.pad(h, ((0, 0), (CONV_PAD, CONV_PAD), (0, 0)))
    return sum(hp[:, j:j + length] * w[j] for j in range(CONV_K))


def conv_rows(h, w):
    b, length, ch = h.shape
    rows = length // GRID_W
    return conv_seq(h.reshape(b * rows, GRID_W, ch), w).reshape(b, length, ch)


def gated_delta_chunked(q, k, v, beta, g, state0, with_output):
    b, length, h, dk = q.shape
    dv = v.shape[-1]
    n = length // CHUNK

    def chunks(t):
        t = t.reshape(b, n, CHUNK, h, *t.shape[3:])
        return jnp.swapaxes(jnp.moveaxis(t, 1, 0), 2, 3)

    qc, kc, vc = chunks(q), chunks(k), chunks(v)
    bc, gc = chunks(beta), chunks(g)
    G = jnp.cumsum(gc, axis=-1)
    pos = jnp.arange(CHUNK)
    incl = pos[:, None] >= pos[None, :]
    strict = pos[:, None] > pos[None, :]
    decay = jnp.exp(jnp.where(incl, G[..., :, None] - G[..., None, :], -jnp.inf))
    kb = kc * bc[..., None]
    m = jnp.where(strict, jnp.einsum('nbhik,nbhjk->nbhij', kb, kc) * decay, 0.0)
    eye = jnp.eye(CHUNK, dtype=m.dtype)
    t_inv = lax.linalg.triangular_solve(eye + m, jnp.broadcast_to(eye, m.shape),
                                        left_side=True, lower=True)
    u = jnp.einsum('nbhij,nbhjd->nbhid', t_inv, vc * bc[..., None])
    w = jnp.einsum('nbhij,nbhjk->nbhik', t_inv, kb * jnp.exp(G)[..., None])
    g_last = G[..., -1]
    k_tail = kc * jnp.exp(g_last[..., None] - G)[..., None]
    if with_output:
        a_intra = jnp.einsum('nbhik,nbhjk->nbhij', qc, kc) * decay
        q_dec = qc * jnp.exp(G)[..., None]
        xs = (u, w, k_tail, g_last, q_dec, a_intra)
    else:
        xs = (u, w, k_tail, g_last)

    def step(s, inp):
        u_c, w_c, kt_c, gl_c = inp[:4]
        v_new = u_c - jnp.einsum('bhck,bhkd->bhcd', w_c, s)
        if with_output:
            qd_c, a_c = inp[4:]
            o = (jnp.einsum('bhck,bhkd->bhcd', qd_c, s)
                 + jnp.einsum('bhij,bhjd->bhid', a_c, v_new))
        else:
            o = None
        s = s * jnp.exp(gl_c)[..., None, None] + jnp.einsum('bhck,bhcd->bhkd', kt_c, v_new)
        return s, o

    s_final, o = lax.scan(step, state0, xs)
    if with_output:
        o = jnp.moveaxis(jnp.swapaxes(o, 2, 3), 0, 1).reshape(b, length, h, dv)
    return o, s_final


def delta_prepare(qkv, beta_raw, a_raw, conv_w, a_log, dt_bias, conv_fn):
    b, length, _ = qkv.shape
    qkv = jax.nn.silu(conv_fn(qkv, conv_w)).astype(jnp.float32)
    q, k, v = jnp.split(qkv, [KEY_DIM, 2 * KEY_DIM], axis=-1)
    q = l2norm(q.reshape(b, length, N_HEADS_A, HEAD_DIM_K)) * HEAD_DIM_K ** -0.5
    k = l2norm(k.reshape(b, length, N_HEADS_A, HEAD_DIM_K))
    v = v.reshape(b, length, N_HEADS_A, HEAD_DIM_V)
    beta = jax.nn.sigmoid(beta_raw.astype(jnp.float32)).reshape(b, length, N_DIR, N_HEADS_A)
    g = -jnp.exp(a_log.astype(jnp.float32)) * jax.nn.softplus(
        a_raw.astype(jnp.float32).reshape(b, length, N_DIR, N_HEADS_A) + dt_bias.astype(jnp.float32))
    return q, k, v, beta, g


def delta_bidir(q, k, v, beta, g, states0, with_output):
    outs, finals = [], []
    for d in range(N_DIR):
        rev = (lambda t: jnp.flip(t, axis=1)) if d == 1 else (lambda t: t)
        o, s = gated_delta_chunked(rev(q), rev(k), rev(v), rev(beta[:, :, d]), rev(g[:, :, d]),
                                   states0[d], with_output)
        outs.append(rev(o) if with_output else None)
        finals.append(s)
    o = outs[0] + outs[1] if with_output else None
    return o, finals


def delta_out(o, z, norm_w, w_out):
    b, length = o.shape[:2]
    o = o * lax.rsqrt(jnp.mean(o * o, axis=-1, keepdims=True) + RMS_EPS) * norm_w.astype(jnp.float32)
    o = o.astype(z.dtype).reshape(b, length, VAL_DIM) * jax.nn.silu(z)
    return o @ w_out


def short_conv_branch(xb, b_gate, c_gate, conv_w, w_out, conv_fn):
    return (b_gate * conv_fn(c_gate * xb, conv_w)) @ w_out


def sq_relu_mlp(u, w1, b1, w2, b2):
    return jnp.square(jax.nn.relu(u @ w1 + b1)) @ w2 + b2


def hybrid_layer(x, ctx, c, c_ctx, w_ada, b_ada, w_in, conv_qkv, a_log, dt_bias, o_norm_w,
                 w_a_out, conv_b, w_b_out, w_o, ln1_g, ln1_b, w_ff1, b_ff1, w_ff2, b_ff2,
                 ln2_g, ln2_b, update_ctx):
    b = x.shape[0]
    mod = jax.nn.silu(c) @ w_ada + b_ada
    sh1, sc1, g1, sh2, sc2, g2 = [t[:, None, :] for t in jnp.split(mod, 6, axis=-1)]
    mod_c = jax.nn.silu(c_ctx) @ w_ada + b_ada
    sh1c, sc1c, g1c, sh2c, sc2c, g2c = jnp.split(mod_c, 6, axis=-1)

    u = x * (1.0 + sc1) + sh1
    uc = ctx * (1.0 + sc1c) + sh1c
    (qkv, br, ar, z, xb, b_gate, c_gate, ga_raw, gb_raw) = split_proj(u @ w_in, SPLIT_SIZES)
    if update_ctx:
        pc = split_proj(uc @ w_in, SPLIT_SIZES)
    else:
        pc = split_proj(uc @ w_in[:, :STATE_COLS], SPLIT_SIZES[:3])
    qkv_c, br_c, ar_c = pc[:3]

    zero_state = jnp.zeros((b, N_HEADS_A, HEAD_DIM_K, HEAD_DIM_V), jnp.float32)
    qc_, kc_, vc_, bc_, gc_ = delta_prepare(qkv_c, br_c, ar_c, conv_qkv, a_log, dt_bias, conv_seq)
    o_ctx, ctx_states = delta_bidir(qc_, kc_, vc_, bc_, gc_, (zero_state, zero_state), update_ctx)
    ql, kl, vl, bl, gl = delta_prepare(qkv, br, ar, conv_qkv, a_log, dt_bias, conv_rows)
    o_lat, _ = delta_bidir(ql, kl, vl, bl, gl, ctx_states, True)

    y_a = delta_out(o_lat, z, o_norm_w, w_a_out)
    y_b = short_conv_branch(xb, b_gate, c_gate, conv_b, w_b_out, conv_rows)
    y = (jax.nn.sigmoid(ga_raw) * y_a + jax.nn.sigmoid(gb_raw) * y_b) @ w_o
    x = layer_norm(DN_ALPHA * x + g1 * y, ln1_g, ln1_b)

    u2 = x * (1.0 + sc2) + sh2
    x = layer_norm(DN_ALPHA * x + g2 * sq_relu_mlp(u2, w_ff1, b_ff1, w_ff2, b_ff2), ln2_g, ln2_b)

    if update_ctx:
        z_c, xb_c, bg_c, cg_c, ga_c, gbr_c = pc[3:]
        yc_a = delta_out(o_ctx, z_c, o_norm_w, w_a_out)
        yc_b = short_conv_branch(xb_c, bg_c, cg_c, conv_b, w_b_out, conv_seq)
        yc = (jax.nn.sigmoid(ga_c) * yc_a + jax.nn.sigmoid(gbr_c) * yc_b) @ w_o
        ctx = layer_norm(DN_ALPHA * ctx + g1c * yc, ln1_g, ln1_b)
        uc2 = ctx * (1.0 + sc2c) + sh2c
        ctx = layer_norm(DN_ALPHA * ctx + g2c * sq_relu_mlp(uc2, w_ff1, b_ff1, w_ff2, b_ff2),
                         ln2_g, ln2_b)
    return x, ctx


def setup_inputs(seed: int = 0) -> dict:
    key = jax.random.key(seed)
    ks = jax.random.split(key, 26)

    def nrm(k, shape, scale):
        return jax.random.normal(k, shape, jnp.float32) * scale

    d = D_MODEL
    dt = jnp.exp(jax.random.uniform(ks[9], (DEPTH, N_DIR, N_HEADS_A), jnp.float32,
                                    minval=np.log(1e-3), maxval=np.log(1e-1)))
    return {
        "x": nrm(ks[0], (BATCH, SEQ, d), 1.0),
        "c": nrm(ks[1], (BATCH, d), 1.0),
        "ctx": nrm(ks[2], (BATCH, CTX_LEN, d), 1.0),
        "c_ctx": nrm(ks[3], (d,), 1.0),
        "w_ada": nrm(ks[4], (DEPTH, d, 6 * d), 0.5 * d ** -0.5),
        "b_ada": nrm(ks[5], (DEPTH, 6 * d), 0.02),
        "w_in": nrm(ks[6], (DEPTH, d, P_TOTAL), d ** -0.5),
        "conv_qkv": nrm(ks[7], (DEPTH, CONV_K, QKV_DIM), CONV_K ** -0.5),
        "a_log": jnp.log(jax.random.uniform(ks[8], (DEPTH, N_DIR, N_HEADS_A), jnp.float32,
                                             minval=1.0, maxval=16.0)),
        "dt_bias": jnp.log(jnp.expm1(dt)),
        "o_norm_w": 1.0 + nrm(ks[10], (DEPTH, HEAD_DIM_V), 0.02),
        "w_a_out": nrm(ks[11], (DEPTH, VAL_DIM, d), VAL_DIM ** -0.5 * DN_BETA),
        "conv_b": nrm(ks[12], (DEPTH, CONV_K, CONV_DIM), CONV_K ** -0.5),
        "w_b_out": nrm(ks[13], (DEPTH, CONV_DIM, d), CONV_DIM ** -0.5 * DN_BETA),
        "w_o": nrm(ks[14], (DEPTH, d, d), d ** -0.5 * DN_BETA),
        "ln1_g": 1.0 + nrm(ks[15], (DEPTH, d), 0.02),
        "ln1_b": nrm(ks[16], (DEPTH, d), 0.02),
        "w_ff1": nrm(ks[17], (DEPTH, d, D_FF), d ** -0.5),
        "b_ff1": nrm(ks[18], (DEPTH, D_FF), 0.02),
        "w_ff2": nrm(ks[19], (DEPTH, D_FF, d), D_FF ** -0.5 * DN_BETA),
        "b_ff2": nrm(ks[20], (DEPTH, d), 0.02),
        "ln2_g": 1.0 + nrm(ks[21], (DEPTH, d), 0.02),
        "ln2_b": nrm(ks[22], (DEPTH, d), 0.02),
    }


def reference(x, c, ctx, c_ctx, w_ada, b_ada, w_in, conv_qkv, a_log, dt_bias, o_norm_w,
              w_a_out, conv_b, w_b_out, w_o, ln1_g, ln1_b, w_ff1, b_ff1, w_ff2, b_ff2,
              ln2_g, ln2_b):
    for i in range(DEPTH):
        x, ctx = hybrid_layer(
            x, ctx, c, c_ctx,
            w_ada=w_ada[i], b_ada=b_ada[i], w_in=w_in[i], conv_qkv=conv_qkv[i],
            a_log=a_log[i], dt_bias=dt_bias[i], o_norm_w=o_norm_w[i], w_a_out=w_a_out[i],
            conv_b=conv_b[i], w_b_out=w_b_out[i], w_o=w_o[i], ln1_g=ln1_g[i], ln1_b=ln1_b[i],
            w_ff1=w_ff1[i], b_ff1=b_ff1[i], w_ff2=w_ff2[i], b_ff2=b_ff2[i],
            ln2_g=ln2_g[i], ln2_b=ln2_b[i], update_ctx=(i < DEPTH - 1))
    return x
```

```python
import numpy as np
from contextlib import ExitStack
import concourse.bass as bass
import concourse.mybir as mybir
from concourse.bass_utils import run_bass_kernel_spmd

F32 = mybir.dt.float32
BF16 = mybir.dt.bfloat16
AF = mybir.ActivationFunctionType
ALU = mybir.AluOpType
AX = mybir.AxisListType

ENGS = ("pe", "act", "dve", "pool", "sp")
N_LANES = 24

D = 1024
SEQ = 2048
CTXL = 256
NTOK = SEQ + CTXL
NBLK = NTOK // 128
NH = 8
P_TOTAL = 9248
ALPHA = 2.0 ** 0.25
LN_EPS = 1e-5
RMS_EPS = 1e-6
L2_EPS = 1e-6
TILES = [(0, 256), (256, 512), (768, 512), (1280, 512), (1792, 512)]
ORDER = [list(range(18)), [1, 0] + list(range(17, 1, -1))]
NCONST = 10
BIG = 30000.0


class Buf:
    __slots__ = ("name", "w", "r", "excl")

    def __init__(self, name, excl=False):
        self.name = name
        self.w = None
        self.r = []
        self.excl = excl


class Op:
    __slots__ = ("eng", "fn", "deps", "idx", "sig", "tok", "is_dma")

    def __init__(self, eng, fn):
        self.eng = eng
        self.fn = fn
        self.deps = []
        self.sig = False
        self.tok = None
        self.is_dma = False


class Sched:
    def __init__(self, nc):
        self.nc = nc
        self.ops = {e: [] for e in ENGS}
        self.lane_cnt = [0] * N_LANES
        self.lane_last = [None] * N_LANES
        self.next_lane = 0

    def _track(self, op, reads, writes):
        deps = []
        rd, wr = [], []
        for b in reads:
            (wr if b.excl else rd).append(b)
        wr.extend(writes)
        for b in rd:
            if b.w is not None:
                deps.append(b.w)
        for b in wr:
            if b.w is not None:
                deps.append(b.w)
            deps.extend(b.r)
        for b in rd:
            b.r.append(op)
        for b in wr:
            b.w = op
            b.r = []
        seen = set()
        for d in deps:
            if d is op or id(d) in seen:
                continue
            seen.add(id(d))
            if d.eng == "pe" and op.eng == "pe" and not d.is_dma and not op.is_dma:
                continue
            op.deps.append(d)

    def op(self, eng, fn, reads=(), writes=()):
        o = Op(eng, fn)
        self._track(o, reads, writes)
        self.ops[eng].append(o)
        return o

    def dma(self, fn, reads=(), writes=(), eng="sp"):
        o = Op(eng, fn)
        o.is_dma = True
        lane = self.next_lane
        self.next_lane = (self.next_lane + 1) % N_LANES
        prev = self.lane_last[lane]
        self._track(o, reads, writes)
        if prev is not None and all(d is not prev for d in o.deps):
            o.deps.append(prev)
        self.lane_cnt[lane] += 16
        o.tok = ("lane", lane, self.lane_cnt[lane])
        self.lane_last[lane] = o
        self.ops[eng].append(o)
        return o

    def barrier(self):
        lasts = []
        for e in ENGS:
            for o in reversed(self.ops[e]):
                if not o.is_dma:
                    lasts.append(o)
                    break
        lasts.extend(o for o in self.lane_last if o is not None)
        for e in ENGS:
            o = Op(e, lambda eng: eng.nop())
            o.deps = [d for d in lasts if not (d.eng == e and not d.is_dma)]
            self.ops[e].append(o)

    def run(self):
        nc = self.nc
        for e in ENGS:
            for o in self.ops[e]:
                for d in o.deps:
                    if not d.is_dma:
                        d.sig = True
        for e in ENGS:
            c = 0
            for o in self.ops[e]:
                if o.is_dma:
                    continue
                if o.sig:
                    c += 1
                    o.tok = ("eng", e, c)
        with ExitStack() as es:
            esem = {e: es.enter_context(nc.semaphore("s_" + e)) for e in ENGS}
            lsem = [es.enter_context(nc.semaphore("l_%d" % i)) for i in range(N_LANES)]
            block = es.enter_context(nc.Block())

            def body(ename, final=False):
                def _f(eng):
                    waited = {}
                    for o in self.ops[ename]:
                        need = {}
                        for d in o.deps:
                            k = (d.tok[0], d.tok[1])
                            if d.tok[2] > need.get(k, 0):
                                need[k] = d.tok[2]
                        for k, v in need.items():
                            if waited.get(k, 0) >= v:
                                continue
                            eng.wait_ge(esem[k[1]] if k[0] == "eng" else lsem[k[1]], v)
                            waited[k] = v
                        ins = o.fn(eng)
                        if o.is_dma:
                            ins.then_inc(lsem[o.tok[1]], 16)
                        elif o.sig:
                            ins.then_inc(esem[ename], 1)
                    if final:
                        for lane in range(N_LANES):
                            if self.lane_cnt[lane] > 0:
                                eng.wait_ge(lsem[lane], self.lane_cnt[lane])
                return _f

            block.tensor(body("pe"))
            block.scalar(body("act"))
            block.vector(body("dve"))
            block.gpsimd(body("pool"))
            block.sync(body("sp", final=True))


class T:
    __slots__ = ("ap", "b")

    def __init__(self, ap, b):
        self.ap = ap
        self.b = b


class Arena:
    def __init__(self, tens, nf32):
        self.t = tens
        self.cap = nf32 * 4
        self.top = 0
        self.peak = 0

    def alloc(self, free_shape, dt, name, at=None):
        n = int(np.prod(free_shape))
        esz = 4 if dt == F32 else 2
        nbytes = (n * esz + 3) // 4 * 4
        if at is None:
            off = (self.top + 63) // 64 * 64
            self.top = off + nbytes
            self.peak = max(self.peak, self.top)
            assert self.top <= self.cap, ("SBUF arena overflow", name, self.top, self.cap)
        else:
            off = at
        ap = self.t[:, off // 4: off // 4 + nbytes // 4]
        if dt == BF16:
            ap = ap.bitcast(BF16)
        ap = ap[:, 0:n]
        if len(free_shape) == 2:
            ap = ap.rearrange("p (a b) -> p a b", a=free_shape[0])
        elif len(free_shape) == 3:
            ap = ap.rearrange("p (a b c) -> p a b c", a=free_shape[0], b=free_shape[1])
        return T(ap, Buf(name))


class Pool:
    def __init__(self, arena, n, free_shape, dt, name):
        self.items = [arena.alloc(free_shape, dt, "%s%d" % (name, i)) for i in range(n)]
        self.i = 0

    def get(self):
        t = self.items[self.i]
        self.i = (self.i + 1) % len(self.items)
        return t


def build_program(debug=None, stage=99, nbatch=2):
    nc = bass.Bass("TRN2", target_bir_lowering=False)
    S = Sched(nc)
    dbg_outs = {}

    def din(name, shape, dt=F32):
        return nc.dram_tensor(name, list(shape), dt, kind="ExternalInput").ap()

    x_d = din("x", [2, SEQ, D])
    ctx_d = din("ctx", [2, CTXL, D])
    cT_d = din("cT", [128, 24])
    wada_d = din("w_ada", [D, 6 * D])
    badaT_d = din("b_adaT", [128, 48])
    win_d = din("w_in", [D, P_TOTAL])
    cwq_d = din("cwq", [128, 72])
    alog_d = din("alog_bc", [128, 16])
    dtb_d = din("dtb_bc", [128, 16])
    onw_d = din("onw_col", [128, 1])
    wa_d = din("w_a_out", [D, D])
    cwb_d = din("cwb", [128, 24])
    wb_d = din("w_b_out", [D, D])
    wo_d = din("w_o", [D, D])
    ln1g_d = din("ln1gT", [128, 8])
    ln1b_d = din("ln1bT", [128, 8])
    wf1_d = din("w_ff1", [D, 4 * D])
    bf1_d = din("b_ff1T", [128, 32])
    wf2_d = din("w_ff2", [4 * D, D])
    bf2_d = din("b_ff2T", [128, 8])
    ln2g_d = din("ln2gT", [128, 8])
    ln2b_d = din("ln2bT", [128, 8])
    const_d = din("consts", [128, NCONST * 128])
    out_d = nc.dram_tensor("out", [2, SEQ, D], F32, kind="ExternalOutput").ap()

    def dscr(name, nchunk, K):
        t = nc.dram_tensor(name, [nchunk * 128, K * 128], BF16, kind="Internal").ap()
        return t, [Buf("%s_%d" % (name, i)) for i in range(nchunk)]

    sc_in, sb_in = dscr("sc_in", 72, 8)
    sc_bg_t = nc.dram_tensor("sc_bg", [128, 256], BF16, kind="Internal").ap()
    sb_bg = Buf("sc_bg")
    sc_a, sb_a = dscr("sc_a", 8, 8)
    sc_b, sb_b = dscr("sc_b", 8, 8)
    sc_o, sb_o = dscr("sc_o", 8, 8)
    sc_f1, sb_f1 = dscr("sc_f1", 32, 8)
    sc_f2, sb_f2 = dscr("sc_f2", 8, 32)

    def dbg(name, t, shape, dt=F32):
        if not debug or name not in debug:
            return
        o = nc.dram_tensor("dbg_" + name, list(shape), dt, kind="ExternalOutput").ap()
        dbg_outs[name] = o
        ap = t.ap
        S.dma(lambda e: e.dma_start(out=o, in_=ap), reads=[t.b])

    es = ExitStack()
    NF32 = 53200
    arena_t = es.enter_context(nc.sbuf_tensor("arena", [128, NF32], F32))
    AR = Arena(arena_t, NF32)
    banks = []
    for i in range(8):
        pt = es.enter_context(nc.psum_tensor("ps%d" % i, [128, 512], F32))
        banks.append(T(pt[:, :], Buf("ps%d" % i, excl=True)))
    rr = [0]
    rr_lo = [0]

    def nextbank():
        if rr[0] < rr_lo[0]:
            rr[0] = rr_lo[0]
        b = banks[rr[0]]
        rr[0] += 1
        if rr[0] >= 8:
            rr[0] = rr_lo[0]
        return b

    def MM(out, lhsT, rhs, start=True, stop=True, r=(), w=()):
        S.op("pe", lambda e: e.matmul(out, lhsT=lhsT, rhs=rhs, start=start, stop=stop), r, w)

    def ACT(out, in_, func, r=(), w=(), bias=None, scale=1.0):
        if bias is None:
            S.op("act", lambda e: e.activation(out=out, in_=in_, func=func, scale=scale), r, w)
        else:
            S.op("act", lambda e: e.activation(out=out, in_=in_, func=func, bias=bias, scale=scale), r, w)

    def TTop(eng, out, in0, in1, op, r=(), w=()):
        S.op(eng, lambda e: e.tensor_tensor(out=out, in0=in0, in1=in1, op=op), r, w)

    def STT(eng, out, in0, scalar, in1, op0, op1, r=(), w=()):
        S.op(eng, lambda e: e.scalar_tensor_tensor(out=out, in0=in0, scalar=scalar, in1=in1, op0=op0, op1=op1), r, w)

    def TS(eng, out, in0, s1, s2, op0, op1=None, r=(), w=()):
        if op1 is None:
            S.op(eng, lambda e: e.tensor_scalar(out=out, in0=in0, scalar1=s1, scalar2=None, op0=op0), r, w)
        else:
            S.op(eng, lambda e: e.tensor_scalar(out=out, in0=in0, scalar1=s1, scalar2=s2, op0=op0, op1=op1), r, w)

    def CP(eng, out, in_, r=(), w=()):
        if eng == "act":
            S.op("act", lambda e: e.copy(out=out, in_=in_), r, w)
        else:
            S.op(eng, lambda e: e.tensor_copy(out=out, in_=in_), r, w)

    def DMA(out, in_, r=(), w=()):
        S.dma(lambda e: e.dma_start(out=out, in_=in_), r, w)

    cst = AR.alloc([NCONST, 128], F32, "cst")
    DMA(cst.ap.rearrange("p a b -> p (a b)"), const_d, w=[cst.b])
    IDENT, ONES, CUM0, CUM1, MPOS0, MPOS1, MTN0, MTN1, BDM, NBDM = range(10)

    def C(i):
        return cst.ap[:, i, :]

    cstb = AR.alloc([4, 128], BF16, "cstb")
    for j, i in enumerate([IDENT, ONES, BDM, NBDM]):
        CP("dve", cstb.ap[:, j, :], C(i), r=[cst.b], w=[cstb.b])
    IDB, ONB, BDB, NBDB = [cstb.ap[:, j, :] for j in range(4)]

    kc = AR.alloc([8], F32, "kcols")
    KV = {"eps4": 4 * L2_EPS, "eps4q": 4 * L2_EPS * 128.0, "rmseps": RMS_EPS, "lneps": LN_EPS, "mhalf": -0.5,
          "one": 1.0, "zero": 0.0}
    KI = {}
    for j, (k, v) in enumerate(KV.items()):
        KI[k] = j
        S.op("pool", lambda e, j=j, v=v: e.memset(kc.ap[:, j:j + 1], v), (), [kc.b])

    def KC(name):
        j = KI[name]
        return kc.ap[:, j:j + 1]

    def small(name, shape):
        return AR.alloc(shape, F32, name)

    cT = small("cT", [8, 3])
    badaT = small("badaT", [48])
    cwq = small("cwq", [24, 3])
    cwb = small("cwb", [8, 3])
    alog = small("alog", [16])
    dtb = small("dtb", [16])
    onwc = small("onwc", [1])
    ln1g = small("ln1g", [8]); ln1b = small("ln1b", [8]); ln2g = small("ln2g", [8]); ln2b = small("ln2b", [8])
    bf1 = small("bf1", [32]); bf2 = small("bf2", [8])
    DMA(cT.ap.rearrange("p a b -> p (a b)"), cT_d, w=[cT.b])
    DMA(badaT.ap, badaT_d, w=[badaT.b])
    DMA(cwq.ap.rearrange("p a b -> p (a b)"), cwq_d, w=[cwq.b])
    DMA(cwb.ap.rearrange("p a b -> p (a b)"), cwb_d, w=[cwb.b])
    DMA(alog.ap, alog_d, w=[alog.b])
    DMA(dtb.ap, dtb_d, w=[dtb.b])
    DMA(onwc.ap, onw_d, w=[onwc.b])
    for t_, d_ in ((ln1g, ln1g_d), (ln1b, ln1b_d), (ln2g, ln2g_d), (ln2b, ln2b_d), (bf1, bf1_d), (bf2, bf2_d)):
        DMA(t_.ap, d_, w=[t_.b])
    ACT(onwc.ap, onwc.ap, AF.Copy, r=[onwc.b], w=[onwc.b], scale=0.5)
    nA = small("nA", [16])
    ACT(nA.ap, alog.ap, AF.Exp, r=[alog.b], w=[nA.b])
    TS("dve", nA.ap, nA.ap, -1.0, None, ALU.mult, r=[nA.b], w=[nA.b])
    modT = small("modT", [48, 3])
    s1c = small("s1c", [8, 3])
    hg1 = small("hg1", [8, 2])
    A1s = small("A1s", [8])
    A1b = small("A1b", [8, 2])
    U2s = small("U2s", [8, 2])
    U2b = small("U2b", [8, 2])
    wtp = Pool(AR, 6, [8, 128], BF16, "wt")
    mark_persist = AR.top

    cast_head, cast_rest = [], []
    cast_jobs = cast_head

    def mk_job(W, K0, K, c0, ncol, dst_ap, dst_buf):
        src = W.rearrange("(k p) c -> p k c", p=128)[:, K0:K0 + K, c0:c0 + ncol]
        dst = dst_ap.rearrange("p (k c) -> p k c", k=K)
        cast_jobs.append((src, dst, dst_buf))

    def in_col(c):
        return c * 128 if c < 24 else 3104 + (c - 24) * 128

    mk_job(win_d, 0, 8, 3072, 32, sc_bg_t, sb_bg)
    for h in range(NH):
        for c in (h, 8 + h, 16 + h, 24 + h):
            mk_job(win_d, 0, 8, in_col(c), 128, sc_in[c * 128:(c + 1) * 128, :], sb_in[c])
    cast_jobs = cast_rest
    for c in range(32, 72):
        mk_job(win_d, 0, 8, in_col(c), 128, sc_in[c * 128:(c + 1) * 128, :], sb_in[c])
    for W, sc, sb in ((wa_d, sc_a, sb_a), (wb_d, sc_b, sb_b), (wo_d, sc_o, sb_o)):
        for c in range(8):
            mk_job(W, 0, 8, c * 128, 128, sc[c * 128:(c + 1) * 128, :], sb[c])
    for c in range(32):
        mk_job(wf1_d, 0, 8, c * 128, 128, sc_f1[c * 128:(c + 1) * 128, :], sb_f1[c])
    for c in range(8):
        for kq in range(4):
            mk_job(wf2_d, kq * 8, 8, c * 128, 128, sc_f2[c * 128:(c + 1) * 128, kq * 1024:(kq + 1) * 1024], sb_f2[c])
    def emit_casts(lst, n):
        for _ in range(n):
            if not lst:
                return
            src, dst, dbuf = lst.pop(0)
            S.dma(lambda e, src=src, dst=dst: e.dma_start(out=dst, in_=src), reads=[], writes=[dbuf], eng="pool")

    def mod_phase():
        stg = Pool(AR, 3, [8, 128], F32, "mstg")
        sil = AR.alloc([8, 3], F32, "silc")
        th = AR.alloc([8, 3], F32, "silt")
        ACT(th.ap, cT.ap, AF.Tanh, r=[cT.b], w=[th.b], scale=0.5)
        STT("dve", sil.ap, th.ap, 1.0, cT.ap, ALU.add, ALU.mult, r=[th.b, cT.b], w=[sil.b])
        TS("dve", sil.ap, sil.ap, 0.5, None, ALU.mult, r=[sil.b], w=[sil.b])
        pb = banks[0]
        for fc in range(48):
            s = stg.get()
            DMA(s.ap, wada_d.rearrange("(k p) c -> p k c", p=128)[:, :, fc * 128:(fc + 1) * 128], w=[s.b])
            for k in range(8):
                MM(pb.ap[:, fc * 3:fc * 3 + 3], s.ap[:, k, :], sil.ap[:, k, :], start=(k == 0), stop=(k == 7),
                   r=[s.b, sil.b], w=[pb.b])
        TTop("dve", modT.ap, pb.ap[:, 0:144].rearrange("p (a b) -> p a b", a=48),
             badaT.ap.unsqueeze(2).to_broadcast([128, 48, 3]), ALU.add, r=[pb.b, badaT.b], w=[modT.b])
        m = modT.ap
        TS("dve", s1c.ap, m[:, 8:16, :], 1.0, None, ALU.add, r=[modT.b], w=[s1c.b])
        TS("dve", hg1.ap, m[:, 16:24, 0:2], 0.5, None, ALU.mult, r=[modT.b], w=[hg1.b])
        TS("dve", A1s.ap, ln1g.ap, ALPHA, None, ALU.mult, r=[ln1g.b], w=[A1s.b])
        tmp = AR.alloc([8, 2], F32, "mtmp")
        TTop("dve", tmp.ap, m[:, 40:48, 0:2], bf2.ap.unsqueeze(2).to_broadcast([128, 8, 2]), ALU.mult,
             r=[modT.b, bf2.b], w=[tmp.b])
        STT("dve", A1b.ap, ln1b.ap.unsqueeze(2).to_broadcast([128, 8, 2]), ALPHA, tmp.ap, ALU.mult, ALU.add,
            r=[ln1b.b, tmp.b], w=[A1b.b])
        tmp2 = AR.alloc([8, 2], F32, "mtmp2")
        TS("dve", tmp2.ap, m[:, 32:40, 0:2], 1.0, None, ALU.add, r=[modT.b], w=[tmp2.b])
        TTop("dve", U2s.ap, tmp2.ap, ln1g.ap.unsqueeze(2).to_broadcast([128, 8, 2]), ALU.mult,
             r=[tmp2.b, ln1g.b], w=[U2s.b])
        TTop("dve", U2b.ap, tmp2.ap, ln1b.ap.unsqueeze(2).to_broadcast([128, 8, 2]), ALU.mult,
             r=[tmp2.b, ln1b.b], w=[U2b.b])
        TTop("dve", U2b.ap, U2b.ap, m[:, 24:32, 0:2], ALU.add, r=[U2b.b, modT.b], w=[U2b.b])

    m0 = AR.top
    emit_casts(cast_head, 1 + 8)
    mod_phase()
    dbg("modT", modT, [128, 144])
    S.barrier()
    AR.top = m0

    def batch_program(b):
        mb = AR.top
        uT = [AR.alloc([8, ln], BF16, "uT%d" % ti) for ti, (st, ln) in enumerate(TILES)]
        og_scr = nc.dram_tensor("og_scr%d" % b, [NH, 128, SEQ], BF16, kind="Internal").ap()
        og_buf = [[Buf("ogs%d_%d_%d" % (b, h, t)) for t in range(4)] for h in range(NH)]
        m_phase = AR.top

        def utile(tok):
            for ti, (st, ln) in enumerate(TILES):
                if st <= tok < st + ln:
                    return ti, tok - st
            raise ValueError

        xs = Pool(AR, 6, [1024], F32, "xs")
        for ti, (st, ln) in enumerate(TILES):
            nb = ln // 128
            blks = []
            for j in range(nb):
                s = xs.get()
                if ti == 0:
                    src = ctx_d[b, j * 128:(j + 1) * 128, :]
                else:
                    src = x_d[b, st - CTXL + j * 128: st - CTXL + (j + 1) * 128, :]
                DMA(s.ap, src, w=[s.b])
                blks.append(s)
            jm = 2 if ti == 0 else b
            for fc in range(8):
                pb = nextbank()
                for j in range(nb):
                    MM(pb.ap[:, j * 128:(j + 1) * 128], blks[j].ap[:, fc * 128:(fc + 1) * 128], C(IDENT),
                       r=[blks[j].b, cst.b], w=[pb.b])
                ACT(uT[ti].ap[:, fc, :], pb.ap[:, 0:ln], AF.Identity, r=[pb.b, s1c.b, modT.b], w=[uT[ti].b],
                    bias=modT.ap[:, fc, jm:jm + 1], scale=s1c.ap[:, fc, jm:jm + 1])
        if b == 0:
            dbg("uT1", uT[1], [128, 8, 512], BF16)
        S.barrier()
        AR.top = m_phase
        if stage < 1:
            AR.top = mb
            return

        def pcol(name):
            return AR.alloc([NBLK, 16], F32, name)

        bt = pcol("bt"); hbt = pcol("hbt"); gg = pcol("gg"); Gc = pcol("Gc")
        n2eG = pcol("n2eG"); kdec = pcol("kdec"); eGl = pcol("eGl")
        m_A = AR.top
        wbg = AR.alloc([8, 32], BF16, "wbg")
        DMA(wbg.ap.rearrange("p a b -> p (a b)"), sc_bg_t, r=[sb_bg], w=[wbg.b])
        tb = AR.alloc([NBLK, 16], F32, "pc_tb")
        xg = AR.alloc([NBLK, 16], F32, "pc_xg")
        for half in range(2):
            pb = nextbank()
            for j in range(9):
                blk = half * 9 + j
                ti, off = utile(blk * 128)
                for k in range(8):
                    MM(pb.ap[:, j * 32:(j + 1) * 32], uT[ti].ap[:, k, off:off + 128], wbg.ap[:, k, :],
                       start=(k == 0), stop=(k == 7), r=[uT[ti].b, wbg.b], w=[pb.b])
            pv = pb.ap[:, 0:288].rearrange("p (a b) -> p a b", a=9)
            sl = slice(half * 9, half * 9 + 9)
            ACT(tb.ap[:, sl, :], pv[:, :, 0:16], AF.Tanh, r=[pb.b], w=[tb.b], scale=0.5)
            TTop("dve", xg.ap[:, sl, :], pv[:, :, 16:32], dtb.ap.unsqueeze(1).to_broadcast([128, 9, 16]), ALU.add,
                 r=[pb.b, dtb.b], w=[xg.b])
        TS("dve", bt.ap, tb.ap, 0.5, 0.5, ALU.mult, ALU.add, r=[tb.b], w=[bt.b])
        TS("dve", hbt.ap, tb.ap, 0.25, 0.25, ALU.mult, ALU.add, r=[tb.b], w=[hbt.b])
        TS("dve", xg.ap, xg.ap, 30.0, None, ALU.min, r=[xg.b], w=[xg.b])
        ACT(xg.ap, xg.ap, AF.Exp, r=[xg.b], w=[xg.b])
        ACT(xg.ap, xg.ap, AF.Ln, r=[xg.b, kc.b], w=[xg.b], bias=KC("one"))
        TTop("dve", gg.ap, xg.ap, nA.ap.unsqueeze(1).to_broadcast([128, NBLK, 16]), ALU.mult, r=[xg.b, nA.b], w=[gg.b])
        pb = nextbank()
        pv = pb.ap[:, 0:288].rearrange("p (a b) -> p a b", a=NBLK)
        for d in range(2):
            MM(pv[:, :, d * 8:(d + 1) * 8], C(CUM0 + d), gg.ap[:, :, d * 8:(d + 1) * 8], r=[cst.b, gg.b], w=[pb.b])
        CP("act", Gc.ap, pv, r=[pb.b], w=[Gc.b])
        pb2 = nextbank()
        pv2 = pb2.ap[:, 0:288].rearrange("p (a b) -> p a b", a=NBLK)
        MM(pb2.ap[:, 0:288], C(ONES), gg.ap.rearrange("p a b -> p (a b)"), r=[cst.b, gg.b], w=[pb2.b])
        ACT(eGl.ap, pv2, AF.Exp, r=[pb2.b], w=[eGl.b])
        TTop("dve", kdec.ap, pv2, Gc.ap, ALU.subtract, r=[pb2.b, Gc.b], w=[kdec.b])
        ACT(kdec.ap, kdec.ap, AF.Exp, r=[kdec.b], w=[kdec.b])
        ACT(n2eG.ap, Gc.ap, AF.Exp, r=[Gc.b], w=[n2eG.b])
        TS("dve", n2eG.ap, n2eG.ap, -2.0, None, ALU.mult, r=[n2eG.b], w=[n2eG.b])
        if b == 0:
            dbg("bt", bt, [128, NBLK, 16]); dbg("gg", gg, [128, NBLK, 16]); dbg("Gc", Gc, [128, NBLK, 16])
            dbg("kdec", kdec, [128, NBLK, 16]); dbg("eGl", eGl, [128, NBLK, 16])
        AR.top = m_A

        qTs = [AR.alloc([NTOK], BF16, "qT%d" % i) for i in range(2)]
        kTs = [AR.alloc([NTOK], BF16, "kT%d" % i) for i in range(2)]
        k_tms = [AR.alloc([NBLK, 128], BF16, "k_tm%d" % i) for i in range(2)]
        v_tms = [AR.alloc([NBLK, 128], BF16, "v_tm%d" % i) for i in range(2)]
        zs2Ts = [AR.alloc([SEQ], BF16, "zs2T%d" % i) for i in range(2)]
        WSn = ("Q", "AT", "Ub", "nWT", "Bb")
        WS = [{n_: AR.alloc([12, 128], BF16, "ws%d_%s" % (p_, n_)) for n_ in WSn} for p_ in range(2)]
        o_acc = AR.alloc([16, 128], F32, "o_acc")
        tf = Pool(AR, 3, [512], F32, "tf")
        tb16 = Pool(AR, 3, [512], BF16, "tb")
        chain = [[AR.alloc([4, 128], BF16, "ch%d_%d" % (g_, i)) for i in range(8)] for g_ in range(3)]
        S32 = [[AR.alloc([128], F32, "S32_%d_%d" % (d, i)) for i in range(2)] for d in range(2)]
        Sb = [[AR.alloc([128], BF16, "Sb_%d_%d" % (d, i)) for i in range(2)] for d in range(2)]
        e_ct = AR.alloc([512], F32, "e_ct")
        e_s2 = AR.alloc([512], F32, "e_s2")
        e_dg = AR.alloc([512], F32, "e_dg")
        e_sq = AR.alloc([512], BF16, "e_sq")
        e_vt = e_sq
        e_rs = AR.alloc([8], F32, "e_rs")
        on_all = AR.alloc([16, 128], BF16, "on_all")
        dec = [(AR.alloc([512], F32, "nD%d" % g_), AR.alloc([512], F32, "DT%d" % g_)) for g_ in range(3)]
        ssq = AR.alloc([16], F32, "ssq")
        rstd = AR.alloc([16], F32, "rstd")

        print("phase A top:", AR.top, "cap", AR.cap)

        def early_gen(h):
            qT, kT, k_tm, v_tm, zs2T = qTs[h % 2], kTs[h % 2], k_tms[h % 2], v_tms[h % 2], zs2Ts[h % 2]
            wts = []
            for c in (h, 8 + h, 16 + h, 24 + h):
                wt = wtp.get()
                DMA(wt.ap.rearrange("p a b -> p (a b)"), sc_in[c * 128:(c + 1) * 128, :], r=[sb_in[c]], w=[wt.b])
                wts.append(wt)
            yield
            for ti, (st, ln) in enumerate(TILES):
                R = 1 if ti == 0 else ln // 64
                L = ln // R
                nb = ln // 128
                for wi, name in enumerate(("q", "k", "v", "z")):
                    if name == "z" and ti == 0:
                        continue
                    pb = nextbank()
                    for k in range(8):
                        MM(pb.ap[:, 0:ln], wts[wi].ap[:, k, :], uT[ti].ap[:, k, :], start=(k == 0), stop=(k == 7),
                           r=[wts[wi].b, uT[ti].b], w=[pb.b])
                    if name == "z":
                        ACT(e_ct.ap[:, 0:ln], pb.ap[:, 0:ln], AF.Tanh, r=[pb.b], w=[e_ct.b], scale=0.5)
                        STT("dve", zs2T.ap[:, st - CTXL: st - CTXL + ln], e_ct.ap[:, 0:ln], 1.0, pb.ap[:, 0:ln],
                            ALU.add, ALU.mult, r=[e_ct.b, pb.b], w=[zs2T.b])
                        yield
                        continue
                    fcw = wi * 8 + h
                    ct = e_ct
                    ACT(ct.ap[:, 0:ln], pb.ap[:, 0:ln], AF.Copy, r=[pb.b, cwq.b], w=[ct.b], scale=cwq.ap[:, fcw, 1:2])
                    pv = pb.ap[:, 0:ln].rearrange("p (r l) -> p r l", r=R)
                    cv = ct.ap[:, 0:ln].rearrange("p (r l) -> p r l", r=R)
                    STT("dve", cv[:, :, 1:L], pv[:, :, 0:L - 1], cwq.ap[:, fcw, 0:1], cv[:, :, 1:L], ALU.mult, ALU.add,
                        r=[pb.b, cwq.b, ct.b], w=[ct.b])
                    STT("dve", cv[:, :, 0:L - 1], pv[:, :, 1:L], cwq.ap[:, fcw, 2:3], cv[:, :, 0:L - 1], ALU.mult, ALU.add,
                        r=[pb.b, cwq.b, ct.b], w=[ct.b])
                    s2 = e_s2
                    ACT(s2.ap[:, 0:ln], ct.ap[:, 0:ln], AF.Tanh, r=[ct.b], w=[s2.b], scale=0.5)
                    if name == "v":
                        STT("dve", e_vt.ap[:, 0:ln], s2.ap[:, 0:ln], 1.0, ct.ap[:, 0:ln], ALU.add, ALU.mult,
                            r=[s2.b, ct.b], w=[e_vt.b])
                        yield
                        pbv = nextbank()
                        for j in range(nb):
                            MM(pbv.ap[:, j * 128:(j + 1) * 128], e_vt.ap[:, j * 128:(j + 1) * 128], IDB,
                               r=[e_vt.b, cstb.b], w=[pbv.b])
                        CP("act", v_tm.ap[:, st // 128: st // 128 + nb, :].rearrange("p a b -> p (a b)"),
                           pbv.ap[:, 0:ln], r=[pbv.b], w=[v_tm.b])
                        yield
                        continue
                    STT("dve", s2.ap[:, 0:ln], s2.ap[:, 0:ln], 1.0, ct.ap[:, 0:ln], ALU.add, ALU.mult,
                        r=[s2.b, ct.b], w=[s2.b])
                    TTop("pool", e_sq.ap[:, 0:ln], s2.ap[:, 0:ln], s2.ap[:, 0:ln], ALU.mult, r=[s2.b], w=[e_sq.b])
                    yield
                    pb2 = nextbank()
                    for j in range(nb):
                        MM(pb2.ap[:, j:j + 1], e_sq.ap[:, j * 128:(j + 1) * 128], ONB[:, 0:1], r=[e_sq.b, cstb.b], w=[pb2.b])
                    rs = e_rs
                    if name == "q":
                        ACT(rs.ap[:, 0:nb], pb2.ap[:, 0:nb], AF.Identity, r=[pb2.b, kc.b], w=[rs.b],
                            bias=KC("eps4q"), scale=128.0)
                    else:
                        ACT(rs.ap[:, 0:nb], pb2.ap[:, 0:nb], AF.Identity, r=[pb2.b, kc.b], w=[rs.b],
                            bias=KC("eps4"), scale=1.0)
                    TTop("pool", rs.ap[:, 0:nb], rs.ap[:, 0:nb], KC("mhalf").to_broadcast([128, nb]), ALU.pow,
                         r=[rs.b, kc.b], w=[rs.b])
                    dg = e_dg
                    dgv = dg.ap[:, 0:ln].rearrange("p (a b) -> p a b", a=nb)
                    TTop("pool", dgv, C(IDENT).unsqueeze(1).to_broadcast([128, nb, 128]),
                         rs.ap[:, 0:nb].unsqueeze(2).to_broadcast([128, nb, 128]), ALU.mult,
                         r=[cst.b, rs.b], w=[dg.b])
                    yield
                    pb3 = nextbank()
                    for j in range(nb):
                        MM(pb3.ap[:, j * 128:(j + 1) * 128], C(ONES), dgv[:, j, :], r=[cst.b, dg.b], w=[pb3.b])
                    dst = qT if name == "q" else kT
                    TTop("dve", dst.ap[:, st:st + ln], s2.ap[:, 0:ln], pb3.ap[:, 0:ln], ALU.mult,
                         r=[s2.b, pb3.b], w=[dst.b])
                    yield
                    if name == "k":
                        pbt = nextbank()
                        for j in range(nb):
                            MM(pbt.ap[:, j * 128:(j + 1) * 128], kT.ap[:, st + j * 128: st + (j + 1) * 128], IDB,
                               r=[kT.b, cstb.b], w=[pbt.b])
                        CP("act", k_tm.ap[:, st // 128: st // 128 + nb, :].rearrange("p a b -> p (a b)"),
                           pbt.ap[:, 0:ln], r=[pbt.b], w=[k_tm.b])
                        yield
            if b == 0 and h == 0:
                dbg("qT", qT, [128, NTOK], BF16); dbg("kT", kT, [128, NTOK], BF16)
                dbg("v_tm", v_tm, [128, NBLK, 128], BF16); dbg("zs2T", zs2T, [128, SEQ], BF16)

        def adv(gen, n):
            if gen is None:
                return
            for _ in range(n):
                try:
                    next(gen)
                except StopIteration:
                    return

        def head_ctx(h):
            qT, kT, k_tm, v_tm, zs2T = qTs[h % 2], kTs[h % 2], k_tms[h % 2], v_tms[h % 2], zs2Ts[h % 2]

            def init():
                for d in range(2):
                    S.op("pool", lambda e, d=d: e.memset(S32[d][0].ap, 0.0), (), [S32[d][0].b])
                    S.op("pool", lambda e, d=d: e.memset(Sb[d][0].ap, 0.0), (), [Sb[d][0].b])

            def col(t_, blk, d):
                return t_.ap[:, blk, d * 8 + h: d * 8 + h + 1]

            def wave_prep(w):
                units = [(d, 6 * w + s_) for d in range(2) for s_ in range(6)]
                f2 = lambda t_: t_.ap.rearrange("p a b -> p (a b)")
                ws = WS[(3 * h + w) % 2]
                G = []
                for gq in range(3):
                    ch = chain[gq]
                    G.append({"us": units[gq * 4: gq * 4 + 4], "gq": gq, "Mt": ch[0], "R": [ch[1], ch[2]],
                              "P": [ch[3], ch[4]], "Y": [ch[5], ch[6]], "Moff": ch[7], "rp": 0, "y": 0})
                gcs = []
                for g in G:
                    gcum = tf.get()
                    gcv = gcum.ap.rearrange("p (a b) -> p a b", a=4)
                    for j, (d, s_) in enumerate(g["us"]):
                        blk = ORDER[d][s_]
                        TTop("pool", gcv[:, j, :], C(CUM0 + d), col(gg, blk, d).to_broadcast([128, 128]), ALU.mult,
                             r=[cst.b, gg.b], w=[gcum.b])
                    gcs.append((gcum, gcv))
                yield
                for g, (gcum, gcv) in zip(G, gcs):
                    gq = g["gq"]
                    pg = nextbank()
                    for j in range(4):
                        MM(pg.ap[:, j * 128:(j + 1) * 128], C(ONES), gcv[:, j, :], r=[cst.b, gcum.b], w=[pg.b])
                    eGr = gcum
                    ACT(eGr.ap, pg.ap, AF.Exp, r=[pg.b], w=[eGr.b])
                    nD, DT = dec[gq]
                    for j, (d, s_) in enumerate(g["us"]):
                        blk = ORDER[d][s_]
                        sl = slice(j * 128, (j + 1) * 128)
                        STT("dve", nD.ap[:, sl], pg.ap[:, sl], col(Gc, blk, d), C(MPOS0 + d), ALU.subtract, ALU.max,
                            r=[pg.b, Gc.b, cst.b], w=[nD.b])
                        STT("dve", DT.ap[:, sl], pg.ap[:, sl], col(Gc, blk, d), C(MTN0 + d), ALU.subtract, ALU.min,
                            r=[pg.b, Gc.b, cst.b], w=[DT.b])
                    for j, (d, s_) in enumerate(g["us"]):
                        blk = ORDER[d][s_]
                        sl = slice(j * 128, (j + 1) * 128)
                        TTop("pool", ws["Q"].ap[:, gq * 4 + j, :], qT.ap[:, blk * 128:(blk + 1) * 128], eGr.ap[:, sl],
                             ALU.mult, r=[qT.b, eGr.b], w=[ws["Q"].b])
                yield
                for g in G:
                    gq = g["gq"]
                    us = g["us"]
                    nD, DT = dec[gq]
                    ACT(nD.ap, nD.ap, AF.Exp, r=[nD.b], w=[nD.b], scale=-1.0)
                    ACT(DT.ap, DT.ap, AF.Exp, r=[DT.b], w=[DT.b])
                    pkk = nextbank()
                    pkq = nextbank()
                    for j, (d, s_) in enumerate(us):
                        blk = ORDER[d][s_]
                        ks = kT.ap[:, blk * 128:(blk + 1) * 128]
                        MM(pkk.ap[:, j * 128:(j + 1) * 128], ks, ks, r=[kT.b], w=[pkk.b])
                    for j, (d, s_) in enumerate(us):
                        blk = ORDER[d][s_]
                        ks = kT.ap[:, blk * 128:(blk + 1) * 128]
                        MM(pkq.ap[:, j * 128:(j + 1) * 128], ks, qT.ap[:, blk * 128:(blk + 1) * 128],
                           r=[kT.b, qT.b], w=[pkq.b])
                    Mt = g["Mt"]
                    for j, (d, s_) in enumerate(us):
                        blk = ORDER[d][s_]
                        sl = slice(j * 128, (j + 1) * 128)
                        STT("dve", Mt.ap[:, j, :], pkk.ap[:, sl], col(bt, blk, d), nD.ap[:, sl], ALU.mult, ALU.mult,
                            r=[pkk.b, bt.b, nD.b], w=[Mt.b])
                    TTop("dve", ws["AT"].ap[:, gq * 4:gq * 4 + 4, :].rearrange("p a b -> p (a b)"), pkq.ap, DT.ap, ALU.mult,
                         r=[pkq.b, DT.b], w=[ws["AT"].b])
                yield
                for g in G:
                    Mt = g["Mt"]
                    TTop("pool", g["R"][0].ap, Mt.ap, BDB.unsqueeze(1).to_broadcast([128, 4, 128]), ALU.mult,
                         r=[Mt.b, cstb.b], w=[g["R"][0].b])
                    TTop("pool", g["Moff"].ap, Mt.ap, NBDB.unsqueeze(1).to_broadcast([128, 4, 128]), ALU.mult,
                         r=[Mt.b, cstb.b], w=[g["Moff"].b])
                yield
                for g in G:
                    R_, P_, Y_ = g["R"], g["P"], g["Y"]
                    pb = nextbank()
                    for j in range(4):
                        MM(pb.ap[:, j * 128:(j + 1) * 128], R_[0].ap[:, j, :], IDB, r=[R_[0].b, cstb.b], w=[pb.b])
                    CP("act", f2(P_[0]), pb.ap, r=[pb.b], w=[P_[0].b])
                    TTop("dve", Y_[0].ap, IDB.unsqueeze(1).to_broadcast([128, 4, 128]),
                         pb.ap.rearrange("p (a b) -> p a b", a=4), ALU.subtract, r=[cstb.b, pb.b], w=[Y_[0].b])
                yield
                for seg in range(6):
                    for g in G:
                        R_, P_, Y_ = g["R"], g["P"], g["Y"]
                        rp, y = g["rp"], g["y"]
                        if seg >= 1:
                            pc = nextbank()
                            for j in range(4):
                                MM(pc.ap[:, j * 128:(j + 1) * 128], R_[rp].ap[:, j, :], Y_[y].ap[:, j, :],
                                   r=[R_[rp].b, Y_[y].b], w=[pc.b])
                            TTop("dve", f2(Y_[1 - y]), f2(Y_[y]), pc.ap, ALU.add, r=[Y_[y].b, pc.b], w=[Y_[1 - y].b])
                            g["y"] = 1 - y
                        if seg <= 4:
                            pa = nextbank()
                            for j in range(4):
                                MM(pa.ap[:, j * 128:(j + 1) * 128], P_[rp].ap[:, j, :], R_[rp].ap[:, j, :],
                                   r=[P_[rp].b, R_[rp].b], w=[pa.b])
                            CP("act", f2(R_[1 - rp]), pa.ap, r=[pa.b], w=[R_[1 - rp].b])
                            if seg <= 3:
                                pbk = nextbank()
                                for j in range(4):
                                    MM(pbk.ap[:, j * 128:(j + 1) * 128], R_[rp].ap[:, j, :], P_[rp].ap[:, j, :],
                                       r=[P_[rp].b, R_[rp].b], w=[pbk.b])
                                CP("act", f2(P_[1 - rp]), pbk.ap, r=[pbk.b], w=[P_[1 - rp].b])
                            g["rp"] = 1 - rp
                    yield
                for g in G:
                    rp, y = g["rp"], g["y"]
                    g["Yf"], g["YT"], g["Z1"] = g["Y"][y], g["P"][1], g["P"][0]
                    g["ek"], g["kd"], g["Wt"] = g["R"][1 - rp], g["Y"][1 - y], g["R"][rp]
                    Yf, YT, Z1, Moff = g["Yf"], g["YT"], g["Z1"], g["Moff"]
                    pb = nextbank()
                    for j in range(4):
                        MM(pb.ap[:, j * 128:(j + 1) * 128], Yf.ap[:, j, :], IDB, r=[Yf.b, cstb.b], w=[pb.b])
                    CP("act", f2(YT), pb.ap, r=[pb.b], w=[YT.b])
                    pz = nextbank()
                    for j in range(4):
                        MM(pz.ap[:, j * 128:(j + 1) * 128], Moff.ap[:, j, :], Yf.ap[:, j, :], r=[Moff.b, Yf.b], w=[pz.b])
                    CP("act", f2(Z1), pz.ap, r=[pz.b], w=[Z1.b])
                    for j, (d, s_) in enumerate(g["us"]):
                        blk = ORDER[d][s_]
                        TTop("pool", g["ek"].ap[:, j, :], k_tm.ap[:, blk, :], col(n2eG, blk, d).to_broadcast([128, 128]),
                             ALU.mult, r=[k_tm.b, n2eG.b], w=[g["ek"].b])
                        TTop("pool", g["kd"].ap[:, j, :], k_tm.ap[:, blk, :], col(kdec, blk, d).to_broadcast([128, 128]),
                             ALU.mult, r=[k_tm.b, kdec.b], w=[g["kd"].b])
                yield
                for g in G:
                    Yf, YT, Z1 = g["Yf"], g["YT"], g["Z1"]
                    pz2 = nextbank()
                    for j in range(4):
                        MM(pz2.ap[:, j * 128:(j + 1) * 128], YT.ap[:, j, :], Z1.ap[:, j, :], r=[YT.b, Z1.b], w=[pz2.b])
                    tmpT = tf.get()
                    TTop("dve", tmpT.ap, f2(Yf), pz2.ap, ALU.subtract, r=[Yf.b, pz2.b], w=[tmpT.b])
                    TTs = g["Mt"]
                    for j, (d, s_) in enumerate(g["us"]):
                        blk = ORDER[d][s_]
                        TTop("pool", TTs.ap[:, j, :], tmpT.ap[:, j * 128:(j + 1) * 128],
                             col(hbt, blk, d).to_broadcast([128, 128]), ALU.mult, r=[tmpT.b, hbt.b], w=[TTs.b])
                yield
                for g in G:
                    gq = g["gq"]
                    TTs, ek, Wt = g["Mt"], g["ek"], g["Wt"]
                    usl = slice(gq * 4, gq * 4 + 4)
                    pU = nextbank()
                    pW = nextbank()
                    for j, (d, s_) in enumerate(g["us"]):
                        blk = ORDER[d][s_]
                        MM(pU.ap[:, j * 128:(j + 1) * 128], TTs.ap[:, j, :], v_tm.ap[:, blk, :], r=[TTs.b, v_tm.b], w=[pU.b])
                    for j in range(4):
                        MM(pW.ap[:, j * 128:(j + 1) * 128], TTs.ap[:, j, :], ek.ap[:, j, :], r=[TTs.b, ek.b], w=[pW.b])
                    CP("act", ws["Ub"].ap[:, usl, :].rearrange("p a b -> p (a b)"), pU.ap, r=[pU.b], w=[ws["Ub"].b])
                    CP("dve", f2(Wt), pW.ap, r=[pW.b], w=[Wt.b])
                yield
                for g in G:
                    gq = g["gq"]
                    kd, Wt = g["kd"], g["Wt"]
                    usl = slice(gq * 4, gq * 4 + 4)
                    pWT = nextbank()
                    pB = nextbank()
                    pQ = nextbank()
                    for j in range(4):
                        MM(pWT.ap[:, j * 128:(j + 1) * 128], Wt.ap[:, j, :], kd.ap[:, j, :], r=[Wt.b, kd.b], w=[pWT.b])
                    for j in range(4):
                        MM(pB.ap[:, j * 128:(j + 1) * 128], kd.ap[:, j, :], ws["Ub"].ap[:, gq * 4 + j, :],
                           r=[kd.b, ws["Ub"].b], w=[pB.b])
                    for j in range(4):
                        MM(pQ.ap[:, j * 128:(j + 1) * 128], Wt.ap[:, j, :], ws["AT"].ap[:, gq * 4 + j, :],
                           r=[Wt.b, ws["AT"].b], w=[pQ.b])
                    CP("act", ws["nWT"].ap[:, usl, :].rearrange("p a b -> p (a b)"), pWT.ap, r=[pWT.b], w=[ws["nWT"].b])
                    CP("act", ws["Bb"].ap[:, usl, :].rearrange("p a b -> p (a b)"), pB.ap, r=[pB.b], w=[ws["Bb"].b])
                    qv = ws["Q"].ap[:, usl, :].rearrange("p a b -> p (a b)")
                    TTop("dve", qv, qv, pQ.ap, ALU.add, r=[ws["Q"].b, pQ.b], w=[ws["Q"].b])
                yield

            def scan_step(d, s_):
                blk = ORDER[d][s_]
                ws = WS[(3 * h + s_ // 6) % 2]
                u = d * 6 + (s_ % 6)
                cur = s_ % 2
                nxt = 1 - cur
                pst_ = banks[d]
                MM(pst_.ap[:, 0:128], ws["nWT"].ap[:, u, :], Sb[d][cur].ap, start=True, stop=False,
                   r=[ws["nWT"].b, Sb[d][cur].b], w=[pst_.b])
                MM(pst_.ap[:, 0:128], IDB, ws["Bb"].ap[:, u, :], start=False, stop=True,
                   r=[cstb.b, ws["Bb"].b], w=[pst_.b])
                STT("dve", Sb[d][nxt].ap, S32[d][cur].ap, col(eGl, blk, d), pst_.ap[:, 0:128], ALU.mult, ALU.add,
                    r=[S32[d][cur].b, eGl.b, pst_.b], w=[Sb[d][nxt].b])
                STT("dve", S32[d][nxt].ap, S32[d][cur].ap, col(eGl, blk, d), pst_.ap[:, 0:128], ALU.mult, ALU.add,
                    r=[S32[d][cur].b, eGl.b, pst_.b], w=[S32[d][nxt].b])
                if blk >= 2:
                    li = blk - 2
                    po = banks[2]
                    MM(po.ap[:, d * 128:(d + 1) * 128], ws["Q"].ap[:, u, :], Sb[d][cur].ap, start=True, stop=False,
                       r=[ws["Q"].b, Sb[d][cur].b], w=[po.b])
                    MM(po.ap[:, d * 128:(d + 1) * 128], ws["AT"].ap[:, u, :], ws["Ub"].ap[:, u, :], start=False, stop=True,
                       r=[ws["AT"].b, ws["Ub"].b], w=[po.b])
                    first = (d == 0 and li < 8) or (d == 1 and li >= 8)
                    if first:
                        CP("act", o_acc.ap[:, li, :], po.ap[:, d * 128:(d + 1) * 128], r=[po.b], w=[o_acc.b])
                    else:
                        TTop("dve", o_acc.ap[:, li, :], o_acc.ap[:, li, :], po.ap[:, d * 128:(d + 1) * 128], ALU.add,
                             r=[po.b, o_acc.b], w=[o_acc.b])

            def drain(gen):
                for _ in gen:
                    pass

            def finish_gen():
                S.op("pool", lambda e: e.memset(ssq.ap, 0.0), (), [ssq.b])
                for li in range(16):
                    junk = tb16.get()
                    S.op("act", lambda e, li=li, junk=junk: e.activation(out=junk.ap[:, 0:128], in_=o_acc.ap[:, li, :],
                                                                          func=AF.Square, accum_out=ssq.ap[:, li:li + 1]),
                         [o_acc.b], [junk.b, ssq.b])
                ACT(rstd.ap, ssq.ap, AF.Identity, r=[ssq.b, kc.b], w=[rstd.b], bias=KC("rmseps"), scale=1.0 / 128.0)
                TTop("pool", rstd.ap, rstd.ap, KC("mhalf").to_broadcast([128, 16]), ALU.pow, r=[rstd.b, kc.b], w=[rstd.b])
                for li in range(16):
                    ACT(on_all.ap[:, li, :], o_acc.ap[:, li, :], AF.Copy, r=[o_acc.b, rstd.b], w=[on_all.b],
                        scale=rstd.ap[:, li:li + 1])
                if b == 0 and h == 0:
                    dbg("o_acc", o_acc, [128, 16, 128])
                yield
                for g0 in range(0, 16, 4):
                    pb = nextbank()
                    for j in range(4):
                        MM(pb.ap[:, j * 128:(j + 1) * 128], on_all.ap[:, g0 + j, :], IDB, r=[on_all.b, cstb.b], w=[pb.b])
                    ti = g0 // 4
                    ogt = tb16.get()
                    STT("dve", ogt.ap, pb.ap, onwc.ap[:, 0:1], zs2T.ap[:, ti * 512:(ti + 1) * 512], ALU.mult, ALU.mult,
                        r=[pb.b, onwc.b, zs2T.b], w=[ogt.b])
                    DMA(og_scr[h, :, ti * 512:(ti + 1) * 512], ogt.ap, r=[ogt.b], w=[og_buf[h][ti]])
                    if b == 0 and h == 0:
                        dbg("og0_%d" % ti, ogt, [128, 512], BF16)
                    yield

            return {"prep": wave_prep, "scan": scan_step, "fin": finish_gen, "init": init}

        nheads = NH if stage >= 4 else 1
        en = early_gen(0)
        for _ in en:
            pass
        if stage >= 2:
            ctx = [head_ctx(h) for h in range(nheads)]
            seq = [(h, w) for h in range(nheads) for w in range(3)]
            st = {"en": early_gen(1) if nheads > 1 else None, "fin": None}

            def side(n=1):
                for _ in range(n):
                    if st["fin"] is not None:
                        try:
                            next(st["fin"])
                        except StopIteration:
                            st["fin"] = None
                    elif st["en"] is not None:
                        try:
                            next(st["en"])
                            next(st["en"])
                        except StopIteration:
                            st["en"] = None
                    if b == 0:
                        emit_casts(cast_head, 1)
                        emit_casts(cast_rest, 1)

            rr_lo[0] = 3
            cur = ctx[0]["prep"](0)
            for _ in cur:
                side()
            for i, (h, w) in enumerate(seq):
                if w == 0:
                    ctx[h]["init"]()
                nxt = None
                if i + 1 < len(seq):
                    h2, w2 = seq[i + 1]
                    if w2 == 0:
                        while st["fin"] is not None or st["en"] is not None:
                            side()
                        st["en"] = None
                    nxt = ctx[h2]["prep"](w2)
                if w == 0 and h + 1 < nheads and h >= 1:
                    st["pending_en"] = h + 1
                for s_ in range(6 * w, 6 * w + 6):
                    for d in range(2):
                        ctx[h]["scan"](d, s_)
                        adv(nxt, 1)
                        side()
                        adv(nxt, 1)
                if nxt is not None:
                    for _ in nxt:
                        side()
                if w == 2:
                    st["fin"] = ctx[h]["fin"]()
                    next(st["fin"])
                    if h + 2 < nheads:
                        st["en"] = early_gen(h + 2)
            while st["fin"] is not None or st["en"] is not None:
                side()
            rr_lo[0] = 0
        emit_casts(cast_head, 100)
        emit_casts(cast_rest, 1000)
        S.barrier()
        AR.top = m_phase
        if stage < 4:
            AR.top = mb
            return

        ymT = [[AR.alloc([512], BF16, "ym%d_%d" % (fc, t)) for t in range(4)] for fc in range(8)]
        m_B = AR.top
        pT = [[AR.alloc([512], BF16, "p%d_%d" % (fc, t)) for t in range(4)] for fc in range(8)]
        tf = Pool(AR, 8, [512], F32, "tfB")

        def wload(sc, sbufs, c):
            wt = wtp.get()
            DMA(wt.ap.rearrange("p a b -> p (a b)"), sc[c * 128:(c + 1) * 128, :], r=[sbufs[c]], w=[wt.b])
            return wt

        for fc in range(8):
            wxb = wload(sc_in, sb_in, 32 + fc)
            wbgt = wload(sc_in, sb_in, 40 + fc)
            wcg = wload(sc_in, sb_in, 48 + fc)
            for t in range(4):
                ti = t + 1
                pbs = []
                for wt in (wxb, wbgt, wcg):
                    pb = nextbank()
                    for k in range(8):
                        MM(pb.ap, wt.ap[:, k, :], uT[ti].ap[:, k, :], start=(k == 0), stop=(k == 7),
                           r=[wt.b, uT[ti].b], w=[pb.b])
                    pbs.append(pb)
                cgs = tf.get()
                CP("act", cgs.ap, pbs[2].ap, r=[pbs[2].b], w=[cgs.b])
                cgx = tf.get()
                TTop("dve", cgx.ap, pbs[0].ap, cgs.ap, ALU.mult, r=[pbs[0].b, cgs.b], w=[cgx.b])
                ct = tf.get()
                TTop("pool", ct.ap, cgx.ap, cwb.ap[:, fc, 1:2].to_broadcast([128, 512]), ALU.mult,
                     r=[cgx.b, cwb.b], w=[ct.b])
                xv = cgx.ap.rearrange("p (r l) -> p r l", r=8)
                cv = ct.ap.rearrange("p (r l) -> p r l", r=8)
                STT("dve", cv[:, :, 1:64], xv[:, :, 0:63], cwb.ap[:, fc, 0:1], cv[:, :, 1:64], ALU.mult, ALU.add,
                    r=[cgx.b, cwb.b, ct.b], w=[ct.b])
                STT("dve", cv[:, :, 0:63], xv[:, :, 1:64], cwb.ap[:, fc, 2:3], cv[:, :, 0:63], ALU.mult, ALU.add,
                    r=[cgx.b, cwb.b, ct.b], w=[ct.b])
                TTop("dve", pT[fc][t].ap, pbs[1].ap, ct.ap, ALU.mult, r=[pbs[1].b, ct.b], w=[pT[fc][t].b])
        if b == 0:
            dbg("pT0_0", pT[0][0], [128, 512], BF16)
        ogp = Pool(AR, 2, [8, 512], BF16, "ogp")
        for t in range(4):
            ti = t + 1
            ogt = ogp.get()
            DMA(ogt.ap, og_scr[:, :, t * 512:(t + 1) * 512].rearrange("k p c -> p k c"),
                r=[og_buf[k][t] for k in range(NH)], w=[ogt.b])
            for fc in range(8):
                wga = wload(sc_in, sb_in, 56 + fc)
                wgb = wload(sc_in, sb_in, 64 + fc)
                wa = wload(sc_a, sb_a, fc)
                wb_ = wload(sc_b, sb_b, fc)
                pga = nextbank(); pgb = nextbank(); pya = nextbank(); pyb = nextbank()
                for k in range(8):
                    MM(pga.ap, wga.ap[:, k, :], uT[ti].ap[:, k, :], start=(k == 0), stop=(k == 7),
                       r=[wga.b, uT[ti].b], w=[pga.b])
                for k in range(8):
                    MM(pgb.ap, wgb.ap[:, k, :], uT[ti].ap[:, k, :], start=(k == 0), stop=(k == 7),
                       r=[wgb.b, uT[ti].b], w=[pgb.b])
                for k in range(8):
                    MM(pya.ap, wa.ap[:, k, :], ogt.ap[:, k, :], start=(k == 0), stop=(k == 7),
                       r=[wa.b, ogt.b], w=[pya.b])
                for k in range(8):
                    MM(pyb.ap, wb_.ap[:, k, :], pT[k][t].ap, start=(k == 0), stop=(k == 7),
                       r=[wb_.b, pT[k][t].b], w=[pyb.b])
                ta = tf.get(); tb_ = tf.get()
                ACT(ta.ap, pga.ap, AF.Tanh, r=[pga.b], w=[ta.b], scale=0.5)
                ACT(tb_.ap, pgb.ap, AF.Tanh, r=[pgb.b], w=[tb_.b], scale=0.5)
                STT("dve", ta.ap, ta.ap, 1.0, pya.ap, ALU.add, ALU.mult, r=[ta.b, pya.b], w=[ta.b])
                STT("dve", tb_.ap, tb_.ap, 1.0, pyb.ap, ALU.add, ALU.mult, r=[tb_.b, pyb.b], w=[tb_.b])
                TTop("pool", ymT[fc][t].ap, ta.ap, tb_.ap, ALU.add, r=[ta.b, tb_.b], w=[ymT[fc][t].b])
        if b == 0:
            dbg("ym0_0", ymT[0][0], [128, 512], BF16)
        S.barrier()
        AR.top = m_B
        if stage < 5:
            AR.top = mb
            return

        m_main = AR.top
        AR.top = mb
        hsqT = AR.alloc([32, 512], BF16, "hsqT")
        assert AR.top <= m_phase, (AR.top, m_phase)
        AR.top = m_main
        xa = AR.alloc([8, 512], F32, "xa")
        outT = AR.alloc([8, 512], F32, "outT")
        u2T = AR.alloc([8, 512], BF16, "u2T")
        tf = Pool(AR, 8, [512], F32, "tfC")
        w2p = Pool(AR, 2, [32, 128], BF16, "w2p")
        xs = Pool(AR, 6, [1024], F32, "xsC")

        sq_all = AR.alloc([8, 512], BF16, "sq_all")
        ln_dg = AR.alloc([8, 128], F32, "ln_dg")
        ln_A = AR.alloc([512], F32, "ln_A")
        ln_B = AR.alloc([512], F32, "ln_B")
        ln_mean = AR.alloc([4], F32, "ln_mean")
        ln_msq = AR.alloc([4], F32, "ln_msq")
        ln_var = AR.alloc([4], F32, "ln_var")
        ln_nmr = AR.alloc([4], F32, "ln_nmr")

        def layer_norm(src, dst_fns):
            for fc in range(8):
                ACT(sq_all.ap[:, fc, :], src.ap[:, fc, :], AF.Square, r=[src.b], w=[sq_all.b])
            pst = nextbank()
            for j in range(4):
                for fc in range(8):
                    MM(pst.ap[:, 2 * j:2 * j + 1], src.ap[:, fc, j * 128:(j + 1) * 128], C(ONES)[:, 0:1],
                       start=(fc == 0), stop=(fc == 7), r=[src.b, cst.b], w=[pst.b])
                for fc in range(8):
                    MM(pst.ap[:, 2 * j + 1:2 * j + 2], sq_all.ap[:, fc, j * 128:(j + 1) * 128], ONB[:, 0:1],
                       start=(fc == 0), stop=(fc == 7), r=[sq_all.b, cstb.b], w=[pst.b])
            pv = pst.ap[:, 0:8].rearrange("p (a b) -> p a b", a=4)
            ACT(ln_mean.ap, pv[:, :, 0], AF.Copy, r=[pst.b], w=[ln_mean.b], scale=1.0 / D)
            TTop("dve", ln_msq.ap, ln_mean.ap, ln_mean.ap, ALU.mult, r=[ln_mean.b], w=[ln_msq.b])
            STT("dve", ln_var.ap, pv[:, :, 1], 1.0 / D, ln_msq.ap, ALU.mult, ALU.subtract,
                r=[pst.b, ln_msq.b], w=[ln_var.b])
            ACT(ln_var.ap, ln_var.ap, AF.Identity, r=[ln_var.b, kc.b], w=[ln_var.b], bias=KC("lneps"), scale=1.0)
            TTop("pool", ln_var.ap, ln_var.ap, KC("mhalf").to_broadcast([128, 4]), ALU.pow,
                 r=[ln_var.b, kc.b], w=[ln_var.b])
            STT("dve", ln_nmr.ap, ln_mean.ap, -1.0, ln_var.ap, ALU.mult, ALU.mult,
                r=[ln_mean.b, ln_var.b], w=[ln_nmr.b])
            idb = C(IDENT).unsqueeze(1).to_broadcast([128, 4, 128])
            TTop("pool", ln_dg.ap[:, 0:4, :], idb, ln_var.ap.unsqueeze(2).to_broadcast([128, 4, 128]), ALU.mult,
                 r=[cst.b, ln_var.b], w=[ln_dg.b])
            TTop("pool", ln_dg.ap[:, 4:8, :], idb, ln_nmr.ap.unsqueeze(2).to_broadcast([128, 4, 128]), ALU.mult,
                 r=[cst.b, ln_nmr.b], w=[ln_dg.b])
            pA = nextbank()
            pB = nextbank()
            for j in range(4):
                MM(pA.ap[:, j * 128:(j + 1) * 128], C(ONES), ln_dg.ap[:, j, :], r=[cst.b, ln_dg.b], w=[pA.b])
            for j in range(4):
                MM(pB.ap[:, j * 128:(j + 1) * 128], C(ONES), ln_dg.ap[:, 4 + j, :], r=[cst.b, ln_dg.b], w=[pB.b])
            CP("act", ln_A.ap, pA.ap, r=[pA.b], w=[ln_A.b])
            CP("act", ln_B.ap, pB.ap, r=[pB.b], w=[ln_B.b])
            for fc in range(8):
                xc = tf.get()
                TTop("dve", xc.ap, src.ap[:, fc, :], ln_A.ap, ALU.mult, r=[src.b, ln_A.b], w=[xc.b])
                TTop("pool", xc.ap, xc.ap, ln_B.ap, ALU.add, r=[xc.b, ln_B.b], w=[xc.b])
                for (oap, sc_, bi_, rb, wbuf) in dst_fns:
                    ACT(oap(fc), xc.ap, AF.Identity, r=[xc.b] + rb, w=[wbuf], bias=bi_(fc), scale=sc_(fc))

        wo_t = [None] * 8
        for t in range(4):
            ti = t + 1
            blks = []
            for j in range(4):
                s = xs.get()
                DMA(s.ap, x_d[b, t * 512 + j * 128: t * 512 + (j + 1) * 128, :], w=[s.b])
                blks.append(s)
            for fc in range(8):
                pb = nextbank()
                for j in range(4):
                    MM(pb.ap[:, j * 128:(j + 1) * 128], blks[j].ap[:, fc * 128:(fc + 1) * 128], C(IDENT),
                       r=[blks[j].b, cst.b], w=[pb.b])
                ACT(xa.ap[:, fc, :], pb.ap, AF.Copy, r=[pb.b], w=[xa.b], scale=ALPHA)
            for fc in range(8):
                wo = wload(sc_o, sb_o, fc)
                pb = nextbank()
                for k in range(8):
                    MM(pb.ap, wo.ap[:, k, :], ymT[k][t].ap, start=(k == 0), stop=(k == 7), r=[wo.b, ymT[k][t].b], w=[pb.b])
                STT("dve", xa.ap[:, fc, :], pb.ap, hg1.ap[:, fc, b:b + 1], xa.ap[:, fc, :], ALU.mult, ALU.add,
                    r=[pb.b, hg1.b, xa.b], w=[xa.b])
            if b == 0 and t == 0:
                dbg("pre1", xa, [128, 8, 512])
            layer_norm(xa, [
                (lambda fc: outT.ap[:, fc, :], lambda fc: A1s.ap[:, fc:fc + 1], lambda fc: A1b.ap[:, fc, b:b + 1],
                 [A1s.b, A1b.b], outT.b),
                (lambda fc: u2T.ap[:, fc, :], lambda fc: U2s.ap[:, fc, b:b + 1], lambda fc: U2b.ap[:, fc, b:b + 1],
                 [U2s.b, U2b.b], u2T.b)])
            if b == 0 and t == 0:
                dbg("x1a", outT, [128, 8, 512]); dbg("u2T", u2T, [128, 8, 512], BF16)
            for ffc in range(32):
                w1 = wload(sc_f1, sb_f1, ffc)
                pb = nextbank()
                for k in range(8):
                    MM(pb.ap, w1.ap[:, k, :], u2T.ap[:, k, :], start=(k == 0), stop=(k == 7), r=[w1.b, u2T.b], w=[pb.b])
                rl = tf.get()
                ACT(rl.ap, pb.ap, AF.Relu, r=[pb.b, bf1.b], w=[rl.b], bias=bf1.ap[:, ffc:ffc + 1], scale=1.0)
                STT("dve", hsqT.ap[:, ffc, :], pb.ap, bf1.ap[:, ffc:ffc + 1], rl.ap, ALU.add, ALU.mult,
                    r=[pb.b, bf1.b, rl.b], w=[hsqT.b])
            for fc in range(8):
                w2 = w2p.get()
                DMA(w2.ap.rearrange("p a b -> p (a b)"), sc_f2[fc * 128:(fc + 1) * 128, :], r=[sb_f2[fc]], w=[w2.b])
                pb = nextbank()
                for k in range(32):
                    MM(pb.ap, w2.ap[:, k, :], hsqT.ap[:, k, :], start=(k == 0), stop=(k == 31), r=[w2.b, hsqT.b], w=[pb.b])
                STT("dve", outT.ap[:, fc, :], pb.ap, modT.ap[:, 40 + fc, b:b + 1], outT.ap[:, fc, :], ALU.mult, ALU.add,
                    r=[pb.b, modT.b, outT.b], w=[outT.b])
            if b == 0 and t == 0:
                dbg("pre2", outT, [128, 8, 512])
            layer_norm(outT, [
                (lambda fc: xa.ap[:, fc, :], lambda fc: ln2g.ap[:, fc:fc + 1], lambda fc: ln2b.ap[:, fc:fc + 1],
                 [ln2g.b, ln2b.b], xa.b)])
            if b == 0 and t == 0:
                dbg("xo", xa, [128, 8, 512])
            for j in range(4):
                os_ = xs.get()
                for half in range(2):
                    pb = nextbank()
                    for q in range(4):
                        fc = half * 4 + q
                        MM(pb.ap[:, q * 128:(q + 1) * 128], xa.ap[:, fc, j * 128:(j + 1) * 128], C(IDENT),
                           r=[xa.b, cst.b], w=[pb.b])
                    CP("act" if half == 0 else "dve", os_.ap[:, half * 512:(half + 1) * 512], pb.ap, r=[pb.b], w=[os_.b])
                DMA(out_d[b, t * 512 + j * 128: t * 512 + (j + 1) * 128, :], os_.ap, r=[os_.b])
        S.barrier()
        AR.top = mb

    for b in range(nbatch):
        batch_program(b)

    print("arena peak bytes/partition:", AR.peak, "ops:", {e: len(S.ops[e]) for e in ENGS})
    S.run()
    es.close()
    return nc, dbg_outs


_CACHE = {}


def _consts():
    i = np.arange(128)[:, None]
    j = np.arange(128)[None, :]
    c = np.zeros((NCONST, 128, 128), np.float32)
    c[0] = (i == j)
    c[1] = 1.0
    c[2] = (i <= j)
    c[3] = (i >= j)
    c[4] = np.where(j < i, 0.0, BIG)
    c[5] = np.where(j > i, 0.0, BIG)
    c[6] = np.where(j >= i, 0.0, -BIG)
    c[7] = np.where(j <= i, 0.0, -BIG)
    bd = ((i // 64) == (j // 64)).astype(np.float32)
    c[8] = bd
    c[9] = 1.0 - bd
    return np.ascontiguousarray(c.transpose(1, 0, 2).reshape(128, NCONST * 128))


def _pf(v, n):
    return np.ascontiguousarray(np.asarray(v, np.float32).reshape(n, 128).T)


def prepare_inputs(inputs, core):
    f = lambda k: np.ascontiguousarray(np.asarray(inputs[k], np.float32))
    b0 = 2 * core
    cc = np.stack([f("c")[b0], f("c")[b0 + 1], f("c_ctx")], axis=0)
    cT = np.ascontiguousarray(cc.reshape(3, 8, 128).transpose(2, 1, 0).reshape(128, 24))
    cq = f("conv_qkv")[0]
    cwq = np.ascontiguousarray(cq.reshape(3, 24, 128).transpose(2, 1, 0).reshape(128, 72))
    cb = f("conv_b")[0]
    cwb = np.ascontiguousarray(cb.reshape(3, 8, 128).transpose(2, 1, 0).reshape(128, 24))
    m = {
        "x": np.ascontiguousarray(f("x")[b0:b0 + 2]),
        "ctx": np.ascontiguousarray(f("ctx")[b0:b0 + 2]),
        "cT": cT,
        "w_ada": f("w_ada")[0],
        "b_adaT": _pf(f("b_ada")[0], 48),
        "w_in": f("w_in")[0],
        "cwq": cwq,
        "alog_bc": np.ascontiguousarray(np.broadcast_to(f("a_log")[0].reshape(1, 16), (128, 16))),
        "dtb_bc": np.ascontiguousarray(np.broadcast_to(f("dt_bias")[0].reshape(1, 16), (128, 16))),
        "onw_col": np.ascontiguousarray(f("o_norm_w")[0].reshape(128, 1)),
        "w_a_out": f("w_a_out")[0],
        "cwb": cwb,
        "w_b_out": f("w_b_out")[0],
        "w_o": f("w_o")[0],
        "ln1gT": _pf(f("ln1_g")[0], 8),
        "ln1bT": _pf(f("ln1_b")[0], 8),
        "w_ff1": f("w_ff1")[0],
        "b_ff1T": _pf(f("b_ff1")[0], 32),
        "w_ff2": f("w_ff2")[0],
        "b_ff2T": _pf(f("b_ff2")[0], 8),
        "ln2gT": _pf(f("ln2_g")[0], 8),
        "ln2bT": _pf(f("ln2_b")[0], 8),
        "consts": _consts(),
    }
    return m


def kernel(**inputs):
    if "nc" not in _CACHE:
        _CACHE["nc"] = build_program()[0]
    nc = _CACHE["nc"]
    n = 8
    shared = None
    in_maps = []
    for c in range(n):
        m = prepare_inputs(inputs, c)
        if shared is None:
            shared = m
        else:
            for k in m:
                if k not in ("x", "ctx", "cT"):
                    m[k] = shared[k]
        in_maps.append(m)
    res = run_bass_kernel_spmd(nc, in_maps, core_ids=list(range(n)))
    out = np.concatenate([np.asarray(r["out"], np.float32) for r in res.results], axis=0)
    return out
```

```python
import numpy as np
from contextlib import ExitStack
import concourse.bass as bass
import concourse.mybir as mybir
from concourse.bass_utils import run_bass_kernel_spmd

F32 = mybir.dt.float32
BF16 = mybir.dt.bfloat16
AF = mybir.ActivationFunctionType
ALU = mybir.AluOpType
AX = mybir.AxisListType

ENGS = ("pe", "act", "dve", "pool", "sp")
N_LANES = 24

D = 1024
SEQ = 2048
CTXL = 256
NTOK = SEQ + CTXL
NBLK = NTOK // 128
NH = 8
P_TOTAL = 9248
ALPHA = 2.0 ** 0.25
LN_EPS = 1e-5
RMS_EPS = 1e-6
L2_EPS = 1e-6
TILES = [(0, 256), (256, 512), (768, 512), (1280, 512), (1792, 512)]
ORDER = [list(range(18)), [1, 0] + list(range(17, 1, -1))]
NCONST = 10
BIG = 30000.0


class Buf:
    __slots__ = ("name", "w", "r", "excl")

    def __init__(self, name, excl=False):
        self.name = name
        self.w = None
        self.r = []
        self.excl = excl


class Op:
    __slots__ = ("eng", "fn", "deps", "idx", "sig", "tok", "is_dma")

    def __init__(self, eng, fn):
        self.eng = eng
        self.fn = fn
        self.deps = []
        self.sig = False
        self.tok = None
        self.is_dma = False


class Sched:
    def __init__(self, nc):
        self.nc = nc
        self.ops = {e: [] for e in ENGS}
        self.lane_cnt = [0] * N_LANES
        self.lane_last = [None] * N_LANES
        self.next_lane = 0

    def _track(self, op, reads, writes):
        deps = []
        rd, wr = [], []
        for b in reads:
            (wr if b.excl else rd).append(b)
        wr.extend(writes)
        for b in rd:
            if b.w is not None:
                deps.append(b.w)
        for b in wr:
            if b.w is not None:
                deps.append(b.w)
            deps.extend(b.r)
        for b in rd:
            b.r.append(op)
        for b in wr:
            b.w = op
            b.r = []
        seen = set()
        for d in deps:
            if d is op or id(d) in seen:
                continue
            seen.add(id(d))
            if d.eng == "pe" and op.eng == "pe" and not d.is_dma and not op.is_dma:
                continue
            op.deps.append(d)

    def op(self, eng, fn, reads=(), writes=()):
        o = Op(eng, fn)
        self._track(o, reads, writes)
        self.ops[eng].append(o)
        return o

    def dma(self, fn, reads=(), writes=(), eng="sp"):
        o = Op(eng, fn)
        o.is_dma = True
        lane = self.next_lane
        self.next_lane = (self.next_lane + 1) % N_LANES
        prev = self.lane_last[lane]
        self._track(o, reads, writes)
        if prev is not None and all(d is not prev for d in o.deps):
            o.deps.append(prev)
        self.lane_cnt[lane] += 16
        o.tok = ("lane", lane, self.lane_cnt[lane])
        self.lane_last[lane] = o
        self.ops[eng].append(o)
        return o

    def barrier(self):
        lasts = []
        for e in ENGS:
            for o in reversed(self.ops[e]):
                if not o.is_dma:
                    lasts.append(o)
                    break
        lasts.extend(o for o in self.lane_last if o is not None)
        for e in ENGS:
            o = Op(e, lambda eng: eng.nop())
            o.deps = [d for d in lasts if not (d.eng == e and not d.is_dma)]
            self.ops[e].append(o)

    def run(self):
        nc = self.nc
        for e in ENGS:
            for o in self.ops[e]:
                for d in o.deps:
                    if not d.is_dma:
                        d.sig = True
        for e in ENGS:
            c = 0
            for o in self.ops[e]:
                if o.is_dma:
                    continue
                if o.sig:
                    c += 1
                    o.tok = ("eng", e, c)
        with ExitStack() as es:
            esem = {e: es.enter_context(nc.semaphore("s_" + e)) for e in ENGS}
            lsem = [es.enter_context(nc.semaphore("l_%d" % i)) for i in range(N_LANES)]
            block = es.enter_context(nc.Block())

            def body(ename, final=False):
                def _f(eng):
                    waited = {}
                    for o in self.ops[ename]:
                        need = {}
                        for d in o.deps:
                            k = (d.tok[0], d.tok[1])
                            if d.tok[2] > need.get(k, 0):
                                need[k] = d.tok[2]
                        for k, v in need.items():
                            if waited.get(k, 0) >= v:
                                continue
                            eng.wait_ge(esem[k[1]] if k[0] == "eng" else lsem[k[1]], v)
                            waited[k] = v
                        ins = o.fn(eng)
                        if o.is_dma:
                            ins.then_inc(lsem[o.tok[1]], 16)
                        elif o.sig:
                            ins.then_inc(esem[ename], 1)
                    if final:
                        for lane in range(N_LANES):
                            if self.lane_cnt[lane] > 0:
                                eng.wait_ge(lsem[lane], self.lane_cnt[lane])
                return _f

            block.tensor(body("pe"))
            block.scalar(body("act"))
            block.vector(body("dve"))
            block.gpsimd(body("pool"))
            block.sync(body("sp", final=True))


class T:
    __slots__ = ("ap", "b")

    def __init__(self, ap, b):
        self.ap = ap
        self.b = b


class Arena:
    def __init__(self, tens, nf32):
        self.t = tens
        self.cap = nf32 * 4
        self.top = 0
        self.peak = 0

    def alloc(self, free_shape, dt, name, at=None):
        n = int(np.prod(free_shape))
        esz = 4 if dt == F32 else 2
        nbytes = (n * esz + 3) // 4 * 4
        if at is None:
            off = (self.top + 63) // 64 * 64
            self.top = off + nbytes
            self.peak = max(self.peak, self.top)
            assert self.top <= self.cap, ("SBUF arena overflow", name, self.top, self.cap)
        else:
            off = at
        ap = self.t[:, off // 4: off // 4 + nbytes // 4]
        if dt == BF16:
            ap = ap.bitcast(BF16)
        ap = ap[:, 0:n]
        if len(free_shape) == 2:
            ap = ap.rearrange("p (a b) -> p a b", a=free_shape[0])
        elif len(free_shape) == 3:
            ap = ap.rearrange("p (a b c) -> p a b c", a=free_shape[0], b=free_shape[1])
        return T(ap, Buf(name))


class Pool:
    def __init__(self, arena, n, free_shape, dt, name):
        self.items = [arena.alloc(free_shape, dt, "%s%d" % (name, i)) for i in range(n)]
        self.i = 0

    def get(self):
        t = self.items[self.i]
        self.i = (self.i + 1) % len(self.items)
        return t


def build_program(debug=None, stage=99, nbatch=2):
    nc = bass.Bass("TRN2", target_bir_lowering=False)
    S = Sched(nc)
    dbg_outs = {}

    def din(name, shape, dt=F32):
        return nc.dram_tensor(name, list(shape), dt, kind="ExternalInput").ap()

    x_d = din("x", [2, SEQ, D])
    ctx_d = din("ctx", [2, CTXL, D])
    cT_d = din("cT", [128, 24])
    wada_d = din("w_ada", [D, 6 * D])
    badaT_d = din("b_adaT", [128, 48])
    win_d = din("w_in", [D, P_TOTAL])
    cwq_d = din("cwq", [128, 72])
    alog_d = din("alog_bc", [128, 16])
    dtb_d = din("dtb_bc", [128, 16])
    onw_d = din("onw_col", [128, 1])
    wa_d = din("w_a_out", [D, D])
    cwb_d = din("cwb", [128, 24])
    wb_d = din("w_b_out", [D, D])
    wo_d = din("w_o", [D, D])
    ln1g_d = din("ln1gT", [128, 8])
    ln1b_d = din("ln1bT", [128, 8])
    wf1_d = din("w_ff1", [D, 4 * D])
    bf1_d = din("b_ff1T", [128, 32])
    wf2_d = din("w_ff2", [4 * D, D])
    bf2_d = din("b_ff2T", [128, 8])
    ln2g_d = din("ln2gT", [128, 8])
    ln2b_d = din("ln2bT", [128, 8])
    const_d = din("consts", [128, NCONST * 128])
    out_d = nc.dram_tensor("out", [2, SEQ, D], F32, kind="ExternalOutput").ap()

    def dscr(name, nchunk, K):
        t = nc.dram_tensor(name, [nchunk * 128, K * 128], BF16, kind="Internal").ap()
        return t, [Buf("%s_%d" % (name, i)) for i in range(nchunk)]

    sc_in, sb_in = dscr("sc_in", 72, 8)
    sc_bg_t = nc.dram_tensor("sc_bg", [128, 256], BF16, kind="Internal").ap()
    sb_bg = Buf("sc_bg")
    sc_a, sb_a = dscr("sc_a", 8, 8)
    sc_b, sb_b = dscr("sc_b", 8, 8)
    sc_o, sb_o = dscr("sc_o", 8, 8)
    sc_f1, sb_f1 = dscr("sc_f1", 32, 8)
    sc_f2, sb_f2 = dscr("sc_f2", 8, 32)

    def dbg(name, t, shape, dt=F32):
        if not debug or name not in debug:
            return
        o = nc.dram_tensor("dbg_" + name, list(shape), dt, kind="ExternalOutput").ap()
        dbg_outs[name] = o
        ap = t.ap
        S.dma(lambda e: e.dma_start(out=o, in_=ap), reads=[t.b])

    es = ExitStack()
    NF32 = 53200
    arena_t = es.enter_context(nc.sbuf_tensor("arena", [128, NF32], F32))
    AR = Arena(arena_t, NF32)
    banks = []
    for i in range(8):
        pt = es.enter_context(nc.psum_tensor("ps%d" % i, [128, 512], F32))
        banks.append(T(pt[:, :], Buf("ps%d" % i, excl=True)))
    rr = [0]
    rr_lo = [0]

    def nextbank():
        if rr[0] < rr_lo[0]:
            rr[0] = rr_lo[0]
        b = banks[rr[0]]
        rr[0] += 1
        if rr[0] >= 8:
            rr[0] = rr_lo[0]
        return b

    def MM(out, lhsT, rhs, start=True, stop=True, r=(), w=()):
        S.op("pe", lambda e: e.matmul(out, lhsT=lhsT, rhs=rhs, start=start, stop=stop), r, w)

    def ACT(out, in_, func, r=(), w=(), bias=None, scale=1.0):
        if bias is None:
            S.op("act", lambda e: e.activation(out=out, in_=in_, func=func, scale=scale), r, w)
        else:
            S.op("act", lambda e: e.activation(out=out, in_=in_, func=func, bias=bias, scale=scale), r, w)

    def TTop(eng, out, in0, in1, op, r=(), w=()):
        S.op(eng, lambda e: e.tensor_tensor(out=out, in0=in0, in1=in1, op=op), r, w)

    def STT(eng, out, in0, scalar, in1, op0, op1, r=(), w=()):
        S.op(eng, lambda e: e.scalar_tensor_tensor(out=out, in0=in0, scalar=scalar, in1=in1, op0=op0, op1=op1), r, w)

    def TS(eng, out, in0, s1, s2, op0, op1=None, r=(), w=()):
        if op1 is None:
            S.op(eng, lambda e: e.tensor_scalar(out=out, in0=in0, scalar1=s1, scalar2=None, op0=op0), r, w)
        else:
            S.op(eng, lambda e: e.tensor_scalar(out=out, in0=in0, scalar1=s1, scalar2=s2, op0=op0, op1=op1), r, w)

    def CP(eng, out, in_, r=(), w=()):
        if eng == "act":
            S.op("act", lambda e: e.copy(out=out, in_=in_), r, w)
        else:
            S.op(eng, lambda e: e.tensor_copy(out=out, in_=in_), r, w)

    def DMA(out, in_, r=(), w=()):
        S.dma(lambda e: e.dma_start(out=out, in_=in_), r, w)

    cst = AR.alloc([NCONST, 128], F32, "cst")
    DMA(cst.ap.rearrange("p a b -> p (a b)"), const_d, w=[cst.b])
    IDENT, ONES, CUM0, CUM1, MPOS0, MPOS1, MTN0, MTN1, BDM, NBDM = range(10)

    def C(i):
        return cst.ap[:, i, :]

    cstb = AR.alloc([4, 128], BF16, "cstb")
    for j, i in enumerate([IDENT, ONES, BDM, NBDM]):
        CP("dve", cstb.ap[:, j, :], C(i), r=[cst.b], w=[cstb.b])
    IDB, ONB, BDB, NBDB = [cstb.ap[:, j, :] for j in range(4)]

    kc = AR.alloc([8], F32, "kcols")
    KV = {"eps4": 4 * L2_EPS, "eps4q": 4 * L2_EPS * 128.0, "rmseps": RMS_EPS, "lneps": LN_EPS, "mhalf": -0.5,
          "one": 1.0, "zero": 0.0}
    KI = {}
    for j, (k, v) in enumerate(KV.items()):
        KI[k] = j
        S.op("pool", lambda e, j=j, v=v: e.memset(kc.ap[:, j:j + 1], v), (), [kc.b])

    def KC(name):
        j = KI[name]
        return kc.ap[:, j:j + 1]

    def small(name, shape):
        return AR.alloc(shape, F32, name)

    cT = small("cT", [8, 3])
    badaT = small("badaT", [48])
    cwq = small("cwq", [24, 3])
    cwb = small("cwb", [8, 3])
    alog = small("alog", [16])
    dtb = small("dtb", [16])
    onwc = small("onwc", [1])
    ln1g = small("ln1g", [8]); ln1b = small("ln1b", [8]); ln2g = small("ln2g", [8]); ln2b = small("ln2b", [8])
    bf1 = small("bf1", [32]); bf2 = small("bf2", [8])
    DMA(cT.ap.rearrange("p a b -> p (a b)"), cT_d, w=[cT.b])
    DMA(badaT.ap, badaT_d, w=[badaT.b])
    DMA(cwq.ap.rearrange("p a b -> p (a b)"), cwq_d, w=[cwq.b])
    DMA(cwb.ap.rearrange("p a b -> p (a b)"), cwb_d, w=[cwb.b])
    DMA(alog.ap, alog_d, w=[alog.b])
    DMA(dtb.ap, dtb_d, w=[dtb.b])
    DMA(onwc.ap, onw_d, w=[onwc.b])
    for t_, d_ in ((ln1g, ln1g_d), (ln1b, ln1b_d), (ln2g, ln2g_d), (ln2b, ln2b_d), (bf1, bf1_d), (bf2, bf2_d)):
        DMA(t_.ap, d_, w=[t_.b])
    ACT(onwc.ap, onwc.ap, AF.Copy, r=[onwc.b], w=[onwc.b], scale=0.5)
    nA = small("nA", [16])
    ACT(nA.ap, alog.ap, AF.Exp, r=[alog.b], w=[nA.b])
    TS("dve", nA.ap, nA.ap, -1.0, None, ALU.mult, r=[nA.b], w=[nA.b])
    modT = small("modT", [48, 3])
    s1c = small("s1c", [8, 3])
    hg1 = small("hg1", [8, 2])
    A1s = small("A1s", [8])
    A1b = small("A1b", [8, 2])
    U2s = small("U2s", [8, 2])
    U2b = small("U2b", [8, 2])
    wtp = Pool(AR, 6, [8, 128], BF16, "wt")
    mark_persist = AR.top

    cast_head, cast_rest = [], []
    cast_jobs = cast_head

    def mk_job(W, K0, K, c0, ncol, dst_ap, dst_buf):
        src = W.rearrange("(k p) c -> p k c", p=128)[:, K0:K0 + K, c0:c0 + ncol]
        dst = dst_ap.rearrange("p (k c) -> p k c", k=K)
        cast_jobs.append((src, dst, dst_buf))

    def in_col(c):
        return c * 128 if c < 24 else 3104 + (c - 24) * 128

    mk_job(win_d, 0, 8, 3072, 32, sc_bg_t, sb_bg)
    for h in range(NH):
        for c in (h, 8 + h, 16 + h, 24 + h):
            mk_job(win_d, 0, 8, in_col(c), 128, sc_in[c * 128:(c + 1) * 128, :], sb_in[c])
    cast_jobs = cast_rest
    for c in range(32, 72):
        mk_job(win_d, 0, 8, in_col(c), 128, sc_in[c * 128:(c + 1) * 128, :], sb_in[c])
    for W, sc, sb in ((wa_d, sc_a, sb_a), (wb_d, sc_b, sb_b), (wo_d, sc_o, sb_o)):
        for c in range(8):
            mk_job(W, 0, 8, c * 128, 128, sc[c * 128:(c + 1) * 128, :], sb[c])
    for c in range(32):
        mk_job(wf1_d, 0, 8, c * 128, 128, sc_f1[c * 128:(c + 1) * 128, :], sb_f1[c])
    for c in range(8):
        for kq in range(4):
            mk_job(wf2_d, kq * 8, 8, c * 128, 128, sc_f2[c * 128:(c + 1) * 128, kq * 1024:(kq + 1) * 1024], sb_f2[c])
    def emit_casts(lst, n):
        for _ in range(n):
            if not lst:
                return
            src, dst, dbuf = lst.pop(0)
            S.dma(lambda e, src=src, dst=dst: e.dma_start(out=dst, in_=src), reads=[], writes=[dbuf], eng="pool")

    def mod_phase():
        stg = Pool(AR, 3, [8, 128], F32, "mstg")
        sil = AR.alloc([8, 3], F32, "silc")
        th = AR.alloc([8, 3], F32, "silt")
        ACT(th.ap, cT.ap, AF.Tanh, r=[cT.b], w=[th.b], scale=0.5)
        STT("dve", sil.ap, th.ap, 1.0, cT.ap, ALU.add, ALU.mult, r=[th.b, cT.b], w=[sil.b])
        TS("dve", sil.ap, sil.ap, 0.5, None, ALU.mult, r=[sil.b], w=[sil.b])
        pb = banks[0]
        for fc in range(48):
            s = stg.get()
            DMA(s.ap, wada_d.rearrange("(k p) c -> p k c", p=128)[:, :, fc * 128:(fc + 1) * 128], w=[s.b])
            for k in range(8):
                MM(pb.ap[:, fc * 3:fc * 3 + 3], s.ap[:, k, :], sil.ap[:, k, :], start=(k == 0), stop=(k == 7),
                   r=[s.b, sil.b], w=[pb.b])
        TTop("dve", modT.ap, pb.ap[:, 0:144].rearrange("p (a b) -> p a b", a=48),
             badaT.ap.unsqueeze(2).to_broadcast([128, 48, 3]), ALU.add, r=[pb.b, badaT.b], w=[modT.b])
        m = modT.ap
        TS("dve", s1c.ap, m[:, 8:16, :], 1.0, None, ALU.add, r=[modT.b], w=[s1c.b])
        TS("dve", hg1.ap, m[:, 16:24, 0:2], 0.5, None, ALU.mult, r=[modT.b], w=[hg1.b])
        TS("dve", A1s.ap, ln1g.ap, ALPHA, None, ALU.mult, r=[ln1g.b], w=[A1s.b])
        tmp = AR.alloc([8, 2], F32, "mtmp")
        TTop("dve", tmp.ap, m[:, 40:48, 0:2], bf2.ap.unsqueeze(2).to_broadcast([128, 8, 2]), ALU.mult,
             r=[modT.b, bf2.b], w=[tmp.b])
        STT("dve", A1b.ap, ln1b.ap.unsqueeze(2).to_broadcast([128, 8, 2]), ALPHA, tmp.ap, ALU.mult, ALU.add,
            r=[ln1b.b, tmp.b], w=[A1b.b])
        tmp2 = AR.alloc([8, 2], F32, "mtmp2")
        TS("dve", tmp2.ap, m[:, 32:40, 0:2], 1.0, None, ALU.add, r=[modT.b], w=[tmp2.b])
        TTop("dve", U2s.ap, tmp2.ap, ln1g.ap.unsqueeze(2).to_broadcast([128, 8, 2]), ALU.mult,
             r=[tmp2.b, ln1g.b], w=[U2s.b])
        TTop("dve", U2b.ap, tmp2.ap, ln1b.ap.unsqueeze(2).to_broadcast([128, 8, 2]), ALU.mult,
             r=[tmp2.b, ln1b.b], w=[U2b.b])
        TTop("dve", U2b.ap, U2b.ap, m[:, 24:32, 0:2], ALU.add, r=[U2b.b, modT.b], w=[U2b.b])

    m0 = AR.top
    emit_casts(cast_head, 1 + 8)
    mod_phase()
    dbg("modT", modT, [128, 144])
    S.barrier()
    AR.top = m0

    def batch_program(b):
        mb = AR.top
        uT = [AR.alloc([8, ln], BF16, "uT%d" % ti) for ti, (st, ln) in enumerate(TILES)]
        og_scr = nc.dram_tensor("og_scr%d" % b, [NH, 128, SEQ], BF16, kind="Internal").ap()
        og_buf = [[Buf("ogs%d_%d_%d" % (b, h, t)) for t in range(4)] for h in range(NH)]
        m_phase = AR.top

        def utile(tok):
            for ti, (st, ln) in enumerate(TILES):
                if st <= tok < st + ln:
                    return ti, tok - st
            raise ValueError

        xs = Pool(AR, 6, [1024], F32, "xs")
        for ti, (st, ln) in enumerate(TILES):
            nb = ln // 128
            blks = []
            for j in range(nb):
                s = xs.get()
                if ti == 0:
                    src = ctx_d[b, j * 128:(j + 1) * 128, :]
                else:
                    src = x_d[b, st - CTXL + j * 128: st - CTXL + (j + 1) * 128, :]
                DMA(s.ap, src, w=[s.b])
                blks.append(s)
            jm = 2 if ti == 0 else b
            for fc in range(8):
                pb = nextbank()
                for j in range(nb):
                    MM(pb.ap[:, j * 128:(j + 1) * 128], blks[j].ap[:, fc * 128:(fc + 1) * 128], C(IDENT),
                       r=[blks[j].b, cst.b], w=[pb.b])
                ACT(uT[ti].ap[:, fc, :], pb.ap[:, 0:ln], AF.Identity, r=[pb.b, s1c.b, modT.b], w=[uT[ti].b],
                    bias=modT.ap[:, fc, jm:jm + 1], scale=s1c.ap[:, fc, jm:jm + 1])
        if b == 0:
            dbg("uT1", uT[1], [128, 8, 512], BF16)
        S.barrier()
        AR.top = m_phase
        if stage < 1:
            AR.top = mb
            return

        def pcol(name):
            return AR.alloc([NBLK, 16], F32, name)

        bt = pcol("bt"); hbt = pcol("hbt"); gg = pcol("gg"); Gc = pcol("Gc")
        n2eG = pcol("n2eG"); kdec = pcol("kdec"); eGl = pcol("eGl")
        m_A = AR.top
        wbg = AR.alloc([8, 32], BF16, "wbg")
        DMA(wbg.ap.rearrange("p a b -> p (a b)"), sc_bg_t, r=[sb_bg], w=[wbg.b])
        tb = AR.alloc([NBLK, 16], F32, "pc_tb")
        xg = AR.alloc([NBLK, 16], F32, "pc_xg")
        for half in range(2):
            pb = nextbank()
            for j in range(9):
                blk = half * 9 + j
                ti, off = utile(blk * 128)
                for k in range(8):
                    MM(pb.ap[:, j * 32:(j + 1) * 32], uT[ti].ap[:, k, off:off + 128], wbg.ap[:, k, :],
                       start=(k == 0), stop=(k == 7), r=[uT[ti].b, wbg.b], w=[pb.b])
            pv = pb.ap[:, 0:288].rearrange("p (a b) -> p a b", a=9)
            sl = slice(half * 9, half * 9 + 9)
            ACT(tb.ap[:, sl, :], pv[:, :, 0:16], AF.Tanh, r=[pb.b], w=[tb.b], scale=0.5)
            TTop("dve", xg.ap[:, sl, :], pv[:, :, 16:32], dtb.ap.unsqueeze(1).to_broadcast([128, 9, 16]), ALU.add,
                 r=[pb.b, dtb.b], w=[xg.b])
        TS("dve", bt.ap, tb.ap, 0.5, 0.5, ALU.mult, ALU.add, r=[tb.b], w=[bt.b])
        TS("dve", hbt.ap, tb.ap, 0.25, 0.25, ALU.mult, ALU.add, r=[tb.b], w=[hbt.b])
        TS("dve", xg.ap, xg.ap, 30.0, None, ALU.min, r=[xg.b], w=[xg.b])
        ACT(xg.ap, xg.ap, AF.Exp, r=[xg.b], w=[xg.b])
        ACT(xg.ap, xg.ap, AF.Ln, r=[xg.b, kc.b], w=[xg.b], bias=KC("one"))
        TTop("dve", gg.ap, xg.ap, nA.ap.unsqueeze(1).to_broadcast([128, NBLK, 16]), ALU.mult, r=[xg.b, nA.b], w=[gg.b])
        pb = nextbank()
        pv = pb.ap[:, 0:288].rearrange("p (a b) -> p a b", a=NBLK)
        for d in range(2):
            MM(pv[:, :, d * 8:(d + 1) * 8], C(CUM0 + d), gg.ap[:, :, d * 8:(d + 1) * 8], r=[cst.b, gg.b], w=[pb.b])
        CP("act", Gc.ap, pv, r=[pb.b], w=[Gc.b])
        pb2 = nextbank()
        pv2 = pb2.ap[:, 0:288].rearrange("p (a b) -> p a b", a=NBLK)
        MM(pb2.ap[:, 0:288], C(ONES), gg.ap.rearrange("p a b -> p (a b)"), r=[cst.b, gg.b], w=[pb2.b])
        ACT(eGl.ap, pv2, AF.Exp, r=[pb2.b], w=[eGl.b])
        TTop("dve", kdec.ap, pv2, Gc.ap, ALU.subtract, r=[pb2.b, Gc.b], w=[kdec.b])
        ACT(kdec.ap, kdec.ap, AF.Exp, r=[kdec.b], w=[kdec.b])
        ACT(n2eG.ap, Gc.ap, AF.Exp, r=[Gc.b], w=[n2eG.b])
        TS("dve", n2eG.ap, n2eG.ap, -2.0, None, ALU.mult, r=[n2eG.b], w=[n2eG.b])
        if b == 0:
            dbg("bt", bt, [128, NBLK, 16]); dbg("gg", gg, [128, NBLK, 16]); dbg("Gc", Gc, [128, NBLK, 16])
            dbg("kdec", kdec, [128, NBLK, 16]); dbg("eGl", eGl, [128, NBLK, 16])
        AR.top = m_A

        qTs = [AR.alloc([NTOK], BF16, "qT%d" % i) for i in range(2)]
        kTs = [AR.alloc([NTOK], BF16, "kT%d" % i) for i in range(2)]
        k_tms = [AR.alloc([NBLK, 128], BF16, "k_tm%d" % i) for i in range(2)]
        v_tms = [AR.alloc([NBLK, 128], BF16, "v_tm%d" % i) for i in range(2)]
        zs2Ts = [AR.alloc([SEQ], BF16, "zs2T%d" % i) for i in range(2)]
        WSn = ("Q", "AT", "Ub", "nWT", "Bb")
        WS = [{n_: AR.alloc([12, 128], BF16, "ws%d_%s" % (p_, n_)) for n_ in WSn} for p_ in range(2)]
        o_acc = AR.alloc([16, 128], F32, "o_acc")
        tf = Pool(AR, 3, [512], F32, "tf")
        tb16 = Pool(AR, 3, [512], BF16, "tb")
        chain = [[AR.alloc([4, 128], BF16, "ch%d_%d" % (g_, i)) for i in range(8)] for g_ in range(3)]
        S32 = [[AR.alloc([128], F32, "S32_%d_%d" % (d, i)) for i in range(2)] for d in range(2)]
        Sb = [[AR.alloc([128], BF16, "Sb_%d_%d" % (d, i)) for i in range(2)] for d in range(2)]
        e_ct = AR.alloc([512], F32, "e_ct")
        e_s2 = AR.alloc([512], F32, "e_s2")
        e_dg = AR.alloc([512], F32, "e_dg")
        e_sq = AR.alloc([512], BF16, "e_sq")
        e_vt = e_sq
        e_rs = AR.alloc([8], F32, "e_rs")
        on_all = AR.alloc([16, 128], BF16, "on_all")
        dec = [(AR.alloc([512], F32, "nD%d" % g_), AR.alloc([512], F32, "DT%d" % g_)) for g_ in range(3)]
        ssq = AR.alloc([16], F32, "ssq")
        rstd = AR.alloc([16], F32, "rstd")

        print("phase A top:", AR.top, "cap", AR.cap)

        def early_gen(h):
            qT, kT, k_tm, v_tm, zs2T = qTs[h % 2], kTs[h % 2], k_tms[h % 2], v_tms[h % 2], zs2Ts[h % 2]
            wts = []
            for c in (h, 8 + h, 16 + h, 24 + h):
                wt = wtp.get()
                DMA(wt.ap.rearrange("p a b -> p (a b)"), sc_in[c * 128:(c + 1) * 128, :], r=[sb_in[c]], w=[wt.b])
                wts.append(wt)
            yield
            for ti, (st, ln) in enumerate(TILES):
                R = 1 if ti == 0 else ln // 64
                L = ln // R
                nb = ln // 128
                for wi, name in enumerate(("q", "k", "v")):
                    pb = nextbank()
                    for k in range(8):
                        MM(pb.ap[:, 0:ln], wts[wi].ap[:, k, :], uT[ti].ap[:, k, :], start=(k == 0), stop=(k == 7),
                           r=[wts[wi].b, uT[ti].b], w=[pb.b])
                    fcw = wi * 8 + h
                    ct = e_ct
                    ACT(ct.ap[:, 0:ln], pb.ap[:, 0:ln], AF.Copy, r=[pb.b, cwq.b], w=[ct.b], scale=cwq.ap[:, fcw, 1:2])
                    pv = pb.ap[:, 0:ln].rearrange("p (r l) -> p r l", r=R)
                    cv = ct.ap[:, 0:ln].rearrange("p (r l) -> p r l", r=R)
                    STT("dve", cv[:, :, 1:L], pv[:, :, 0:L - 1], cwq.ap[:, fcw, 0:1], cv[:, :, 1:L], ALU.mult, ALU.add,
                        r=[pb.b, cwq.b, ct.b], w=[ct.b])
                    STT("dve", cv[:, :, 0:L - 1], pv[:, :, 1:L], cwq.ap[:, fcw, 2:3], cv[:, :, 0:L - 1], ALU.mult, ALU.add,
                        r=[pb.b, cwq.b, ct.b], w=[ct.b])
                    s2 = e_s2
                    ACT(s2.ap[:, 0:ln], ct.ap[:, 0:ln], AF.Tanh, r=[ct.b], w=[s2.b], scale=0.5)
                    if name == "v":
                        STT("dve", e_vt.ap[:, 0:ln], s2.ap[:, 0:ln], 1.0, ct.ap[:, 0:ln], ALU.add, ALU.mult,
                            r=[s2.b, ct.b], w=[e_vt.b])
                        yield
                        pbv = nextbank()
                        for j in range(nb):
                            MM(pbv.ap[:, j * 128:(j + 1) * 128], e_vt.ap[:, j * 128:(j + 1) * 128], IDB,
                               r=[e_vt.b, cstb.b], w=[pbv.b])
                        CP("act", v_tm.ap[:, st // 128: st // 128 + nb, :].rearrange("p a b -> p (a b)"),
                           pbv.ap[:, 0:ln], r=[pbv.b], w=[v_tm.b])
                        yield
                        continue
                    STT("dve", s2.ap[:, 0:ln], s2.ap[:, 0:ln], 1.0, ct.ap[:, 0:ln], ALU.add, ALU.mult,
                        r=[s2.b, ct.b], w=[s2.b])
                    TTop("pool", e_sq.ap[:, 0:ln], s2.ap[:, 0:ln], s2.ap[:, 0:ln], ALU.mult, r=[s2.b], w=[e_sq.b])
                    yield
                    pb2 = nextbank()
                    for j in range(nb):
                        MM(pb2.ap[:, j:j + 1], e_sq.ap[:, j * 128:(j + 1) * 128], ONB[:, 0:1], r=[e_sq.b, cstb.b], w=[pb2.b])
                    rs = e_rs
                    if name == "q":
                        ACT(rs.ap[:, 0:nb], pb2.ap[:, 0:nb], AF.Identity, r=[pb2.b, kc.b], w=[rs.b],
                            bias=KC("eps4q"), scale=128.0)
                    else:
                        ACT(rs.ap[:, 0:nb], pb2.ap[:, 0:nb], AF.Identity, r=[pb2.b, kc.b], w=[rs.b],
                            bias=KC("eps4"), scale=1.0)
                    TTop("pool", rs.ap[:, 0:nb], rs.ap[:, 0:nb], KC("mhalf").to_broadcast([128, nb]), ALU.pow,
                         r=[rs.b, kc.b], w=[rs.b])
                    dg = e_dg
                    dgv = dg.ap[:, 0:ln].rearrange("p (a b) -> p a b", a=nb)
                    TTop("pool", dgv, C(IDENT).unsqueeze(1).to_broadcast([128, nb, 128]),
                         rs.ap[:, 0:nb].unsqueeze(2).to_broadcast([128, nb, 128]), ALU.mult,
                         r=[cst.b, rs.b], w=[dg.b])
                    yield
                    pb3 = nextbank()
                    for j in range(nb):
                        MM(pb3.ap[:, j * 128:(j + 1) * 128], C(ONES), dgv[:, j, :], r=[cst.b, dg.b], w=[pb3.b])
                    dst = qT if name == "q" else kT
                    TTop("dve", dst.ap[:, st:st + ln], s2.ap[:, 0:ln], pb3.ap[:, 0:ln], ALU.mult,
                         r=[s2.b, pb3.b], w=[dst.b])
                    yield
                    if name == "k":
                        pbt = nextbank()
                        for j in range(nb):
                            MM(pbt.ap[:, j * 128:(j + 1) * 128], kT.ap[:, st + j * 128: st + (j + 1) * 128], IDB,
                               r=[kT.b, cstb.b], w=[pbt.b])
                        CP("act", k_tm.ap[:, st // 128: st // 128 + nb, :].rearrange("p a b -> p (a b)"),
                           pbt.ap[:, 0:ln], r=[pbt.b], w=[k_tm.b])
                        yield
            for ti, (st, ln) in enumerate(TILES):
                if ti == 0:
                    continue
                pb = nextbank()
                for k in range(8):
                    MM(pb.ap[:, 0:ln], wts[3].ap[:, k, :], uT[ti].ap[:, k, :], start=(k == 0), stop=(k == 7),
                       r=[wts[3].b, uT[ti].b], w=[pb.b])
                ACT(e_ct.ap[:, 0:ln], pb.ap[:, 0:ln], AF.Tanh, r=[pb.b], w=[e_ct.b], scale=0.5)
                STT("dve", zs2T.ap[:, st - CTXL: st - CTXL + ln], e_ct.ap[:, 0:ln], 1.0, pb.ap[:, 0:ln],
                    ALU.add, ALU.mult, r=[e_ct.b, pb.b], w=[zs2T.b])
                yield
            if b == 0 and h == 0:
                dbg("qT", qT, [128, NTOK], BF16); dbg("kT", kT, [128, NTOK], BF16)
                dbg("v_tm", v_tm, [128, NBLK, 128], BF16); dbg("zs2T", zs2T, [128, SEQ], BF16)

        def adv(gen, n):
            if gen is None:
                return
            for _ in range(n):
                try:
                    next(gen)
                except StopIteration:
                    return

        def head_ctx(h):
            qT, kT, k_tm, v_tm, zs2T = qTs[h % 2], kTs[h % 2], k_tms[h % 2], v_tms[h % 2], zs2Ts[h % 2]

            def init():
                for d in range(2):
                    S.op("pool", lambda e, d=d: e.memset(S32[d][0].ap, 0.0), (), [S32[d][0].b])
                    S.op("pool", lambda e, d=d: e.memset(Sb[d][0].ap, 0.0), (), [Sb[d][0].b])

            def col(t_, blk, d):
                return t_.ap[:, blk, d * 8 + h: d * 8 + h + 1]

            def wave_prep(w):
                units = [(d, 6 * w + s_) for d in range(2) for s_ in range(6)]
                f2 = lambda t_: t_.ap.rearrange("p a b -> p (a b)")
                ws = WS[(3 * h + w) % 2]
                G = []
                for gq in range(3):
                    ch = chain[gq]
                    G.append({"us": units[gq * 4: gq * 4 + 4], "gq": gq, "Mt": ch[0], "R": [ch[1], ch[2]],
                              "P": [ch[3], ch[4]], "Y": [ch[5], ch[6]], "Moff": ch[7], "rp": 0, "y": 0})
                gcs = []
                for g in G:
                    gcum = tf.get()
                    gcv = gcum.ap.rearrange("p (a b) -> p a b", a=4)
                    for j, (d, s_) in enumerate(g["us"]):
                        blk = ORDER[d][s_]
                        TTop("pool", gcv[:, j, :], C(CUM0 + d), col(gg, blk, d).to_broadcast([128, 128]), ALU.mult,
                             r=[cst.b, gg.b], w=[gcum.b])
                    gcs.append((gcum, gcv))
                yield
                for g, (gcum, gcv) in zip(G, gcs):
                    gq = g["gq"]
                    pg = nextbank()
                    for j in range(4):
                        MM(pg.ap[:, j * 128:(j + 1) * 128], C(ONES), gcv[:, j, :], r=[cst.b, gcum.b], w=[pg.b])
                    eGr = gcum
                    ACT(eGr.ap, pg.ap, AF.Exp, r=[pg.b], w=[eGr.b])
                    nD, DT = dec[gq]
                    for j, (d, s_) in enumerate(g["us"]):
                        blk = ORDER[d][s_]
                        sl = slice(j * 128, (j + 1) * 128)
                        STT("dve", nD.ap[:, sl], pg.ap[:, sl], col(Gc, blk, d), C(MPOS0 + d), ALU.subtract, ALU.max,
                            r=[pg.b, Gc.b, cst.b], w=[nD.b])
                        STT("dve", DT.ap[:, sl], pg.ap[:, sl], col(Gc, blk, d), C(MTN0 + d), ALU.subtract, ALU.min,
                            r=[pg.b, Gc.b, cst.b], w=[DT.b])
                    for j, (d, s_) in enumerate(g["us"]):
                        blk = ORDER[d][s_]
                        sl = slice(j * 128, (j + 1) * 128)
                        TTop("pool", ws["Q"].ap[:, gq * 4 + j, :], qT.ap[:, blk * 128:(blk + 1) * 128], eGr.ap[:, sl],
                             ALU.mult, r=[qT.b, eGr.b], w=[ws["Q"].b])
                yield
                for g in G:
                    gq = g["gq"]
                    us = g["us"]
                    nD, DT = dec[gq]
                    ACT(nD.ap, nD.ap, AF.Exp, r=[nD.b], w=[nD.b], scale=-1.0)
                    ACT(DT.ap, DT.ap, AF.Exp, r=[DT.b], w=[DT.b])
                    pkk = nextbank()
                    pkq = nextbank()
                    for j, (d, s_) in enumerate(us):
                        blk = ORDER[d][s_]
                        ks = kT.ap[:, blk * 128:(blk + 1) * 128]
                        MM(pkk.ap[:, j * 128:(j + 1) * 128], ks, ks, r=[kT.b], w=[pkk.b])
                    for j, (d, s_) in enumerate(us):
                        blk = ORDER[d][s_]
                        ks = kT.ap[:, blk * 128:(blk + 1) * 128]
                        MM(pkq.ap[:, j * 128:(j + 1) * 128], ks, qT.ap[:, blk * 128:(blk + 1) * 128],
                           r=[kT.b, qT.b], w=[pkq.b])
                    Mt = g["Mt"]
                    for j, (d, s_) in enumerate(us):
                        blk = ORDER[d][s_]
                        sl = slice(j * 128, (j + 1) * 128)
                        STT("dve", Mt.ap[:, j, :], pkk.ap[:, sl], col(bt, blk, d), nD.ap[:, sl], ALU.mult, ALU.mult,
                            r=[pkk.b, bt.b, nD.b], w=[Mt.b])
                    TTop("dve", ws["AT"].ap[:, gq * 4:gq * 4 + 4, :].rearrange("p a b -> p (a b)"), pkq.ap, DT.ap, ALU.mult,
                         r=[pkq.b, DT.b], w=[ws["AT"].b])
                yield
                for g in G:
                    Mt = g["Mt"]
                    TTop("pool", g["R"][0].ap, Mt.ap, BDB.unsqueeze(1).to_broadcast([128, 4, 128]), ALU.mult,
                         r=[Mt.b, cstb.b], w=[g["R"][0].b])
                    TTop("pool", g["Moff"].ap, Mt.ap, NBDB.unsqueeze(1).to_broadcast([128, 4, 128]), ALU.mult,
                         r=[Mt.b, cstb.b], w=[g["Moff"].b])
                yield
                for g in G:
                    R_, P_, Y_ = g["R"], g["P"], g["Y"]
                    pb = nextbank()
                    for j in range(4):
                        MM(pb.ap[:, j * 128:(j + 1) * 128], R_[0].ap[:, j, :], IDB, r=[R_[0].b, cstb.b], w=[pb.b])
                    CP("act", f2(P_[0]), pb.ap, r=[pb.b], w=[P_[0].b])
                    TTop("dve", Y_[0].ap, IDB.unsqueeze(1).to_broadcast([128, 4, 128]),
                         pb.ap.rearrange("p (a b) -> p a b", a=4), ALU.subtract, r=[cstb.b, pb.b], w=[Y_[0].b])
                yield
                for seg in range(6):
                    for g in G:
                        R_, P_, Y_ = g["R"], g["P"], g["Y"]
                        rp, y = g["rp"], g["y"]
                        if seg >= 1:
                            pc = nextbank()
                            for j in range(4):
                                MM(pc.ap[:, j * 128:(j + 1) * 128], R_[rp].ap[:, j, :], Y_[y].ap[:, j, :],
                                   r=[R_[rp].b, Y_[y].b], w=[pc.b])
                            TTop("dve", f2(Y_[1 - y]), f2(Y_[y]), pc.ap, ALU.add, r=[Y_[y].b, pc.b], w=[Y_[1 - y].b])
                            g["y"] = 1 - y
                        if seg <= 4:
                            pa = nextbank()
                            for j in range(4):
                                MM(pa.ap[:, j * 128:(j + 1) * 128], P_[rp].ap[:, j, :], R_[rp].ap[:, j, :],
                                   r=[P_[rp].b, R_[rp].b], w=[pa.b])
                            CP("act", f2(R_[1 - rp]), pa.ap, r=[pa.b], w=[R_[1 - rp].b])
                            if seg <= 3:
                                pbk = nextbank()
                                for j in range(4):
                                    MM(pbk.ap[:, j * 128:(j + 1) * 128], R_[rp].ap[:, j, :], P_[rp].ap[:, j, :],
                                       r=[P_[rp].b, R_[rp].b], w=[pbk.b])
                                CP("act", f2(P_[1 - rp]), pbk.ap, r=[pbk.b], w=[P_[1 - rp].b])
                            g["rp"] = 1 - rp
                    yield
                for g in G:
                    rp, y = g["rp"], g["y"]
                    g["Yf"], g["YT"], g["Z1"] = g["Y"][y], g["P"][1], g["P"][0]
                    g["ek"], g["kd"], g["Wt"] = g["R"][1 - rp], g["Y"][1 - y], g["R"][rp]
                    Yf, YT, Z1, Moff = g["Yf"], g["YT"], g["Z1"], g["Moff"]
                    pb = nextbank()
                    for j in range(4):
                        MM(pb.ap[:, j * 128:(j + 1) * 128], Yf.ap[:, j, :], IDB, r=[Yf.b, cstb.b], w=[pb.b])
                    CP("act", f2(YT), pb.ap, r=[pb.b], w=[YT.b])
                    pz = nextbank()
                    for j in range(4):
                        MM(pz.ap[:, j * 128:(j + 1) * 128], Moff.ap[:, j, :], Yf.ap[:, j, :], r=[Moff.b, Yf.b], w=[pz.b])
                    CP("act", f2(Z1), pz.ap, r=[pz.b], w=[Z1.b])
                    for j, (d, s_) in enumerate(g["us"]):
                        blk = ORDER[d][s_]
                        TTop("pool", g["ek"].ap[:, j, :], k_tm.ap[:, blk, :], col(n2eG, blk, d).to_broadcast([128, 128]),
                             ALU.mult, r=[k_tm.b, n2eG.b], w=[g["ek"].b])
                        TTop("pool", g["kd"].ap[:, j, :], k_tm.ap[:, blk, :], col(kdec, blk, d).to_broadcast([128, 128]),
                             ALU.mult, r=[k_tm.b, kdec.b], w=[g["kd"].b])
                yield
                for g in G:
                    Yf, YT, Z1 = g["Yf"], g["YT"], g["Z1"]
                    pz2 = nextbank()
                    for j in range(4):
                        MM(pz2.ap[:, j * 128:(j + 1) * 128], YT.ap[:, j, :], Z1.ap[:, j, :], r=[YT.b, Z1.b], w=[pz2.b])
                    tmpT = tf.get()
                    TTop("dve", tmpT.ap, f2(Yf), pz2.ap, ALU.subtract, r=[Yf.b, pz2.b], w=[tmpT.b])
                    TTs = g["Mt"]
                    for j, (d, s_) in enumerate(g["us"]):
                        blk = ORDER[d][s_]
                        TTop("pool", TTs.ap[:, j, :], tmpT.ap[:, j * 128:(j + 1) * 128],
                             col(hbt, blk, d).to_broadcast([128, 128]), ALU.mult, r=[tmpT.b, hbt.b], w=[TTs.b])
                yield
                for g in G:
                    gq = g["gq"]
                    TTs, ek, Wt = g["Mt"], g["ek"], g["Wt"]
                    usl = slice(gq * 4, gq * 4 + 4)
                    pU = nextbank()
                    pW = nextbank()
                    for j, (d, s_) in enumerate(g["us"]):
                        blk = ORDER[d][s_]
                        MM(pU.ap[:, j * 128:(j + 1) * 128], TTs.ap[:, j, :], v_tm.ap[:, blk, :], r=[TTs.b, v_tm.b], w=[pU.b])
                    for j in range(4):
                        MM(pW.ap[:, j * 128:(j + 1) * 128], TTs.ap[:, j, :], ek.ap[:, j, :], r=[TTs.b, ek.b], w=[pW.b])
                    CP("act", ws["Ub"].ap[:, usl, :].rearrange("p a b -> p (a b)"), pU.ap, r=[pU.b], w=[ws["Ub"].b])
                    CP("dve", f2(Wt), pW.ap, r=[pW.b], w=[Wt.b])
                yield
                for g in G:
                    gq = g["gq"]
                    kd, Wt = g["kd"], g["Wt"]
                    usl = slice(gq * 4, gq * 4 + 4)
                    pWT = nextbank()
                    pB = nextbank()
                    pQ = nextbank()
                    for j in range(4):
                        MM(pWT.ap[:, j * 128:(j + 1) * 128], Wt.ap[:, j, :], kd.ap[:, j, :], r=[Wt.b, kd.b], w=[pWT.b])
                    for j in range(4):
                        MM(pB.ap[:, j * 128:(j + 1) * 128], kd.ap[:, j, :], ws["Ub"].ap[:, gq * 4 + j, :],
                           r=[kd.b, ws["Ub"].b], w=[pB.b])
                    for j in range(4):
                        MM(pQ.ap[:, j * 128:(j + 1) * 128], Wt.ap[:, j, :], ws["AT"].ap[:, gq * 4 + j, :],
                           r=[Wt.b, ws["AT"].b], w=[pQ.b])
                    CP("act", ws["nWT"].ap[:, usl, :].rearrange("p a b -> p (a b)"), pWT.ap, r=[pWT.b], w=[ws["nWT"].b])
                    CP("act", ws["Bb"].ap[:, usl, :].rearrange("p a b -> p (a b)"), pB.ap, r=[pB.b], w=[ws["Bb"].b])
                    qv = ws["Q"].ap[:, usl, :].rearrange("p a b -> p (a b)")
                    TTop("dve", qv, qv, pQ.ap, ALU.add, r=[ws["Q"].b, pQ.b], w=[ws["Q"].b])
                yield

            def scan_step(d, s_):
                blk = ORDER[d][s_]
                ws = WS[(3 * h + s_ // 6) % 2]
                u = d * 6 + (s_ % 6)
                cur = s_ % 2
                nxt = 1 - cur
                pst_ = banks[d]
                MM(pst_.ap[:, 0:128], ws["nWT"].ap[:, u, :], Sb[d][cur].ap, start=True, stop=False,
                   r=[ws["nWT"].b, Sb[d][cur].b], w=[pst_.b])
                MM(pst_.ap[:, 0:128], IDB, ws["Bb"].ap[:, u, :], start=False, stop=True,
                   r=[cstb.b, ws["Bb"].b], w=[pst_.b])
                STT("dve", Sb[d][nxt].ap, S32[d][cur].ap, col(eGl, blk, d), pst_.ap[:, 0:128], ALU.mult, ALU.add,
                    r=[S32[d][cur].b, eGl.b, pst_.b], w=[Sb[d][nxt].b])
                STT("dve", S32[d][nxt].ap, S32[d][cur].ap, col(eGl, blk, d), pst_.ap[:, 0:128], ALU.mult, ALU.add,
                    r=[S32[d][cur].b, eGl.b, pst_.b], w=[S32[d][nxt].b])
                if blk >= 2:
                    li = blk - 2
                    po = banks[2]
                    MM(po.ap[:, d * 128:(d + 1) * 128], ws["Q"].ap[:, u, :], Sb[d][cur].ap, start=True, stop=False,
                       r=[ws["Q"].b, Sb[d][cur].b], w=[po.b])
                    MM(po.ap[:, d * 128:(d + 1) * 128], ws["AT"].ap[:, u, :], ws["Ub"].ap[:, u, :], start=False, stop=True,
                       r=[ws["AT"].b, ws["Ub"].b], w=[po.b])
                    first = (d == 0 and li < 8) or (d == 1 and li >= 8)
                    if first:
                        CP("act", o_acc.ap[:, li, :], po.ap[:, d * 128:(d + 1) * 128], r=[po.b], w=[o_acc.b])
                    else:
                        TTop("dve", o_acc.ap[:, li, :], o_acc.ap[:, li, :], po.ap[:, d * 128:(d + 1) * 128], ALU.add,
                             r=[po.b, o_acc.b], w=[o_acc.b])

            def drain(gen):
                for _ in gen:
                    pass

            def finish_gen():
                S.op("pool", lambda e: e.memset(ssq.ap, 0.0), (), [ssq.b])
                for li in range(16):
                    junk = tb16.get()
                    S.op("act", lambda e, li=li, junk=junk: e.activation(out=junk.ap[:, 0:128], in_=o_acc.ap[:, li, :],
                                                                          func=AF.Square, accum_out=ssq.ap[:, li:li + 1]),
                         [o_acc.b], [junk.b, ssq.b])
                ACT(rstd.ap, ssq.ap, AF.Identity, r=[ssq.b, kc.b], w=[rstd.b], bias=KC("rmseps"), scale=1.0 / 128.0)
                TTop("pool", rstd.ap, rstd.ap, KC("mhalf").to_broadcast([128, 16]), ALU.pow, r=[rstd.b, kc.b], w=[rstd.b])
                for li in range(16):
                    ACT(on_all.ap[:, li, :], o_acc.ap[:, li, :], AF.Copy, r=[o_acc.b, rstd.b], w=[on_all.b],
                        scale=rstd.ap[:, li:li + 1])
                if b == 0 and h == 0:
                    dbg("o_acc", o_acc, [128, 16, 128])
                yield
                for g0 in range(0, 16, 4):
                    pb = nextbank()
                    for j in range(4):
                        MM(pb.ap[:, j * 128:(j + 1) * 128], on_all.ap[:, g0 + j, :], IDB, r=[on_all.b, cstb.b], w=[pb.b])
                    ti = g0 // 4
                    ogt = tb16.get()
                    STT("dve", ogt.ap, pb.ap, onwc.ap[:, 0:1], zs2T.ap[:, ti * 512:(ti + 1) * 512], ALU.mult, ALU.mult,
                        r=[pb.b, onwc.b, zs2T.b], w=[ogt.b])
                    DMA(og_scr[h, :, ti * 512:(ti + 1) * 512], ogt.ap, r=[ogt.b], w=[og_buf[h][ti]])
                    if b == 0 and h == 0:
                        dbg("og0_%d" % ti, ogt, [128, 512], BF16)
                    yield

            return {"prep": wave_prep, "scan": scan_step, "fin": finish_gen, "init": init}

        nheads = NH if stage >= 4 else 1
        en = early_gen(0)
        for _ in en:
            pass
        if stage >= 2:
            ctx = [head_ctx(h) for h in range(nheads)]
            seq = [(h, w) for h in range(nheads) for w in range(3)]
            st = {"en": early_gen(1) if nheads > 1 else None, "fin": None}

            def side(n=1):
                for _ in range(n):
                    if st["fin"] is not None:
                        try:
                            next(st["fin"])
                        except StopIteration:
                            st["fin"] = None
                    elif st["en"] is not None:
                        try:
                            next(st["en"])
                            next(st["en"])
                        except StopIteration:
                            st["en"] = None
                    if b == 0:
                        emit_casts(cast_head, 1)
                        emit_casts(cast_rest, 1)

            rr_lo[0] = 3
            cur = ctx[0]["prep"](0)
            for _ in cur:
                side()
            for i, (h, w) in enumerate(seq):
                if w == 0:
                    ctx[h]["init"]()
                nxt = None
                if i + 1 < len(seq):
                    h2, w2 = seq[i + 1]
                    if w2 == 0:
                        while st["fin"] is not None or st["en"] is not None:
                            side()
                        if h + 2 < nheads:
                            st["en"] = early_gen(h + 2)
                    nxt = ctx[h2]["prep"](w2)
                for s_ in range(6 * w, 6 * w + 6):
                    for d in range(2):
                        ctx[h]["scan"](d, s_)
                        adv(nxt, 1)
                        side()
                        adv(nxt, 1)
                if nxt is not None:
                    for _ in nxt:
                        side()
                if w == 2:
                    st["fin"] = ctx[h]["fin"]()
                    next(st["fin"])
            while st["fin"] is not None or st["en"] is not None:
                side()
            rr_lo[0] = 0
        emit_casts(cast_head, 100)
        emit_casts(cast_rest, 1000)
        S.barrier()
        AR.top = m_phase
        if stage < 4:
            AR.top = mb
            return

        ymT = [[AR.alloc([512], BF16, "ym%d_%d" % (fc, t)) for t in range(4)] for fc in range(8)]
        m_B = AR.top
        pT = [[AR.alloc([512], BF16, "p%d_%d" % (fc, t)) for t in range(4)] for fc in range(8)]
        tf = Pool(AR, 8, [512], F32, "tfB")

        def wload(sc, sbufs, c):
            wt = wtp.get()
            DMA(wt.ap.rearrange("p a b -> p (a b)"), sc[c * 128:(c + 1) * 128, :], r=[sbufs[c]], w=[wt.b])
            return wt

        for fc in range(8):
            wxb = wload(sc_in, sb_in, 32 + fc)
            wbgt = wload(sc_in, sb_in, 40 + fc)
            wcg = wload(sc_in, sb_in, 48 + fc)
            for t in range(4):
                ti = t + 1
                pbs = []
                for wt in (wxb, wbgt, wcg):
                    pb = nextbank()
                    for k in range(8):
                        MM(pb.ap, wt.ap[:, k, :], uT[ti].ap[:, k, :], start=(k == 0), stop=(k == 7),
                           r=[wt.b, uT[ti].b], w=[pb.b])
                    pbs.append(pb)
                cgs = tf.get()
                CP("act", cgs.ap, pbs[2].ap, r=[pbs[2].b], w=[cgs.b])
                cgx = tf.get()
                TTop("dve", cgx.ap, pbs[0].ap, cgs.ap, ALU.mult, r=[pbs[0].b, cgs.b], w=[cgx.b])
                ct = tf.get()
                TTop("pool", ct.ap, cgx.ap, cwb.ap[:, fc, 1:2].to_broadcast([128, 512]), ALU.mult,
                     r=[cgx.b, cwb.b], w=[ct.b])
                xv = cgx.ap.rearrange("p (r l) -> p r l", r=8)
                cv = ct.ap.rearrange("p (r l) -> p r l", r=8)
                STT("dve", cv[:, :, 1:64], xv[:, :, 0:63], cwb.ap[:, fc, 0:1], cv[:, :, 1:64], ALU.mult, ALU.add,
                    r=[cgx.b, cwb.b, ct.b], w=[ct.b])
                STT("dve", cv[:, :, 0:63], xv[:, :, 1:64], cwb.ap[:, fc, 2:3], cv[:, :, 0:63], ALU.mult, ALU.add,
                    r=[cgx.b, cwb.b, ct.b], w=[ct.b])
                TTop("dve", pT[fc][t].ap, pbs[1].ap, ct.ap, ALU.mult, r=[pbs[1].b, ct.b], w=[pT[fc][t].b])
        if b == 0:
            dbg("pT0_0", pT[0][0], [128, 512], BF16)
        ogp = Pool(AR, 2, [8, 512], BF16, "ogp")
        for t in range(4):
            ti = t + 1
            ogt = ogp.get()
            DMA(ogt.ap, og_scr[:, :, t * 512:(t + 1) * 512].rearrange("k p c -> p k c"),
                r=[og_buf[k][t] for k in range(NH)], w=[ogt.b])
            for fc in range(8):
                wga = wload(sc_in, sb_in, 56 + fc)
                wgb = wload(sc_in, sb_in, 64 + fc)
                wa = wload(sc_a, sb_a, fc)
                wb_ = wload(sc_b, sb_b, fc)
                pga = nextbank(); pgb = nextbank(); pya = nextbank(); pyb = nextbank()
                for k in range(8):
                    MM(pga.ap, wga.ap[:, k, :], uT[ti].ap[:, k, :], start=(k == 0), stop=(k == 7),
                       r=[wga.b, uT[ti].b], w=[pga.b])
                for k in range(8):
                    MM(pgb.ap, wgb.ap[:, k, :], uT[ti].ap[:, k, :], start=(k == 0), stop=(k == 7),
                       r=[wgb.b, uT[ti].b], w=[pgb.b])
                for k in range(8):
                    MM(pya.ap, wa.ap[:, k, :], ogt.ap[:, k, :], start=(k == 0), stop=(k == 7),
                       r=[wa.b, ogt.b], w=[pya.b])
                for k in range(8):
                    MM(pyb.ap, wb_.ap[:, k, :], pT[k][t].ap, start=(k == 0), stop=(k == 7),
                       r=[wb_.b, pT[k][t].b], w=[pyb.b])
                ta = tf.get(); tb_ = tf.get()
                ACT(ta.ap, pga.ap, AF.Tanh, r=[pga.b], w=[ta.b], scale=0.5)
                ACT(tb_.ap, pgb.ap, AF.Tanh, r=[pgb.b], w=[tb_.b], scale=0.5)
                STT("dve", ta.ap, ta.ap, 1.0, pya.ap, ALU.add, ALU.mult, r=[ta.b, pya.b], w=[ta.b])
                STT("dve", tb_.ap, tb_.ap, 1.0, pyb.ap, ALU.add, ALU.mult, r=[tb_.b, pyb.b], w=[tb_.b])
                TTop("pool", ymT[fc][t].ap, ta.ap, tb_.ap, ALU.add, r=[ta.b, tb_.b], w=[ymT[fc][t].b])
        if b == 0:
            dbg("ym0_0", ymT[0][0], [128, 512], BF16)
        S.barrier()
        AR.top = m_B
        if stage < 5:
            AR.top = mb
            return

        m_main = AR.top
        AR.top = mb
        hsqT = AR.alloc([32, 512], BF16, "hsqT")
        assert AR.top <= m_phase, (AR.top, m_phase)
        AR.top = m_main
        xa = AR.alloc([8, 512], F32, "xa")
        outT = AR.alloc([8, 512], F32, "outT")
        u2T = AR.alloc([8, 512], BF16, "u2T")
        tf = Pool(AR, 8, [512], F32, "tfC")
        w2p = Pool(AR, 2, [32, 128], BF16, "w2p")
        xs = Pool(AR, 6, [1024], F32, "xsC")

        sq_all = AR.alloc([8, 512], BF16, "sq_all")
        ln_dg = AR.alloc([8, 128], F32, "ln_dg")
        ln_A = AR.alloc([512], F32, "ln_A")
        ln_B = AR.alloc([512], F32, "ln_B")
        ln_mean = AR.alloc([4], F32, "ln_mean")
        ln_msq = AR.alloc([4], F32, "ln_msq")
        ln_var = AR.alloc([4], F32, "ln_var")
        ln_nmr = AR.alloc([4], F32, "ln_nmr")

        def layer_norm(src, dst_fns):
            for fc in range(8):
                ACT(sq_all.ap[:, fc, :], src.ap[:, fc, :], AF.Square, r=[src.b], w=[sq_all.b])
            pst = nextbank()
            for j in range(4):
                for fc in range(8):
                    MM(pst.ap[:, 2 * j:2 * j + 1], src.ap[:, fc, j * 128:(j + 1) * 128], C(ONES)[:, 0:1],
                       start=(fc == 0), stop=(fc == 7), r=[src.b, cst.b], w=[pst.b])
                for fc in range(8):
                    MM(pst.ap[:, 2 * j + 1:2 * j + 2], sq_all.ap[:, fc, j * 128:(j + 1) * 128], ONB[:, 0:1],
                       start=(fc == 0), stop=(fc == 7), r=[sq_all.b, cstb.b], w=[pst.b])
            pv = pst.ap[:, 0:8].rearrange("p (a b) -> p a b", a=4)
            ACT(ln_mean.ap, pv[:, :, 0], AF.Copy, r=[pst.b], w=[ln_mean.b], scale=1.0 / D)
            TTop("dve", ln_msq.ap, ln_mean.ap, ln_mean.ap, ALU.mult, r=[ln_mean.b], w=[ln_msq.b])
            STT("dve", ln_var.ap, pv[:, :, 1], 1.0 / D, ln_msq.ap, ALU.mult, ALU.subtract,
                r=[pst.b, ln_msq.b], w=[ln_var.b])
            ACT(ln_var.ap, ln_var.ap, AF.Identity, r=[ln_var.b, kc.b], w=[ln_var.b], bias=KC("lneps"), scale=1.0)
            TTop("pool", ln_var.ap, ln_var.ap, KC("mhalf").to_broadcast([128, 4]), ALU.pow,
                 r=[ln_var.b, kc.b], w=[ln_var.b])
            STT("dve", ln_nmr.ap, ln_mean.ap, -1.0, ln_var.ap, ALU.mult, ALU.mult,
                r=[ln_mean.b, ln_var.b], w=[ln_nmr.b])
            idb = C(IDENT).unsqueeze(1).to_broadcast([128, 4, 128])
            TTop("pool", ln_dg.ap[:, 0:4, :], idb, ln_var.ap.unsqueeze(2).to_broadcast([128, 4, 128]), ALU.mult,
                 r=[cst.b, ln_var.b], w=[ln_dg.b])
            TTop("pool", ln_dg.ap[:, 4:8, :], idb, ln_nmr.ap.unsqueeze(2).to_broadcast([128, 4, 128]), ALU.mult,
                 r=[cst.b, ln_nmr.b], w=[ln_dg.b])
            pA = nextbank()
            pB = nextbank()
            for j in range(4):
                MM(pA.ap[:, j * 128:(j + 1) * 128], C(ONES), ln_dg.ap[:, j, :], r=[cst.b, ln_dg.b], w=[pA.b])
            for j in range(4):
                MM(pB.ap[:, j * 128:(j + 1) * 128], C(ONES), ln_dg.ap[:, 4 + j, :], r=[cst.b, ln_dg.b], w=[pB.b])
            CP("act", ln_A.ap, pA.ap, r=[pA.b], w=[ln_A.b])
            CP("act", ln_B.ap, pB.ap, r=[pB.b], w=[ln_B.b])
            for fc in range(8):
                xc = tf.get()
                TTop("dve", xc.ap, src.ap[:, fc, :], ln_A.ap, ALU.mult, r=[src.b, ln_A.b], w=[xc.b])
                TTop("pool", xc.ap, xc.ap, ln_B.ap, ALU.add, r=[xc.b, ln_B.b], w=[xc.b])
                for (oap, sc_, bi_, rb, wbuf) in dst_fns:
                    ACT(oap(fc), xc.ap, AF.Identity, r=[xc.b] + rb, w=[wbuf], bias=bi_(fc), scale=sc_(fc))

        wo_t = [None] * 8
        for t in range(4):
            ti = t + 1
            blks = []
            for j in range(4):
                s = xs.get()
                DMA(s.ap, x_d[b, t * 512 + j * 128: t * 512 + (j + 1) * 128, :], w=[s.b])
                blks.append(s)
            for fc in range(8):
                pb = nextbank()
                for j in range(4):
                    MM(pb.ap[:, j * 128:(j + 1) * 128], blks[j].ap[:, fc * 128:(fc + 1) * 128], C(IDENT),
                       r=[blks[j].b, cst.b], w=[pb.b])
                ACT(xa.ap[:, fc, :], pb.ap, AF.Copy, r=[pb.b], w=[xa.b], scale=ALPHA)
            for fc in range(8):
                wo = wload(sc_o, sb_o, fc)
                pb = nextbank()
                for k in range(8):
                    MM(pb.ap, wo.ap[:, k, :], ymT[k][t].ap, start=(k == 0), stop=(k == 7), r=[wo.b, ymT[k][t].b], w=[pb.b])
                STT("dve", xa.ap[:, fc, :], pb.ap, hg1.ap[:, fc, b:b + 1], xa.ap[:, fc, :], ALU.mult, ALU.add,
                    r=[pb.b, hg1.b, xa.b], w=[xa.b])
            if b == 0 and t == 0:
                dbg("pre1", xa, [128, 8, 512])
            layer_norm(xa, [
                (lambda fc: outT.ap[:, fc, :], lambda fc: A1s.ap[:, fc:fc + 1], lambda fc: A1b.ap[:, fc, b:b + 1],
                 [A1s.b, A1b.b], outT.b),
                (lambda fc: u2T.ap[:, fc, :], lambda fc: U2s.ap[:, fc, b:b + 1], lambda fc: U2b.ap[:, fc, b:b + 1],
                 [U2s.b, U2b.b], u2T.b)])
            if b == 0 and t == 0:
                dbg("x1a", outT, [128, 8, 512]); dbg("u2T", u2T, [128, 8, 512], BF16)
            for ffc in range(32):
                w1 = wload(sc_f1, sb_f1, ffc)
                pb = nextbank()
                for k in range(8):
                    MM(pb.ap, w1.ap[:, k, :], u2T.ap[:, k, :], start=(k == 0), stop=(k == 7), r=[w1.b, u2T.b], w=[pb.b])
                rl = tf.get()
                ACT(rl.ap, pb.ap, AF.Relu, r=[pb.b, bf1.b], w=[rl.b], bias=bf1.ap[:, ffc:ffc + 1], scale=1.0)
                STT("dve", hsqT.ap[:, ffc, :], pb.ap, bf1.ap[:, ffc:ffc + 1], rl.ap, ALU.add, ALU.mult,
                    r=[pb.b, bf1.b, rl.b], w=[hsqT.b])
            for fc in range(8):
                w2 = w2p.get()
                DMA(w2.ap.rearrange("p a b -> p (a b)"), sc_f2[fc * 128:(fc + 1) * 128, :], r=[sb_f2[fc]], w=[w2.b])
                pb = nextbank()
                for k in range(32):
                    MM(pb.ap, w2.ap[:, k, :], hsqT.ap[:, k, :], start=(k == 0), stop=(k == 31), r=[w2.b, hsqT.b], w=[pb.b])
                STT("dve", outT.ap[:, fc, :], pb.ap, modT.ap[:, 40 + fc, b:b + 1], outT.ap[:, fc, :], ALU.mult, ALU.add,
                    r=[pb.b, modT.b, outT.b], w=[outT.b])
            if b == 0 and t == 0:
                dbg("pre2", outT, [128, 8, 512])
            layer_norm(outT, [
                (lambda fc: xa.ap[:, fc, :], lambda fc: ln2g.ap[:, fc:fc + 1], lambda fc: ln2b.ap[:, fc:fc + 1],
                 [ln2g.b, ln2b.b], xa.b)])
            if b == 0 and t == 0:
                dbg("xo", xa, [128, 8, 512])
            for j in range(4):
                os_ = xs.get()
                for half in range(2):
                    pb = nextbank()
                    for q in range(4):
                        fc = half * 4 + q
                        MM(pb.ap[:, q * 128:(q + 1) * 128], xa.ap[:, fc, j * 128:(j + 1) * 128], C(IDENT),
                           r=[xa.b, cst.b], w=[pb.b])
                    CP("act" if half == 0 else "dve", os_.ap[:, half * 512:(half + 1) * 512], pb.ap, r=[pb.b], w=[os_.b])
                DMA(out_d[b, t * 512 + j * 128: t * 512 + (j + 1) * 128, :], os_.ap, r=[os_.b])
        S.barrier()
        AR.top = mb

    for b in range(nbatch):
        batch_program(b)

    print("arena peak bytes/partition:", AR.peak, "ops:", {e: len(S.ops[e]) for e in ENGS})
    S.run()
    es.close()
    return nc, dbg_outs


_CACHE = {}


def _consts():
    i = np.arange(128)[:, None]
    j = np.arange(128)[None, :]
    c = np.zeros((NCONST, 128, 128), np.float32)
    c[0] = (i == j)
    c[1] = 1.0
    c[2] = (i <= j)
    c[3] = (i >= j)
    c[4] = np.where(j < i, 0.0, BIG)
    c[5] = np.where(j > i, 0.0, BIG)
    c[6] = np.where(j >= i, 0.0, -BIG)
    c[7] = np.where(j <= i, 0.0, -BIG)
    bd = ((i // 64) == (j // 64)).astype(np.float32)
    c[8] = bd
    c[9] = 1.0 - bd
    return np.ascontiguousarray(c.transpose(1, 0, 2).reshape(128, NCONST * 128))


def _pf(v, n):
    return np.ascontiguousarray(np.asarray(v, np.float32).reshape(n, 128).T)


def prepare_inputs(inputs, core):
    f = lambda k: np.ascontiguousarray(np.asarray(inputs[k], np.float32))
    b0 = 2 * core
    cc = np.stack([f("c")[b0], f("c")[b0 + 1], f("c_ctx")], axis=0)
    cT = np.ascontiguousarray(cc.reshape(3, 8, 128).transpose(2, 1, 0).reshape(128, 24))
    cq = f("conv_qkv")[0]
    cwq = np.ascontiguousarray(cq.reshape(3, 24, 128).transpose(2, 1, 0).reshape(128, 72))
    cb = f("conv_b")[0]
    cwb = np.ascontiguousarray(cb.reshape(3, 8, 128).transpose(2, 1, 0).reshape(128, 24))
    m = {
        "x": np.ascontiguousarray(f("x")[b0:b0 + 2]),
        "ctx": np.ascontiguousarray(f("ctx")[b0:b0 + 2]),
        "cT": cT,
        "w_ada": f("w_ada")[0],
        "b_adaT": _pf(f("b_ada")[0], 48),
        "w_in": f("w_in")[0],
        "cwq": cwq,
        "alog_bc": np.ascontiguousarray(np.broadcast_to(f("a_log")[0].reshape(1, 16), (128, 16))),
        "dtb_bc": np.ascontiguousarray(np.broadcast_to(f("dt_bias")[0].reshape(1, 16), (128, 16))),
        "onw_col": np.ascontiguousarray(f("o_norm_w")[0].reshape(128, 1)),
        "w_a_out": f("w_a_out")[0],
        "cwb": cwb,
        "w_b_out": f("w_b_out")[0],
        "w_o": f("w_o")[0],
        "ln1gT": _pf(f("ln1_g")[0], 8),
        "ln1bT": _pf(f("ln1_b")[0], 8),
        "w_ff1": f("w_ff1")[0],
        "b_ff1T": _pf(f("b_ff1")[0], 32),
        "w_ff2": f("w_ff2")[0],
        "b_ff2T": _pf(f("b_ff2")[0], 8),
        "ln2gT": _pf(f("ln2_g")[0], 8),
        "ln2bT": _pf(f("ln2_b")[0], 8),
        "consts": _consts(),
    }
    return m


def kernel(**inputs):
    if "nc" not in _CACHE:
        _CACHE["nc"] = build_program()[0]
    nc = _CACHE["nc"]
    n = 8
    shared = None
    in_maps = []
    for c in range(n):
        m = prepare_inputs(inputs, c)
        if shared is None:
            shared = m
        else:
            for k in m:
                if k not in ("x", "ctx", "cT"):
                    m[k] = shared[k]
        in_maps.append(m)
    res = run_bass_kernel_spmd(nc, in_maps, core_ids=list(range(n)))
    out = np.concatenate([np.asarray(r["out"], np.float32) for r in res.results], axis=0)
    return out
```

```python
import numpy as np
from contextlib import ExitStack
import concourse.bass as bass
import concourse.mybir as mybir
from concourse.bass_utils import run_bass_kernel_spmd

F32 = mybir.dt.float32
BF16 = mybir.dt.bfloat16
AF = mybir.ActivationFunctionType
ALU = mybir.AluOpType
AX = mybir.AxisListType

ENGS = ("pe", "act", "dve", "pool", "sp")
N_LANES = 24

D = 1024
SEQ = 2048
CTXL = 256
NTOK = SEQ + CTXL
NBLK = NTOK // 128
NH = 8
P_TOTAL = 9248
ALPHA = 2.0 ** 0.25
LN_EPS = 1e-5
RMS_EPS = 1e-6
L2_EPS = 1e-6
TILES = [(0, 256), (256, 512), (768, 512), (1280, 512), (1792, 512)]
ORDER = [list(range(18)), [1, 0] + list(range(17, 1, -1))]
NCONST = 10
BIG = 30000.0


class Buf:
    __slots__ = ("name", "w", "r", "excl")

    def __init__(self, name, excl=False):
        self.name = name
        self.w = None
        self.r = []
        self.excl = excl


class Op:
    __slots__ = ("eng", "fn", "deps", "idx", "sig", "tok", "is_dma")

    def __init__(self, eng, fn):
        self.eng = eng
        self.fn = fn
        self.deps = []
        self.sig = False
        self.tok = None
        self.is_dma = False


class Sched:
    def __init__(self, nc):
        self.nc = nc
        self.ops = {e: [] for e in ENGS}
        self.lane_cnt = [0] * N_LANES
        self.lane_last = [None] * N_LANES
        self.next_lane = 0
        self.next_lane2 = 0

    def _track(self, op, reads, writes):
        deps = []
        rd, wr = [], []
        for b in reads:
            (wr if b.excl else rd).append(b)
        wr.extend(writes)
        for b in rd:
            if b.w is not None:
                deps.append(b.w)
        for b in wr:
            if b.w is not None:
                deps.append(b.w)
            deps.extend(b.r)
        for b in rd:
            b.r.append(op)
        for b in wr:
            b.w = op
            b.r = []
        seen = set()
        for d in deps:
            if d is op or id(d) in seen:
                continue
            seen.add(id(d))
            if d.eng == "pe" and op.eng == "pe" and not d.is_dma and not op.is_dma:
                continue
            op.deps.append(d)

    def op(self, eng, fn, reads=(), writes=()):
        o = Op(eng, fn)
        self._track(o, reads, writes)
        self.ops[eng].append(o)
        return o

    def dma(self, fn, reads=(), writes=(), eng="sp"):
        o = Op(eng, fn)
        o.is_dma = True
        lane = self.next_lane
        self.next_lane = (self.next_lane + 1) % N_LANES
        prev = self.lane_last[lane]
        self._track(o, reads, writes)
        if prev is not None and all(d is not prev for d in o.deps):
            o.deps.append(prev)
        self.lane_cnt[lane] += 16
        o.tok = ("lane", lane, self.lane_cnt[lane])
        self.lane_last[lane] = o
        self.ops[eng].append(o)
        return o

    def barrier(self):
        lasts = []
        for e in ENGS:
            for o in reversed(self.ops[e]):
                if not o.is_dma:
                    lasts.append(o)
                    break
        lasts.extend(o for o in self.lane_last if o is not None)
        for e in ENGS:
            o = Op(e, lambda eng: eng.nop())
            o.deps = [d for d in lasts if not (d.eng == e and not d.is_dma)]
            self.ops[e].append(o)

    def run(self):
        nc = self.nc
        for e in ENGS:
            for o in self.ops[e]:
                for d in o.deps:
                    if not d.is_dma:
                        d.sig = True
        for e in ENGS:
            c = 0
            for o in self.ops[e]:
                if o.is_dma:
                    continue
                if o.sig:
                    c += 1
                    o.tok = ("eng", e, c)
        with ExitStack() as es:
            esem = {e: es.enter_context(nc.semaphore("s_" + e)) for e in ENGS}
            lsem = [es.enter_context(nc.semaphore("l_%d" % i)) for i in range(N_LANES)]
            block = es.enter_context(nc.Block())

            def body(ename, final=False):
                def _f(eng):
                    waited = {}
                    for o in self.ops[ename]:
                        need = {}
                        for d in o.deps:
                            k = (d.tok[0], d.tok[1])
                            if d.tok[2] > need.get(k, 0):
                                need[k] = d.tok[2]
                        for k, v in need.items():
                            if waited.get(k, 0) >= v:
                                continue
                            eng.wait_ge(esem[k[1]] if k[0] == "eng" else lsem[k[1]], v)
                            waited[k] = v
                        ins = o.fn(eng)
                        if o.is_dma:
                            ins.then_inc(lsem[o.tok[1]], 16)
                        elif o.sig:
                            ins.then_inc(esem[ename], 1)
                    if final:
                        for lane in range(N_LANES):
                            if self.lane_cnt[lane] > 0:
                                eng.wait_ge(lsem[lane], self.lane_cnt[lane])
                return _f

            block.tensor(body("pe"))
            block.scalar(body("act"))
            block.vector(body("dve"))
            block.gpsimd(body("pool"))
            block.sync(body("sp", final=True))


class T:
    __slots__ = ("ap", "b")

    def __init__(self, ap, b):
        self.ap = ap
        self.b = b


class Arena:
    def __init__(self, tens, nf32):
        self.t = tens
        self.cap = nf32 * 4
        self.top = 0
        self.peak = 0

    def alloc(self, free_shape, dt, name, at=None):
        n = int(np.prod(free_shape))
        esz = 4 if dt == F32 else 2
        nbytes = (n * esz + 3) // 4 * 4
        if at is None:
            off = (self.top + 63) // 64 * 64
            self.top = off + nbytes
            self.peak = max(self.peak, self.top)
            assert self.top <= self.cap, ("SBUF arena overflow", name, self.top, self.cap)
        else:
            off = at
        ap = self.t[:, off // 4: off // 4 + nbytes // 4]
        if dt == BF16:
            ap = ap.bitcast(BF16)
        ap = ap[:, 0:n]
        if len(free_shape) == 2:
            ap = ap.rearrange("p (a b) -> p a b", a=free_shape[0])
        elif len(free_shape) == 3:
            ap = ap.rearrange("p (a b c) -> p a b c", a=free_shape[0], b=free_shape[1])
        return T(ap, Buf(name))


class Pool:
    def __init__(self, arena, n, free_shape, dt, name):
        self.items = [arena.alloc(free_shape, dt, "%s%d" % (name, i)) for i in range(n)]
        self.i = 0

    def get(self):
        t = self.items[self.i]
        self.i = (self.i + 1) % len(self.items)
        return t


def build_program(debug=None, stage=99, nbatch=2):
    nc = bass.Bass("TRN2", target_bir_lowering=False)
    S = Sched(nc)
    dbg_outs = {}

    def din(name, shape, dt=F32):
        return nc.dram_tensor(name, list(shape), dt, kind="ExternalInput").ap()

    x_d = din("x", [2, SEQ, D])
    ctx_d = din("ctx", [2, CTXL, D])
    cT_d = din("cT", [128, 24])
    wada_d = din("w_ada", [D, 6 * D])
    badaT_d = din("b_adaT", [128, 48])
    win_d = din("w_in", [D, P_TOTAL])
    cwq_d = din("cwq", [128, 72])
    alog_d = din("alog_bc", [128, 16])
    dtb_d = din("dtb_bc", [128, 16])
    onw_d = din("onw_col", [128, 1])
    wa_d = din("w_a_out", [D, D])
    cwb_d = din("cwb", [128, 24])
    wb_d = din("w_b_out", [D, D])
    wo_d = din("w_o", [D, D])
    ln1g_d = din("ln1gT", [128, 8])
    ln1b_d = din("ln1bT", [128, 8])
    wf1_d = din("w_ff1", [D, 4 * D])
    bf1_d = din("b_ff1T", [128, 32])
    wf2_d = din("w_ff2", [4 * D, D])
    bf2_d = din("b_ff2T", [128, 8])
    ln2g_d = din("ln2gT", [128, 8])
    ln2b_d = din("ln2bT", [128, 8])
    const_d = din("consts", [128, NCONST * 128])
    out_d = nc.dram_tensor("out", [2, SEQ, D], F32, kind="ExternalOutput").ap()

    def dscr(name, nchunk, K):
        t = nc.dram_tensor(name, [nchunk * 128, K * 128], BF16, kind="Internal").ap()
        return t, [Buf("%s_%d" % (name, i)) for i in range(nchunk)]

    sc_in, sb_in = dscr("sc_in", 72, 8)
    sc_bg_t = nc.dram_tensor("sc_bg", [128, 256], BF16, kind="Internal").ap()
    sb_bg = Buf("sc_bg")
    sc_a, sb_a = dscr("sc_a", 8, 8)
    sc_b, sb_b = dscr("sc_b", 8, 8)
    sc_o, sb_o = dscr("sc_o", 8, 8)
    sc_f1, sb_f1 = dscr("sc_f1", 32, 8)
    sc_f2, sb_f2 = dscr("sc_f2", 8, 32)

    def dbg(name, t, shape, dt=F32):
        if not debug or name not in debug:
            return
        o = nc.dram_tensor("dbg_" + name, list(shape), dt, kind="ExternalOutput").ap()
        dbg_outs[name] = o
        ap = t.ap
        S.dma(lambda e: e.dma_start(out=o, in_=ap), reads=[t.b])

    es = ExitStack()
    NF32 = 53200
    arena_t = es.enter_context(nc.sbuf_tensor("arena", [128, NF32], F32))
    AR = Arena(arena_t, NF32)
    banks = []
    for i in range(8):
        pt = es.enter_context(nc.psum_tensor("ps%d" % i, [128, 512], F32))
        banks.append(T(pt[:, :], Buf("ps%d" % i, excl=True)))
    rr = [0]
    rr_lo = [0]

    def nextbank():
        if rr[0] < rr_lo[0]:
            rr[0] = rr_lo[0]
        b = banks[rr[0]]
        rr[0] += 1
        if rr[0] >= 8:
            rr[0] = rr_lo[0]
        return b

    def MM(out, lhsT, rhs, start=True, stop=True, r=(), w=()):
        S.op("pe", lambda e: e.matmul(out, lhsT=lhsT, rhs=rhs, start=start, stop=stop), r, w)

    def ACT(out, in_, func, r=(), w=(), bias=None, scale=1.0):
        if bias is None:
            S.op("act", lambda e: e.activation(out=out, in_=in_, func=func, scale=scale), r, w)
        else:
            S.op("act", lambda e: e.activation(out=out, in_=in_, func=func, bias=bias, scale=scale), r, w)

    def TTop(eng, out, in0, in1, op, r=(), w=()):
        S.op(eng, lambda e: e.tensor_tensor(out=out, in0=in0, in1=in1, op=op), r, w)

    def STT(eng, out, in0, scalar, in1, op0, op1, r=(), w=()):
        S.op(eng, lambda e: e.scalar_tensor_tensor(out=out, in0=in0, scalar=scalar, in1=in1, op0=op0, op1=op1), r, w)

    def TS(eng, out, in0, s1, s2, op0, op1=None, r=(), w=()):
        if op1 is None:
            S.op(eng, lambda e: e.tensor_scalar(out=out, in0=in0, scalar1=s1, scalar2=None, op0=op0), r, w)
        else:
            S.op(eng, lambda e: e.tensor_scalar(out=out, in0=in0, scalar1=s1, scalar2=s2, op0=op0, op1=op1), r, w)

    def CP(eng, out, in_, r=(), w=()):
        if eng == "act":
            S.op("act", lambda e: e.copy(out=out, in_=in_), r, w)
        else:
            S.op(eng, lambda e: e.tensor_copy(out=out, in_=in_), r, w)

    def DMA(out, in_, r=(), w=()):
        S.dma(lambda e: e.dma_start(out=out, in_=in_), r, w)

    cst = AR.alloc([NCONST, 128], F32, "cst")
    DMA(cst.ap.rearrange("p a b -> p (a b)"), const_d, w=[cst.b])
    IDENT, ONES, CUM0, CUM1, MPOS0, MPOS1, MTN0, MTN1, BDM, NBDM = range(10)

    def C(i):
        return cst.ap[:, i, :]

    cstb = AR.alloc([4, 128], BF16, "cstb")
    for j, i in enumerate([IDENT, ONES, BDM, NBDM]):
        CP("dve", cstb.ap[:, j, :], C(i), r=[cst.b], w=[cstb.b])
    IDB, ONB, BDB, NBDB = [cstb.ap[:, j, :] for j in range(4)]

    kc = AR.alloc([8], F32, "kcols")
    KV = {"eps4": 4 * L2_EPS, "eps4q": 4 * L2_EPS * 128.0, "rmseps": RMS_EPS, "lneps": LN_EPS, "mhalf": -0.5,
          "one": 1.0, "zero": 0.0}
    KI = {}
    for j, (k, v) in enumerate(KV.items()):
        KI[k] = j
        S.op("pool", lambda e, j=j, v=v: e.memset(kc.ap[:, j:j + 1], v), (), [kc.b])

    def KC(name):
        j = KI[name]
        return kc.ap[:, j:j + 1]

    def small(name, shape):
        return AR.alloc(shape, F32, name)

    cT = small("cT", [8, 3])
    badaT = small("badaT", [48])
    cwq = small("cwq", [24, 3])
    cwb = small("cwb", [8, 3])
    alog = small("alog", [16])
    dtb = small("dtb", [16])
    onwc = small("onwc", [1])
    ln1g = small("ln1g", [8]); ln1b = small("ln1b", [8]); ln2g = small("ln2g", [8]); ln2b = small("ln2b", [8])
    bf1 = small("bf1", [32]); bf2 = small("bf2", [8])
    DMA(cT.ap.rearrange("p a b -> p (a b)"), cT_d, w=[cT.b])
    DMA(badaT.ap, badaT_d, w=[badaT.b])
    DMA(cwq.ap.rearrange("p a b -> p (a b)"), cwq_d, w=[cwq.b])
    DMA(cwb.ap.rearrange("p a b -> p (a b)"), cwb_d, w=[cwb.b])
    DMA(alog.ap, alog_d, w=[alog.b])
    DMA(dtb.ap, dtb_d, w=[dtb.b])
    DMA(onwc.ap, onw_d, w=[onwc.b])
    for t_, d_ in ((ln1g, ln1g_d), (ln1b, ln1b_d), (ln2g, ln2g_d), (ln2b, ln2b_d), (bf1, bf1_d), (bf2, bf2_d)):
        DMA(t_.ap, d_, w=[t_.b])
    ACT(onwc.ap, onwc.ap, AF.Copy, r=[onwc.b], w=[onwc.b], scale=0.5)
    nA = small("nA", [16])
    ACT(nA.ap, alog.ap, AF.Exp, r=[alog.b], w=[nA.b])
    TS("dve", nA.ap, nA.ap, -1.0, None, ALU.mult, r=[nA.b], w=[nA.b])
    modT = small("modT", [48, 3])
    s1c = small("s1c", [8, 3])
    hg1 = small("hg1", [8, 2])
    A1s = small("A1s", [8])
    A1b = small("A1b", [8, 2])
    U2s = small("U2s", [8, 2])
    U2b = small("U2b", [8, 2])
    wtp = Pool(AR, 6, [8, 128], BF16, "wt")
    mark_persist = AR.top

    cast_head, cast_rest = [], []
    cast_jobs = cast_head

    def mk_job(W, K0, K, c0, ncol, dst_ap, dst_buf):
        src = W.rearrange("(k p) c -> p k c", p=128)[:, K0:K0 + K, c0:c0 + ncol]
        dst = dst_ap.rearrange("p (k c) -> p k c", k=K)
        cast_jobs.append((src, dst, dst_buf))

    def in_col(c):
        return c * 128 if c < 24 else 3104 + (c - 24) * 128

    mk_job(win_d, 0, 8, 3072, 32, sc_bg_t, sb_bg)
    for h in range(NH):
        for c in (h, 8 + h, 16 + h, 24 + h):
            mk_job(win_d, 0, 8, in_col(c), 128, sc_in[c * 128:(c + 1) * 128, :], sb_in[c])
    cast_jobs = cast_rest
    for c in range(32, 72):
        mk_job(win_d, 0, 8, in_col(c), 128, sc_in[c * 128:(c + 1) * 128, :], sb_in[c])
    for W, sc, sb in ((wa_d, sc_a, sb_a), (wb_d, sc_b, sb_b), (wo_d, sc_o, sb_o)):
        for c in range(8):
            mk_job(W, 0, 8, c * 128, 128, sc[c * 128:(c + 1) * 128, :], sb[c])
    for c in range(32):
        mk_job(wf1_d, 0, 8, c * 128, 128, sc_f1[c * 128:(c + 1) * 128, :], sb_f1[c])
    for c in range(8):
        for kq in range(4):
            mk_job(wf2_d, kq * 8, 8, c * 128, 128, sc_f2[c * 128:(c + 1) * 128, kq * 1024:(kq + 1) * 1024], sb_f2[c])
    def emit_casts(lst, n):
        for _ in range(n):
            if not lst:
                return
            src, dst, dbuf = lst.pop(0)
            S.dma(lambda e, src=src, dst=dst: e.dma_start(out=dst, in_=src), reads=[], writes=[dbuf], eng="pool")

    def mod_phase():
        stg = Pool(AR, 3, [8, 128], F32, "mstg")
        sil = AR.alloc([8, 3], F32, "silc")
        th = AR.alloc([8, 3], F32, "silt")
        ACT(th.ap, cT.ap, AF.Tanh, r=[cT.b], w=[th.b], scale=0.5)
        STT("dve", sil.ap, th.ap, 1.0, cT.ap, ALU.add, ALU.mult, r=[th.b, cT.b], w=[sil.b])
        TS("dve", sil.ap, sil.ap, 0.5, None, ALU.mult, r=[sil.b], w=[sil.b])
        pb = banks[0]
        for fc in range(48):
            s = stg.get()
            DMA(s.ap, wada_d.rearrange("(k p) c -> p k c", p=128)[:, :, fc * 128:(fc + 1) * 128], w=[s.b])
            for k in range(8):
                MM(pb.ap[:, fc * 3:fc * 3 + 3], s.ap[:, k, :], sil.ap[:, k, :], start=(k == 0), stop=(k == 7),
                   r=[s.b, sil.b], w=[pb.b])
        TTop("dve", modT.ap, pb.ap[:, 0:144].rearrange("p (a b) -> p a b", a=48),
             badaT.ap.unsqueeze(2).to_broadcast([128, 48, 3]), ALU.add, r=[pb.b, badaT.b], w=[modT.b])
        m = modT.ap
        TS("dve", s1c.ap, m[:, 8:16, :], 1.0, None, ALU.add, r=[modT.b], w=[s1c.b])
        TS("dve", hg1.ap, m[:, 16:24, 0:2], 0.5, None, ALU.mult, r=[modT.b], w=[hg1.b])
        TS("dve", A1s.ap, ln1g.ap, ALPHA, None, ALU.mult, r=[ln1g.b], w=[A1s.b])
        tmp = AR.alloc([8, 2], F32, "mtmp")
        TTop("dve", tmp.ap, m[:, 40:48, 0:2], bf2.ap.unsqueeze(2).to_broadcast([128, 8, 2]), ALU.mult,
             r=[modT.b, bf2.b], w=[tmp.b])
        STT("dve", A1b.ap, ln1b.ap.unsqueeze(2).to_broadcast([128, 8, 2]), ALPHA, tmp.ap, ALU.mult, ALU.add,
            r=[ln1b.b, tmp.b], w=[A1b.b])
        tmp2 = AR.alloc([8, 2], F32, "mtmp2")
        TS("dve", tmp2.ap, m[:, 32:40, 0:2], 1.0, None, ALU.add, r=[modT.b], w=[tmp2.b])
        TTop("dve", U2s.ap, tmp2.ap, ln1g.ap.unsqueeze(2).to_broadcast([128, 8, 2]), ALU.mult,
             r=[tmp2.b, ln1g.b], w=[U2s.b])
        TTop("dve", U2b.ap, tmp2.ap, ln1b.ap.unsqueeze(2).to_broadcast([128, 8, 2]), ALU.mult,
             r=[tmp2.b, ln1b.b], w=[U2b.b])
        TTop("dve", U2b.ap, U2b.ap, m[:, 24:32, 0:2], ALU.add, r=[U2b.b, modT.b], w=[U2b.b])

    m0 = AR.top
    emit_casts(cast_head, 1 + 8)
    mod_phase()
    dbg("modT", modT, [128, 144])
    S.barrier()
    AR.top = m0

    def batch_program(b):
        mb = AR.top
        uT = [AR.alloc([8, ln], BF16, "uT%d" % ti) for ti, (st, ln) in enumerate(TILES)]
        og_scr = nc.dram_tensor("og_scr%d" % b, [NH, 128, SEQ], BF16, kind="Internal").ap()
        og_buf = [[Buf("ogs%d_%d_%d" % (b, h, t)) for t in range(4)] for h in range(NH)]
        m_phase = AR.top

        def utile(tok):
            for ti, (st, ln) in enumerate(TILES):
                if st <= tok < st + ln:
                    return ti, tok - st
            raise ValueError

        xs = Pool(AR, 6, [1024], F32, "xs")
        for ti, (st, ln) in enumerate(TILES):
            nb = ln // 128
            blks = []
            for j in range(nb):
                s = xs.get()
                if ti == 0:
                    src = ctx_d[b, j * 128:(j + 1) * 128, :]
                else:
                    src = x_d[b, st - CTXL + j * 128: st - CTXL + (j + 1) * 128, :]
                DMA(s.ap, src, w=[s.b])
                blks.append(s)
            jm = 2 if ti == 0 else b
            for fc in range(8):
                pb = nextbank()
                for j in range(nb):
                    MM(pb.ap[:, j * 128:(j + 1) * 128], blks[j].ap[:, fc * 128:(fc + 1) * 128], C(IDENT),
                       r=[blks[j].b, cst.b], w=[pb.b])
                ACT(uT[ti].ap[:, fc, :], pb.ap[:, 0:ln], AF.Identity, r=[pb.b, s1c.b, modT.b], w=[uT[ti].b],
                    bias=modT.ap[:, fc, jm:jm + 1], scale=s1c.ap[:, fc, jm:jm + 1])
        if b == 0:
            dbg("uT1", uT[1], [128, 8, 512], BF16)
        S.barrier()
        AR.top = m_phase
        if stage < 1:
            AR.top = mb
            return

        def pcol(name):
            return AR.alloc([NBLK, 16], F32, name)

        bt = pcol("bt"); hbt = pcol("hbt"); gg = pcol("gg"); Gc = pcol("Gc")
        n2eG = pcol("n2eG"); kdec = pcol("kdec"); eGl = pcol("eGl")
        m_A = AR.top
        wbg = AR.alloc([8, 32], BF16, "wbg")
        DMA(wbg.ap.rearrange("p a b -> p (a b)"), sc_bg_t, r=[sb_bg], w=[wbg.b])
        tb = AR.alloc([NBLK, 16], F32, "pc_tb")
        xg = AR.alloc([NBLK, 16], F32, "pc_xg")
        for half in range(2):
            pb = nextbank()
            for j in range(9):
                blk = half * 9 + j
                ti, off = utile(blk * 128)
                for k in range(8):
                    MM(pb.ap[:, j * 32:(j + 1) * 32], uT[ti].ap[:, k, off:off + 128], wbg.ap[:, k, :],
                       start=(k == 0), stop=(k == 7), r=[uT[ti].b, wbg.b], w=[pb.b])
            pv = pb.ap[:, 0:288].rearrange("p (a b) -> p a b", a=9)
            sl = slice(half * 9, half * 9 + 9)
            ACT(tb.ap[:, sl, :], pv[:, :, 0:16], AF.Tanh, r=[pb.b], w=[tb.b], scale=0.5)
            TTop("dve", xg.ap[:, sl, :], pv[:, :, 16:32], dtb.ap.unsqueeze(1).to_broadcast([128, 9, 16]), ALU.add,
                 r=[pb.b, dtb.b], w=[xg.b])
        TS("dve", bt.ap, tb.ap, 0.5, 0.5, ALU.mult, ALU.add, r=[tb.b], w=[bt.b])
        TS("dve", hbt.ap, tb.ap, 0.25, 0.25, ALU.mult, ALU.add, r=[tb.b], w=[hbt.b])
        TS("dve", xg.ap, xg.ap, 30.0, None, ALU.min, r=[xg.b], w=[xg.b])
        ACT(xg.ap, xg.ap, AF.Exp, r=[xg.b], w=[xg.b])
        ACT(xg.ap, xg.ap, AF.Ln, r=[xg.b, kc.b], w=[xg.b], bias=KC("one"))
        TTop("dve", gg.ap, xg.ap, nA.ap.unsqueeze(1).to_broadcast([128, NBLK, 16]), ALU.mult, r=[xg.b, nA.b], w=[gg.b])
        pb = nextbank()
        pv = pb.ap[:, 0:288].rearrange("p (a b) -> p a b", a=NBLK)
        for d in range(2):
            MM(pv[:, :, d * 8:(d + 1) * 8], C(CUM0 + d), gg.ap[:, :, d * 8:(d + 1) * 8], r=[cst.b, gg.b], w=[pb.b])
        CP("act", Gc.ap, pv, r=[pb.b], w=[Gc.b])
        pb2 = nextbank()
        pv2 = pb2.ap[:, 0:288].rearrange("p (a b) -> p a b", a=NBLK)
        MM(pb2.ap[:, 0:288], C(ONES), gg.ap.rearrange("p a b -> p (a b)"), r=[cst.b, gg.b], w=[pb2.b])
        ACT(eGl.ap, pv2, AF.Exp, r=[pb2.b], w=[eGl.b])
        TTop("dve", kdec.ap, pv2, Gc.ap, ALU.subtract, r=[pb2.b, Gc.b], w=[kdec.b])
        ACT(kdec.ap, kdec.ap, AF.Exp, r=[kdec.b], w=[kdec.b])
        ACT(n2eG.ap, Gc.ap, AF.Exp, r=[Gc.b], w=[n2eG.b])
        TS("dve", n2eG.ap, n2eG.ap, -2.0, None, ALU.mult, r=[n2eG.b], w=[n2eG.b])
        if b == 0:
            dbg("bt", bt, [128, NBLK, 16]); dbg("gg", gg, [128, NBLK, 16]); dbg("Gc", Gc, [128, NBLK, 16])
            dbg("kdec", kdec, [128, NBLK, 16]); dbg("eGl", eGl, [128, NBLK, 16])
        AR.top = m_A

        qTs = [AR.alloc([NTOK], BF16, "qT%d" % i) for i in range(2)]
        kTs = [AR.alloc([NTOK], BF16, "kT%d" % i) for i in range(2)]
        k_tms = [AR.alloc([NBLK, 128], BF16, "k_tm%d" % i) for i in range(2)]
        v_tms = [AR.alloc([NBLK, 128], BF16, "v_tm%d" % i) for i in range(2)]
        zs2Ts = [AR.alloc([SEQ], BF16, "zs2T%d" % i) for i in range(2)]
        WSn = ("Q", "AT", "Ub", "nWT", "Bb")
        WS = [{n_: AR.alloc([12, 128], BF16, "ws%d_%s" % (p_, n_)) for n_ in WSn} for p_ in range(2)]
        o_acc = AR.alloc([16, 128], F32, "o_acc")
        tf = Pool(AR, 3, [512], F32, "tf")
        tb16 = Pool(AR, 3, [512], BF16, "tb")
        chain = [[AR.alloc([4, 128], BF16, "ch%d_%d" % (g_, i)) for i in range(8)] for g_ in range(3)]
        S32 = [[AR.alloc([128], F32, "S32_%d_%d" % (d, i)) for i in range(2)] for d in range(2)]
        Sb = [[AR.alloc([128], BF16, "Sb_%d_%d" % (d, i)) for i in range(2)] for d in range(2)]
        e_ct = AR.alloc([512], F32, "e_ct")
        e_s2 = AR.alloc([512], F32, "e_s2")
        e_dg = AR.alloc([512], F32, "e_dg")
        e_sq = AR.alloc([512], BF16, "e_sq")
        e_vt = e_sq
        e_rs = AR.alloc([8], F32, "e_rs")
        on_all = AR.alloc([16, 128], BF16, "on_all")
        dec = [(AR.alloc([512], F32, "nD%d" % g_), AR.alloc([512], F32, "DT%d" % g_)) for g_ in range(3)]
        ssq = AR.alloc([16], F32, "ssq")
        rstd = AR.alloc([16], F32, "rstd")

        print("phase A top:", AR.top, "cap", AR.cap)

        def early_gen(h):
            qT, kT, k_tm, v_tm, zs2T = qTs[h % 2], kTs[h % 2], k_tms[h % 2], v_tms[h % 2], zs2Ts[h % 2]
            wts = []
            for c in (h, 8 + h, 16 + h, 24 + h):
                wt = wtp.get()
                DMA(wt.ap.rearrange("p a b -> p (a b)"), sc_in[c * 128:(c + 1) * 128, :], r=[sb_in[c]], w=[wt.b])
                wts.append(wt)
            yield
            for ti, (st, ln) in enumerate(TILES):
                R = 1 if ti == 0 else ln // 64
                L = ln // R
                nb = ln // 128
                for wi, name in enumerate(("q", "k", "v")):
                    pb = nextbank()
                    for k in range(8):
                        MM(pb.ap[:, 0:ln], wts[wi].ap[:, k, :], uT[ti].ap[:, k, :], start=(k == 0), stop=(k == 7),
                           r=[wts[wi].b, uT[ti].b], w=[pb.b])
                    fcw = wi * 8 + h
                    ct = e_ct
                    ACT(ct.ap[:, 0:ln], pb.ap[:, 0:ln], AF.Copy, r=[pb.b, cwq.b], w=[ct.b], scale=cwq.ap[:, fcw, 1:2])
                    pv = pb.ap[:, 0:ln].rearrange("p (r l) -> p r l", r=R)
                    cv = ct.ap[:, 0:ln].rearrange("p (r l) -> p r l", r=R)
                    STT("dve", cv[:, :, 1:L], pv[:, :, 0:L - 1], cwq.ap[:, fcw, 0:1], cv[:, :, 1:L], ALU.mult, ALU.add,
                        r=[pb.b, cwq.b, ct.b], w=[ct.b])
                    STT("dve", cv[:, :, 0:L - 1], pv[:, :, 1:L], cwq.ap[:, fcw, 2:3], cv[:, :, 0:L - 1], ALU.mult, ALU.add,
                        r=[pb.b, cwq.b, ct.b], w=[ct.b])
                    s2 = e_s2
                    ACT(s2.ap[:, 0:ln], ct.ap[:, 0:ln], AF.Tanh, r=[ct.b], w=[s2.b], scale=0.5)
                    if name == "v":
                        STT("dve", e_vt.ap[:, 0:ln], s2.ap[:, 0:ln], 1.0, ct.ap[:, 0:ln], ALU.add, ALU.mult,
                            r=[s2.b, ct.b], w=[e_vt.b])
                        yield
                        pbv = nextbank()
                        for j in range(nb):
                            MM(pbv.ap[:, j * 128:(j + 1) * 128], e_vt.ap[:, j * 128:(j + 1) * 128], IDB,
                               r=[e_vt.b, cstb.b], w=[pbv.b])
                        CP("act", v_tm.ap[:, st // 128: st // 128 + nb, :].rearrange("p a b -> p (a b)"),
                           pbv.ap[:, 0:ln], r=[pbv.b], w=[v_tm.b])
                        yield
                        continue
                    STT("dve", s2.ap[:, 0:ln], s2.ap[:, 0:ln], 1.0, ct.ap[:, 0:ln], ALU.add, ALU.mult,
                        r=[s2.b, ct.b], w=[s2.b])
                    TTop("pool", e_sq.ap[:, 0:ln], s2.ap[:, 0:ln], s2.ap[:, 0:ln], ALU.mult, r=[s2.b], w=[e_sq.b])
                    yield
                    pb2 = nextbank()
                    for j in range(nb):
                        MM(pb2.ap[:, j:j + 1], e_sq.ap[:, j * 128:(j + 1) * 128], ONB[:, 0:1], r=[e_sq.b, cstb.b], w=[pb2.b])
                    rs = e_rs
                    if name == "q":
                        ACT(rs.ap[:, 0:nb], pb2.ap[:, 0:nb], AF.Identity, r=[pb2.b, kc.b], w=[rs.b],
                            bias=KC("eps4q"), scale=128.0)
                    else:
                        ACT(rs.ap[:, 0:nb], pb2.ap[:, 0:nb], AF.Identity, r=[pb2.b, kc.b], w=[rs.b],
                            bias=KC("eps4"), scale=1.0)
                    TTop("pool", rs.ap[:, 0:nb], rs.ap[:, 0:nb], KC("mhalf").to_broadcast([128, nb]), ALU.pow,
                         r=[rs.b, kc.b], w=[rs.b])
                    dg = e_dg
                    dgv = dg.ap[:, 0:ln].rearrange("p (a b) -> p a b", a=nb)
                    TTop("pool", dgv, C(IDENT).unsqueeze(1).to_broadcast([128, nb, 128]),
                         rs.ap[:, 0:nb].unsqueeze(2).to_broadcast([128, nb, 128]), ALU.mult,
                         r=[cst.b, rs.b], w=[dg.b])
                    yield
                    pb3 = nextbank()
                    for j in range(nb):
                        MM(pb3.ap[:, j * 128:(j + 1) * 128], C(ONES), dgv[:, j, :], r=[cst.b, dg.b], w=[pb3.b])
                    dst = qT if name == "q" else kT
                    TTop("dve", dst.ap[:, st:st + ln], s2.ap[:, 0:ln], pb3.ap[:, 0:ln], ALU.mult,
                         r=[s2.b, pb3.b], w=[dst.b])
                    yield
                    if name == "k":
                        pbt = nextbank()
                        for j in range(nb):
                            MM(pbt.ap[:, j * 128:(j + 1) * 128], kT.ap[:, st + j * 128: st + (j + 1) * 128], IDB,
                               r=[kT.b, cstb.b], w=[pbt.b])
                        CP("act", k_tm.ap[:, st // 128: st // 128 + nb, :].rearrange("p a b -> p (a b)"),
                           pbt.ap[:, 0:ln], r=[pbt.b], w=[k_tm.b])
                        yield
            for ti, (st, ln) in enumerate(TILES):
                if ti == 0:
                    continue
                pb = nextbank()
                for k in range(8):
                    MM(pb.ap[:, 0:ln], wts[3].ap[:, k, :], uT[ti].ap[:, k, :], start=(k == 0), stop=(k == 7),
                       r=[wts[3].b, uT[ti].b], w=[pb.b])
                ACT(e_ct.ap[:, 0:ln], pb.ap[:, 0:ln], AF.Tanh, r=[pb.b], w=[e_ct.b], scale=0.5)
                STT("dve", zs2T.ap[:, st - CTXL: st - CTXL + ln], e_ct.ap[:, 0:ln], 1.0, pb.ap[:, 0:ln],
                    ALU.add, ALU.mult, r=[e_ct.b, pb.b], w=[zs2T.b])
                yield
            if b == 0 and h == 0:
                dbg("qT", qT, [128, NTOK], BF16); dbg("kT", kT, [128, NTOK], BF16)
                dbg("v_tm", v_tm, [128, NBLK, 128], BF16); dbg("zs2T", zs2T, [128, SEQ], BF16)

        def adv(gen, n):
            if gen is None:
                return
            for _ in range(n):
                try:
                    next(gen)
                except StopIteration:
                    return

        def wave_units(w):
            fw = [(0, 6 * w + s_) for s_ in range(6)]
            bw = sorted([(1, 6 * w + s_) for s_ in range(6)], key=lambda t_: ORDER[1][t_[1]])
            return fw + bw

        def head_ctx(h):
            qT, kT, k_tm, v_tm, zs2T = qTs[h % 2], kTs[h % 2], k_tms[h % 2], v_tms[h % 2], zs2Ts[h % 2]

            def init():
                for d in range(2):
                    S.op("pool", lambda e, d=d: e.memset(S32[d][0].ap, 0.0), (), [S32[d][0].b])
                    S.op("pool", lambda e, d=d: e.memset(Sb[d][0].ap, 0.0), (), [Sb[d][0].b])

            def col(t_, blk, d):
                return t_.ap[:, blk, d * 8 + h: d * 8 + h + 1]

            def cols(t_, blk0, n, d):
                return t_.ap[:, blk0:blk0 + n, d * 8 + h]

            def runs(us):
                out = []
                for j, (d, s_) in enumerate(us):
                    blk = ORDER[d][s_]
                    if out and out[-1][2] == d and out[-1][3] + out[-1][1] == blk:
                        out[-1][1] += 1
                    else:
                        out.append([j, 1, d, blk])
                return out

            def wave_prep(w):
                units = wave_units(w)
                f2 = lambda t_: t_.ap.rearrange("p a b -> p (a b)")
                ws = WS[(3 * h + w) % 2]
                G = []
                for gq in range(3):
                    ch = chain[gq]
                    G.append({"us": units[gq * 4: gq * 4 + 4], "gq": gq, "Mt": ch[0], "R": [ch[1], ch[2]],
                              "P": [ch[3], ch[4]], "Y": [ch[5], ch[6]], "Moff": ch[7], "rp": 0, "y": 0})
                gcs = []
                for g in G:
                    gcum = tf.get()
                    gcv = gcum.ap.rearrange("p (a b) -> p a b", a=4)
                    for (j0, n, d, blk0) in runs(g["us"]):
                        TTop("pool", gcv[:, j0:j0 + n, :], C(CUM0 + d).unsqueeze(1).to_broadcast([128, n, 128]),
                             cols(gg, blk0, n, d).unsqueeze(2).to_broadcast([128, n, 128]), ALU.mult,
                             r=[cst.b, gg.b], w=[gcum.b])
                    gcs.append((gcum, gcv))
                yield
                for g, (gcum, gcv) in zip(G, gcs):
                    gq = g["gq"]
                    pg = nextbank()
                    for j in range(4):
                        MM(pg.ap[:, j * 128:(j + 1) * 128], C(ONES), gcv[:, j, :], r=[cst.b, gcum.b], w=[pg.b])
                    eGr = gcum
                    ACT(eGr.ap, pg.ap, AF.Exp, r=[pg.b], w=[eGr.b])
                    nD, DT = dec[gq]
                    pg3 = pg.ap.rearrange("p (a b) -> p a b", a=4)
                    nD3 = nD.ap.rearrange("p (a b) -> p a b", a=4)
                    DT3 = DT.ap.rearrange("p (a b) -> p a b", a=4)
                    for (j0, n, d, blk0) in runs(g["us"]):
                        gcb = cols(Gc, blk0, n, d).unsqueeze(2).to_broadcast([128, n, 128])
                        TTop("dve", nD3[:, j0:j0 + n, :], pg3[:, j0:j0 + n, :], gcb, ALU.subtract,
                             r=[pg.b, Gc.b], w=[nD.b])
                        TTop("pool", DT3[:, j0:j0 + n, :], nD3[:, j0:j0 + n, :],
                             C(MTN0 + d).unsqueeze(1).to_broadcast([128, n, 128]), ALU.add, r=[nD.b, cst.b], w=[DT.b])
                        TTop("pool", nD3[:, j0:j0 + n, :], nD3[:, j0:j0 + n, :],
                             C(MPOS0 + d).unsqueeze(1).to_broadcast([128, n, 128]), ALU.add, r=[nD.b, cst.b], w=[nD.b])
                    for (j0, n, d, blk0) in runs(g["us"]):
                        TTop("pool", ws["Q"].ap[:, gq * 4 + j0:gq * 4 + j0 + n, :].rearrange("p a b -> p (a b)"),
                             qT.ap[:, blk0 * 128:(blk0 + n) * 128], eGr.ap[:, j0 * 128:(j0 + n) * 128],
                             ALU.mult, r=[qT.b, eGr.b], w=[ws["Q"].b])
                yield
                for g in G:
                    gq = g["gq"]
                    us = g["us"]
                    nD, DT = dec[gq]
                    ACT(nD.ap, nD.ap, AF.Exp, r=[nD.b], w=[nD.b], scale=-1.0)
                    ACT(DT.ap, DT.ap, AF.Exp, r=[DT.b], w=[DT.b])
                    pkk = nextbank()
                    pkq = nextbank()
                    for j, (d, s_) in enumerate(us):
                        blk = ORDER[d][s_]
                        ks = kT.ap[:, blk * 128:(blk + 1) * 128]
                        MM(pkk.ap[:, j * 128:(j + 1) * 128], ks, ks, r=[kT.b], w=[pkk.b])
                    for j, (d, s_) in enumerate(us):
                        blk = ORDER[d][s_]
                        ks = kT.ap[:, blk * 128:(blk + 1) * 128]
                        MM(pkq.ap[:, j * 128:(j + 1) * 128], ks, qT.ap[:, blk * 128:(blk + 1) * 128],
                           r=[kT.b, qT.b], w=[pkq.b])
                    Mt = g["Mt"]
                    pk3 = pkk.ap.rearrange("p (a b) -> p a b", a=4)
                    nD3 = nD.ap.rearrange("p (a b) -> p a b", a=4)
                    for (j0, n, d, blk0) in runs(us):
                        TTop("dve", Mt.ap[:, j0:j0 + n, :], pk3[:, j0:j0 + n, :],
                             cols(bt, blk0, n, d).unsqueeze(2).to_broadcast([128, n, 128]), ALU.mult,
                             r=[pkk.b, bt.b], w=[Mt.b])
                    TTop("pool", Mt.ap, Mt.ap, nD3, ALU.mult, r=[Mt.b, nD.b], w=[Mt.b])
                    TTop("dve", ws["AT"].ap[:, gq * 4:gq * 4 + 4, :].rearrange("p a b -> p (a b)"), pkq.ap, DT.ap, ALU.mult,
                         r=[pkq.b, DT.b], w=[ws["AT"].b])
                yield
                for g in G:
                    Mt = g["Mt"]
                    TTop("pool", g["R"][0].ap, Mt.ap, BDB.unsqueeze(1).to_broadcast([128, 4, 128]), ALU.mult,
                         r=[Mt.b, cstb.b], w=[g["R"][0].b])
                    TTop("pool", g["Moff"].ap, Mt.ap, NBDB.unsqueeze(1).to_broadcast([128, 4, 128]), ALU.mult,
                         r=[Mt.b, cstb.b], w=[g["Moff"].b])
                yield
                for g in G:
                    R_, P_, Y_ = g["R"], g["P"], g["Y"]
                    pb = nextbank()
                    for j in range(4):
                        MM(pb.ap[:, j * 128:(j + 1) * 128], R_[0].ap[:, j, :], IDB, r=[R_[0].b, cstb.b], w=[pb.b])
                    CP("act", f2(P_[0]), pb.ap, r=[pb.b], w=[P_[0].b])
                    TTop("dve", Y_[0].ap, IDB.unsqueeze(1).to_broadcast([128, 4, 128]),
                         pb.ap.rearrange("p (a b) -> p a b", a=4), ALU.subtract, r=[cstb.b, pb.b], w=[Y_[0].b])
                yield
                for seg in range(6):
                    for g in G:
                        R_, P_, Y_ = g["R"], g["P"], g["Y"]
                        rp, y = g["rp"], g["y"]
                        if seg >= 1:
                            pc = nextbank()
                            for j in range(4):
                                MM(pc.ap[:, j * 128:(j + 1) * 128], R_[rp].ap[:, j, :], Y_[y].ap[:, j, :],
                                   r=[R_[rp].b, Y_[y].b], w=[pc.b])
                            TTop("dve", f2(Y_[1 - y]), f2(Y_[y]), pc.ap, ALU.add, r=[Y_[y].b, pc.b], w=[Y_[1 - y].b])
                            g["y"] = 1 - y
                        if seg <= 4:
                            pa = nextbank()
                            for j in range(4):
                                MM(pa.ap[:, j * 128:(j + 1) * 128], P_[rp].ap[:, j, :], R_[rp].ap[:, j, :],
                                   r=[P_[rp].b, R_[rp].b], w=[pa.b])
                            CP("act", f2(R_[1 - rp]), pa.ap, r=[pa.b], w=[R_[1 - rp].b])
                            if seg <= 3:
                                pbk = nextbank()
                                for j in range(4):
                                    MM(pbk.ap[:, j * 128:(j + 1) * 128], R_[rp].ap[:, j, :], P_[rp].ap[:, j, :],
                                       r=[P_[rp].b, R_[rp].b], w=[pbk.b])
                                CP("act", f2(P_[1 - rp]), pbk.ap, r=[pbk.b], w=[P_[1 - rp].b])
                            g["rp"] = 1 - rp
                    yield
                for g in G:
                    rp, y = g["rp"], g["y"]
                    g["Yf"], g["YT"], g["Z1"] = g["Y"][y], g["P"][1], g["P"][0]
                    g["ek"], g["kd"], g["Wt"] = g["R"][1 - rp], g["Y"][1 - y], g["R"][rp]
                    Yf, YT, Z1, Moff = g["Yf"], g["YT"], g["Z1"], g["Moff"]
                    pb = nextbank()
                    for j in range(4):
                        MM(pb.ap[:, j * 128:(j + 1) * 128], Yf.ap[:, j, :], IDB, r=[Yf.b, cstb.b], w=[pb.b])
                    CP("act", f2(YT), pb.ap, r=[pb.b], w=[YT.b])
                    pz = nextbank()
                    for j in range(4):
                        MM(pz.ap[:, j * 128:(j + 1) * 128], Moff.ap[:, j, :], Yf.ap[:, j, :], r=[Moff.b, Yf.b], w=[pz.b])
                    CP("act", f2(Z1), pz.ap, r=[pz.b], w=[Z1.b])
                    for (j0, n, d, blk0) in runs(g["us"]):
                        TTop("pool", g["ek"].ap[:, j0:j0 + n, :], k_tm.ap[:, blk0:blk0 + n, :],
                             cols(n2eG, blk0, n, d).unsqueeze(2).to_broadcast([128, n, 128]),
                             ALU.mult, r=[k_tm.b, n2eG.b], w=[g["ek"].b])
                        TTop("pool", g["kd"].ap[:, j0:j0 + n, :], k_tm.ap[:, blk0:blk0 + n, :],
                             cols(kdec, blk0, n, d).unsqueeze(2).to_broadcast([128, n, 128]),
                             ALU.mult, r=[k_tm.b, kdec.b], w=[g["kd"].b])
                yield
                for g in G:
                    Yf, YT, Z1 = g["Yf"], g["YT"], g["Z1"]
                    pz2 = nextbank()
                    for j in range(4):
                        MM(pz2.ap[:, j * 128:(j + 1) * 128], YT.ap[:, j, :], Z1.ap[:, j, :], r=[YT.b, Z1.b], w=[pz2.b])
                    tmpT = tf.get()
                    TTop("dve", tmpT.ap, f2(Yf), pz2.ap, ALU.subtract, r=[Yf.b, pz2.b], w=[tmpT.b])
                    TTs = g["Mt"]
                    tmv = tmpT.ap.rearrange("p (a b) -> p a b", a=4)
                    for (j0, n, d, blk0) in runs(g["us"]):
                        TTop("pool", TTs.ap[:, j0:j0 + n, :], tmv[:, j0:j0 + n, :],
                             cols(hbt, blk0, n, d).unsqueeze(2).to_broadcast([128, n, 128]), ALU.mult,
                             r=[tmpT.b, hbt.b], w=[TTs.b])
                yield
                for g in G:
                    gq = g["gq"]
                    TTs, ek, Wt = g["Mt"], g["ek"], g["Wt"]
                    usl = slice(gq * 4, gq * 4 + 4)
                    pU = nextbank()
                    pW = nextbank()
                    for j, (d, s_) in enumerate(g["us"]):
                        blk = ORDER[d][s_]
                        MM(pU.ap[:, j * 128:(j + 1) * 128], TTs.ap[:, j, :], v_tm.ap[:, blk, :], r=[TTs.b, v_tm.b], w=[pU.b])
                    for j in range(4):
                        MM(pW.ap[:, j * 128:(j + 1) * 128], TTs.ap[:, j, :], ek.ap[:, j, :], r=[TTs.b, ek.b], w=[pW.b])
                    CP("act", ws["Ub"].ap[:, usl, :].rearrange("p a b -> p (a b)"), pU.ap, r=[pU.b], w=[ws["Ub"].b])
                    CP("dve", f2(Wt), pW.ap, r=[pW.b], w=[Wt.b])
                yield
                for g in G:
                    gq = g["gq"]
                    kd, Wt = g["kd"], g["Wt"]
                    usl = slice(gq * 4, gq * 4 + 4)
                    pWT = nextbank()
                    pB = nextbank()
                    pQ = nextbank()
                    for j in range(4):
                        MM(pWT.ap[:, j * 128:(j + 1) * 128], Wt.ap[:, j, :], kd.ap[:, j, :], r=[Wt.b, kd.b], w=[pWT.b])
                    for j in range(4):
                        MM(pB.ap[:, j * 128:(j + 1) * 128], kd.ap[:, j, :], ws["Ub"].ap[:, gq * 4 + j, :],
                           r=[kd.b, ws["Ub"].b], w=[pB.b])
                    for j in range(4):
                        MM(pQ.ap[:, j * 128:(j + 1) * 128], Wt.ap[:, j, :], ws["AT"].ap[:, gq * 4 + j, :],
                           r=[Wt.b, ws["AT"].b], w=[pQ.b])
                    CP("act", ws["nWT"].ap[:, usl, :].rearrange("p a b -> p (a b)"), pWT.ap, r=[pWT.b], w=[ws["nWT"].b])
                    CP("act", ws["Bb"].ap[:, usl, :].rearrange("p a b -> p (a b)"), pB.ap, r=[pB.b], w=[ws["Bb"].b])
                    qv = ws["Q"].ap[:, usl, :].rearrange("p a b -> p (a b)")
                    TTop("dve", qv, qv, pQ.ap, ALU.add, r=[ws["Q"].b, pQ.b], w=[ws["Q"].b])
                yield

            def scan_step(d, s_):
                blk = ORDER[d][s_]
                ws = WS[(3 * h + s_ // 6) % 2]
                u = wave_units(s_ // 6).index((d, s_))
                cur = s_ % 2
                nxt = 1 - cur
                pst_ = banks[d]
                MM(pst_.ap[:, 0:128], ws["nWT"].ap[:, u, :], Sb[d][cur].ap, start=True, stop=False,
                   r=[ws["nWT"].b, Sb[d][cur].b], w=[pst_.b])
                MM(pst_.ap[:, 0:128], IDB, ws["Bb"].ap[:, u, :], start=False, stop=True,
                   r=[cstb.b, ws["Bb"].b], w=[pst_.b])
                STT("dve", Sb[d][nxt].ap, S32[d][cur].ap, col(eGl, blk, d), pst_.ap[:, 0:128], ALU.mult, ALU.add,
                    r=[S32[d][cur].b, eGl.b, pst_.b], w=[Sb[d][nxt].b])
                STT("dve", S32[d][nxt].ap, S32[d][cur].ap, col(eGl, blk, d), pst_.ap[:, 0:128], ALU.mult, ALU.add,
                    r=[S32[d][cur].b, eGl.b, pst_.b], w=[S32[d][nxt].b])
                if blk >= 2:
                    li = blk - 2
                    po = banks[2]
                    MM(po.ap[:, d * 128:(d + 1) * 128], ws["Q"].ap[:, u, :], Sb[d][cur].ap, start=True, stop=False,
                       r=[ws["Q"].b, Sb[d][cur].b], w=[po.b])
                    MM(po.ap[:, d * 128:(d + 1) * 128], ws["AT"].ap[:, u, :], ws["Ub"].ap[:, u, :], start=False, stop=True,
                       r=[ws["AT"].b, ws["Ub"].b], w=[po.b])
                    first = (d == 0 and li < 8) or (d == 1 and li >= 8)
                    if first:
                        CP("act", o_acc.ap[:, li, :], po.ap[:, d * 128:(d + 1) * 128], r=[po.b], w=[o_acc.b])
                    else:
                        TTop("dve", o_acc.ap[:, li, :], o_acc.ap[:, li, :], po.ap[:, d * 128:(d + 1) * 128], ALU.add,
                             r=[po.b, o_acc.b], w=[o_acc.b])

            def drain(gen):
                for _ in gen:
                    pass

            def finish_gen():
                S.op("pool", lambda e: e.memset(ssq.ap, 0.0), (), [ssq.b])
                for li in range(16):
                    junk = tb16.get()
                    S.op("act", lambda e, li=li, junk=junk: e.activation(out=junk.ap[:, 0:128], in_=o_acc.ap[:, li, :],
                                                                          func=AF.Square, accum_out=ssq.ap[:, li:li + 1]),
                         [o_acc.b], [junk.b, ssq.b])
                ACT(e_rs.ap[:, 4:8], kc.ap[:, 0:4], AF.Copy, r=[kc.b, ssq.b], w=[e_rs.b, ssq.b])
                ACT(rstd.ap, ssq.ap, AF.Identity, r=[ssq.b, kc.b], w=[rstd.b], bias=KC("rmseps"), scale=1.0 / 128.0)
                TTop("pool", rstd.ap, rstd.ap, KC("mhalf").to_broadcast([128, 16]), ALU.pow, r=[rstd.b, kc.b], w=[rstd.b])
                for li in range(16):
                    ACT(on_all.ap[:, li, :], o_acc.ap[:, li, :], AF.Copy, r=[o_acc.b, rstd.b], w=[on_all.b],
                        scale=rstd.ap[:, li:li + 1])
                if b == 0 and h == 0:
                    dbg("o_acc", o_acc, [128, 16, 128])
                yield
                for g0 in range(0, 16, 4):
                    pb = nextbank()
                    for j in range(4):
                        MM(pb.ap[:, j * 128:(j + 1) * 128], on_all.ap[:, g0 + j, :], IDB, r=[on_all.b, cstb.b], w=[pb.b])
                    ti = g0 // 4
                    ogt = tb16.get()
                    STT("dve", ogt.ap, pb.ap, onwc.ap[:, 0:1], zs2T.ap[:, ti * 512:(ti + 1) * 512], ALU.mult, ALU.mult,
                        r=[pb.b, onwc.b, zs2T.b], w=[ogt.b])
                    DMA(og_scr[h, :, ti * 512:(ti + 1) * 512], ogt.ap, r=[ogt.b], w=[og_buf[h][ti]])
                    if b == 0 and h == 0:
                        dbg("og0_%d" % ti, ogt, [128, 512], BF16)
                    yield

            return {"prep": wave_prep, "scan": scan_step, "fin": finish_gen, "init": init}

        nheads = NH if stage >= 4 else 1
        en = early_gen(0)
        for _ in en:
            pass
        if stage >= 2:
            ctx = [head_ctx(h) for h in range(nheads)]
            seq = [(h, w) for h in range(nheads) for w in range(3)]
            st = {"en": early_gen(1) if nheads > 1 else None, "fin": None}

            def side(n=1):
                for _ in range(n):
                    if st["fin"] is not None:
                        try:
                            next(st["fin"])
                        except StopIteration:
                            st["fin"] = None
                    elif st["en"] is not None:
                        try:
                            next(st["en"])
                        except StopIteration:
                            st["en"] = None
                    if b == 0:
                        if cast_head:
                            emit_casts(cast_head, 1)
                        else:
                            emit_casts(cast_rest, 1)

            rr_lo[0] = 3
            cur = ctx[0]["prep"](0)
            for _ in cur:
                side()
            for i, (h, w) in enumerate(seq):
                if w == 0:
                    ctx[h]["init"]()
                nxt = None
                if i + 1 < len(seq):
                    h2, w2 = seq[i + 1]
                    if w2 == 0:
                        while st["fin"] is not None or st["en"] is not None:
                            side()
                        if h + 2 < nheads:
                            st["en"] = early_gen(h + 2)
                    nxt = ctx[h2]["prep"](w2)
                for s_ in range(6 * w, 6 * w + 6):
                    for d in range(2):
                        ctx[h]["scan"](d, s_)
                        side()
                        adv(nxt, 1)
                        side()
                        adv(nxt, 1)
                if nxt is not None:
                    for _ in nxt:
                        side()
                if w == 2:
                    st["fin"] = ctx[h]["fin"]()
                    next(st["fin"])
            while st["fin"] is not None or st["en"] is not None:
                side()
            rr_lo[0] = 0
        emit_casts(cast_head, 100)
        emit_casts(cast_rest, 1000)
        S.barrier()
        AR.top = m_phase
        if stage < 4:
            AR.top = mb
            return

        ymT = [[AR.alloc([512], BF16, "ym%d_%d" % (fc, t)) for t in range(4)] for fc in range(8)]
        m_B = AR.top
        pT = [[AR.alloc([512], BF16, "p%d_%d" % (fc, t)) for t in range(4)] for fc in range(8)]
        tf = Pool(AR, 8, [512], F32, "tfB")

        def wload(sc, sbufs, c):
            wt = wtp.get()
            DMA(wt.ap.rearrange("p a b -> p (a b)"), sc[c * 128:(c + 1) * 128, :], r=[sbufs[c]], w=[wt.b])
            return wt

        for fc in range(8):
            wxb = wload(sc_in, sb_in, 32 + fc)
            wbgt = wload(sc_in, sb_in, 40 + fc)
            wcg = wload(sc_in, sb_in, 48 + fc)
            for t in range(4):
                ti = t + 1
                pbs = []
                for wt in (wxb, wbgt, wcg):
                    pb = nextbank()
                    for k in range(8):
                        MM(pb.ap, wt.ap[:, k, :], uT[ti].ap[:, k, :], start=(k == 0), stop=(k == 7),
                           r=[wt.b, uT[ti].b], w=[pb.b])
                    pbs.append(pb)
                cgs = tf.get()
                CP("act", cgs.ap, pbs[2].ap, r=[pbs[2].b], w=[cgs.b])
                cgx = tf.get()
                TTop("dve", cgx.ap, pbs[0].ap, cgs.ap, ALU.mult, r=[pbs[0].b, cgs.b], w=[cgx.b])
                ct = tf.get()
                TTop("pool", ct.ap, cgx.ap, cwb.ap[:, fc, 1:2].to_broadcast([128, 512]), ALU.mult,
                     r=[cgx.b, cwb.b], w=[ct.b])
                xv = cgx.ap.rearrange("p (r l) -> p r l", r=8)
                cv = ct.ap.rearrange("p (r l) -> p r l", r=8)
                STT("dve", cv[:, :, 1:64], xv[:, :, 0:63], cwb.ap[:, fc, 0:1], cv[:, :, 1:64], ALU.mult, ALU.add,
                    r=[cgx.b, cwb.b, ct.b], w=[ct.b])
                STT("dve", cv[:, :, 0:63], xv[:, :, 1:64], cwb.ap[:, fc, 2:3], cv[:, :, 0:63], ALU.mult, ALU.add,
                    r=[cgx.b, cwb.b, ct.b], w=[ct.b])
                TTop("dve", pT[fc][t].ap, pbs[1].ap, ct.ap, ALU.mult, r=[pbs[1].b, ct.b], w=[pT[fc][t].b])
        if b == 0:
            dbg("pT0_0", pT[0][0], [128, 512], BF16)
        ogp = Pool(AR, 2, [8, 512], BF16, "ogp")
        for t in range(4):
            ti = t + 1
            ogt = ogp.get()
            DMA(ogt.ap, og_scr[:, :, t * 512:(t + 1) * 512].rearrange("k p c -> p k c"),
                r=[og_buf[k][t] for k in range(NH)], w=[ogt.b])
            for fc in range(8):
                wga = wload(sc_in, sb_in, 56 + fc)
                wgb = wload(sc_in, sb_in, 64 + fc)
                wa = wload(sc_a, sb_a, fc)
                wb_ = wload(sc_b, sb_b, fc)
                pga = nextbank(); pgb = nextbank(); pya = nextbank(); pyb = nextbank()
                for k in range(8):
                    MM(pga.ap, wga.ap[:, k, :], uT[ti].ap[:, k, :], start=(k == 0), stop=(k == 7),
                       r=[wga.b, uT[ti].b], w=[pga.b])
                for k in range(8):
                    MM(pgb.ap, wgb.ap[:, k, :], uT[ti].ap[:, k, :], start=(k == 0), stop=(k == 7),
                       r=[wgb.b, uT[ti].b], w=[pgb.b])
                for k in range(8):
                    MM(pya.ap, wa.ap[:, k, :], ogt.ap[:, k, :], start=(k == 0), stop=(k == 7),
                       r=[wa.b, ogt.b], w=[pya.b])
                for k in range(8):
                    MM(pyb.ap, wb_.ap[:, k, :], pT[k][t].ap, start=(k == 0), stop=(k == 7),
                       r=[wb_.b, pT[k][t].b], w=[pyb.b])
                ta = tf.get(); tb_ = tf.get()
                ACT(ta.ap, pga.ap, AF.Tanh, r=[pga.b], w=[ta.b], scale=0.5)
                ACT(tb_.ap, pgb.ap, AF.Tanh, r=[pgb.b], w=[tb_.b], scale=0.5)
                STT("dve", ta.ap, ta.ap, 1.0, pya.ap, ALU.add, ALU.mult, r=[ta.b, pya.b], w=[ta.b])
                STT("dve", tb_.ap, tb_.ap, 1.0, pyb.ap, ALU.add, ALU.mult, r=[tb_.b, pyb.b], w=[tb_.b])
                TTop("pool", ymT[fc][t].ap, ta.ap, tb_.ap, ALU.add, r=[ta.b, tb_.b], w=[ymT[fc][t].b])
        if b == 0:
            dbg("ym0_0", ymT[0][0], [128, 512], BF16)
        S.barrier()
        AR.top = m_B
        if stage < 5:
            AR.top = mb
            return

        m_main = AR.top
        AR.top = mb
        hsqT = AR.alloc([32, 512], BF16, "hsqT")
        assert AR.top <= m_phase, (AR.top, m_phase)
        AR.top = m_main
        xa = AR.alloc([8, 512], F32, "xa")
        outT = AR.alloc([8, 512], F32, "outT")
        u2T = AR.alloc([8, 512], BF16, "u2T")
        tf = Pool(AR, 8, [512], F32, "tfC")
        w2p = Pool(AR, 2, [32, 128], BF16, "w2p")
        xs = Pool(AR, 6, [1024], F32, "xsC")

        sq_all = AR.alloc([8, 512], BF16, "sq_all")
        ln_dg = AR.alloc([8, 128], F32, "ln_dg")
        ln_A = AR.alloc([512], F32, "ln_A")
        ln_B = AR.alloc([512], F32, "ln_B")
        ln_mean = AR.alloc([4], F32, "ln_mean")
        ln_msq = AR.alloc([4], F32, "ln_msq")
        ln_var = AR.alloc([4], F32, "ln_var")
        ln_nmr = AR.alloc([4], F32, "ln_nmr")

        def layer_norm(src, dst_fns):
            for fc in range(8):
                ACT(sq_all.ap[:, fc, :], src.ap[:, fc, :], AF.Square, r=[src.b], w=[sq_all.b])
            pst = nextbank()
            for j in range(4):
                for fc in range(8):
                    MM(pst.ap[:, 2 * j:2 * j + 1], src.ap[:, fc, j * 128:(j + 1) * 128], C(ONES)[:, 0:1],
                       start=(fc == 0), stop=(fc == 7), r=[src.b, cst.b], w=[pst.b])
                for fc in range(8):
                    MM(pst.ap[:, 2 * j + 1:2 * j + 2], sq_all.ap[:, fc, j * 128:(j + 1) * 128], ONB[:, 0:1],
                       start=(fc == 0), stop=(fc == 7), r=[sq_all.b, cstb.b], w=[pst.b])
            pv = pst.ap[:, 0:8].rearrange("p (a b) -> p a b", a=4)
            ACT(ln_mean.ap, pv[:, :, 0], AF.Copy, r=[pst.b], w=[ln_mean.b], scale=1.0 / D)
            TTop("dve", ln_msq.ap, ln_mean.ap, ln_mean.ap, ALU.mult, r=[ln_mean.b], w=[ln_msq.b])
            STT("dve", ln_var.ap, pv[:, :, 1], 1.0 / D, ln_msq.ap, ALU.mult, ALU.subtract,
                r=[pst.b, ln_msq.b], w=[ln_var.b])
            ACT(ln_var.ap, ln_var.ap, AF.Identity, r=[ln_var.b, kc.b], w=[ln_var.b], bias=KC("lneps"), scale=1.0)
            TTop("pool", ln_var.ap, ln_var.ap, KC("mhalf").to_broadcast([128, 4]), ALU.pow,
                 r=[ln_var.b, kc.b], w=[ln_var.b])
            STT("dve", ln_nmr.ap, ln_mean.ap, -1.0, ln_var.ap, ALU.mult, ALU.mult,
                r=[ln_mean.b, ln_var.b], w=[ln_nmr.b])
            idb = C(IDENT).unsqueeze(1).to_broadcast([128, 4, 128])
            TTop("pool", ln_dg.ap[:, 0:4, :], idb, ln_var.ap.unsqueeze(2).to_broadcast([128, 4, 128]), ALU.mult,
                 r=[cst.b, ln_var.b], w=[ln_dg.b])
            TTop("pool", ln_dg.ap[:, 4:8, :], idb, ln_nmr.ap.unsqueeze(2).to_broadcast([128, 4, 128]), ALU.mult,
                 r=[cst.b, ln_nmr.b], w=[ln_dg.b])
            pA = nextbank()
            pB = nextbank()
            for j in range(4):
                MM(pA.ap[:, j * 128:(j + 1) * 128], C(ONES), ln_dg.ap[:, j, :], r=[cst.b, ln_dg.b], w=[pA.b])
            for j in range(4):
                MM(pB.ap[:, j * 128:(j + 1) * 128], C(ONES), ln_dg.ap[:, 4 + j, :], r=[cst.b, ln_dg.b], w=[pB.b])
            CP("act", ln_A.ap, pA.ap, r=[pA.b], w=[ln_A.b])
            CP("act", ln_B.ap, pB.ap, r=[pB.b], w=[ln_B.b])
            for fc in range(8):
                xc = tf.get()
                TTop("dve", xc.ap, src.ap[:, fc, :], ln_A.ap, ALU.mult, r=[src.b, ln_A.b], w=[xc.b])
                TTop("pool", xc.ap, xc.ap, ln_B.ap, ALU.add, r=[xc.b, ln_B.b], w=[xc.b])
                for (oap, sc_, bi_, rb, wbuf) in dst_fns:
                    ACT(oap(fc), xc.ap, AF.Identity, r=[xc.b] + rb, w=[wbuf], bias=bi_(fc), scale=sc_(fc))

        wo_t = [None] * 8
        for t in range(4):
            ti = t + 1
            blks = []
            for j in range(4):
                s = xs.get()
                DMA(s.ap, x_d[b, t * 512 + j * 128: t * 512 + (j + 1) * 128, :], w=[s.b])
                blks.append(s)
            for fc in range(8):
                pb = nextbank()
                for j in range(4):
                    MM(pb.ap[:, j * 128:(j + 1) * 128], blks[j].ap[:, fc * 128:(fc + 1) * 128], C(IDENT),
                       r=[blks[j].b, cst.b], w=[pb.b])
                ACT(xa.ap[:, fc, :], pb.ap, AF.Copy, r=[pb.b], w=[xa.b], scale=ALPHA)
            for fc in range(8):
                wo = wload(sc_o, sb_o, fc)
                pb = nextbank()
                for k in range(8):
                    MM(pb.ap, wo.ap[:, k, :], ymT[k][t].ap, start=(k == 0), stop=(k == 7), r=[wo.b, ymT[k][t].b], w=[pb.b])
                STT("dve", xa.ap[:, fc, :], pb.ap, hg1.ap[:, fc, b:b + 1], xa.ap[:, fc, :], ALU.mult, ALU.add,
                    r=[pb.b, hg1.b, xa.b], w=[xa.b])
            if b == 0 and t == 0:
                dbg("pre1", xa, [128, 8, 512])
            layer_norm(xa, [
                (lambda fc: outT.ap[:, fc, :], lambda fc: A1s.ap[:, fc:fc + 1], lambda fc: A1b.ap[:, fc, b:b + 1],
                 [A1s.b, A1b.b], outT.b),
                (lambda fc: u2T.ap[:, fc, :], lambda fc: U2s.ap[:, fc, b:b + 1], lambda fc: U2b.ap[:, fc, b:b + 1],
                 [U2s.b, U2b.b], u2T.b)])
            if b == 0 and t == 0:
                dbg("x1a", outT, [128, 8, 512]); dbg("u2T", u2T, [128, 8, 512], BF16)
            for ffc in range(32):
                w1 = wload(sc_f1, sb_f1, ffc)
                pb = nextbank()
                for k in range(8):
                    MM(pb.ap, w1.ap[:, k, :], u2T.ap[:, k, :], start=(k == 0), stop=(k == 7), r=[w1.b, u2T.b], w=[pb.b])
                rl = tf.get()
                ACT(rl.ap, pb.ap, AF.Relu, r=[pb.b, bf1.b], w=[rl.b], bias=bf1.ap[:, ffc:ffc + 1], scale=1.0)
                STT("dve", hsqT.ap[:, ffc, :], pb.ap, bf1.ap[:, ffc:ffc + 1], rl.ap, ALU.add, ALU.mult,
                    r=[pb.b, bf1.b, rl.b], w=[hsqT.b])
            for fc in range(8):
                w2 = w2p.get()
                DMA(w2.ap.rearrange("p a b -> p (a b)"), sc_f2[fc * 128:(fc + 1) * 128, :], r=[sb_f2[fc]], w=[w2.b])
                pb = nextbank()
                for k in range(32):
                    MM(pb.ap, w2.ap[:, k, :], hsqT.ap[:, k, :], start=(k == 0), stop=(k == 31), r=[w2.b, hsqT.b], w=[pb.b])
                STT("dve", outT.ap[:, fc, :], pb.ap, modT.ap[:, 40 + fc, b:b + 1], outT.ap[:, fc, :], ALU.mult, ALU.add,
                    r=[pb.b, modT.b, outT.b], w=[outT.b])
            if b == 0 and t == 0:
                dbg("pre2", outT, [128, 8, 512])
            layer_norm(outT, [
                (lambda fc: xa.ap[:, fc, :], lambda fc: ln2g.ap[:, fc:fc + 1], lambda fc: ln2b.ap[:, fc:fc + 1],
                 [ln2g.b, ln2b.b], xa.b)])
            if b == 0 and t == 0:
                dbg("xo", xa, [128, 8, 512])
            for j in range(4):
                os_ = xs.get()
                for half in range(2):
                    pb = nextbank()
                    for q in range(4):
                        fc = half * 4 + q
                        MM(pb.ap[:, q * 128:(q + 1) * 128], xa.ap[:, fc, j * 128:(j + 1) * 128], C(IDENT),
                           r=[xa.b, cst.b], w=[pb.b])
                    CP("act" if half == 0 else "dve", os_.ap[:, half * 512:(half + 1) * 512], pb.ap, r=[pb.b], w=[os_.b])
                DMA(out_d[b, t * 512 + j * 128: t * 512 + (j + 1) * 128, :], os_.ap, r=[os_.b])
        S.barrier()
        AR.top = mb

    for b in range(nbatch):
        batch_program(b)

    print("arena peak bytes/partition:", AR.peak, "ops:", {e: len(S.ops[e]) for e in ENGS})
    S.run()
    es.close()
    return nc, dbg_outs


_CACHE = {}


def _consts():
    i = np.arange(128)[:, None]
    j = np.arange(128)[None, :]
    c = np.zeros((NCONST, 128, 128), np.float32)
    c[0] = (i == j)
    c[1] = 1.0
    c[2] = (i <= j)
    c[3] = (i >= j)
    c[4] = np.where(j < i, 0.0, BIG)
    c[5] = np.where(j > i, 0.0, BIG)
    c[6] = np.where(j >= i, 0.0, -BIG)
    c[7] = np.where(j <= i, 0.0, -BIG)
    bd = ((i // 64) == (j // 64)).astype(np.float32)
    c[8] = bd
    c[9] = 1.0 - bd
    return np.ascontiguousarray(c.transpose(1, 0, 2).reshape(128, NCONST * 128))


def _pf(v, n):
    return np.ascontiguousarray(np.asarray(v, np.float32).reshape(n, 128).T)


def prepare_inputs(inputs, core):
    f = lambda k: np.ascontiguousarray(np.asarray(inputs[k], np.float32))
    b0 = 2 * core
    cc = np.stack([f("c")[b0], f("c")[b0 + 1], f("c_ctx")], axis=0)
    cT = np.ascontiguousarray(cc.reshape(3, 8, 128).transpose(2, 1, 0).reshape(128, 24))
    cq = f("conv_qkv")[0]
    cwq = np.ascontiguousarray(cq.reshape(3, 24, 128).transpose(2, 1, 0).reshape(128, 72))
    cb = f("conv_b")[0]
    cwb = np.ascontiguousarray(cb.reshape(3, 8, 128).transpose(2, 1, 0).reshape(128, 24))
    m = {
        "x": np.ascontiguousarray(f("x")[b0:b0 + 2]),
        "ctx": np.ascontiguousarray(f("ctx")[b0:b0 + 2]),
        "cT": cT,
        "w_ada": f("w_ada")[0],
        "b_adaT": _pf(f("b_ada")[0], 48),
        "w_in": f("w_in")[0],
        "cwq": cwq,
        "alog_bc": np.ascontiguousarray(np.broadcast_to(f("a_log")[0].reshape(1, 16), (128, 16))),
        "dtb_bc": np.ascontiguousarray(np.broadcast_to(f("dt_bias")[0].reshape(1, 16), (128, 16))),
        "onw_col": np.ascontiguousarray(f("o_norm_w")[0].reshape(128, 1)),
        "w_a_out": f("w_a_out")[0],
        "cwb": cwb,
        "w_b_out": f("w_b_out")[0],
        "w_o": f("w_o")[0],
        "ln1gT": _pf(f("ln1_g")[0], 8),
        "ln1bT": _pf(f("ln1_b")[0], 8),
        "w_ff1": f("w_ff1")[0],
        "b_ff1T": _pf(f("b_ff1")[0], 32),
        "w_ff2": f("w_ff2")[0],
        "b_ff2T": _pf(f("b_ff2")[0], 8),
        "ln2gT": _pf(f("ln2_g")[0], 8),
        "ln2bT": _pf(f("ln2_b")[0], 8),
        "consts": _consts(),
    }
    return m


def kernel(**inputs):
    if "nc" not in _CACHE:
        _CACHE["nc"] = build_program()[0]
    nc = _CACHE["nc"]
    n = 8
    shared = None
    in_maps = []
    for c in range(n):
        m = prepare_inputs(inputs, c)
        if shared is None:
            shared = m
        else:
            for k in m:
                if k not in ("x", "ctx", "cT"):
                    m[k] = shared[k]
        in_maps.append(m)
    res = run_bass_kernel_spmd(nc, in_maps, core_ids=list(range(n)))
    out = np.concatenate([np.asarray(r["out"], np.float32) for r in res.results], axis=0)
    return out
```

```python
import numpy as np
from contextlib import ExitStack
import concourse.bass as bass
import concourse.mybir as mybir
from concourse.bass_utils import run_bass_kernel_spmd

F32 = mybir.dt.float32
BF16 = mybir.dt.bfloat16
AF = mybir.ActivationFunctionType
ALU = mybir.AluOpType
AX = mybir.AxisListType

ENGS = ("pe", "act", "dve", "pool", "sp")
N_LANES = 24

D = 1024
SEQ = 2048
CTXL = 256
NTOK = SEQ + CTXL
NBLK = NTOK // 128
NH = 8
P_TOTAL = 9248
ALPHA = 2.0 ** 0.25
LN_EPS = 1e-5
RMS_EPS = 1e-6
L2_EPS = 1e-6
TILES = [(0, 256), (256, 512), (768, 512), (1280, 512), (1792, 512)]
ORDER = [list(range(18)), [1, 0] + list(range(17, 1, -1))]
NCONST = 10
BIG = 30000.0


class Buf:
    __slots__ = ("name", "w", "r", "excl")

    def __init__(self, name, excl=False):
        self.name = name
        self.w = None
        self.r = []
        self.excl = excl


class Op:
    __slots__ = ("eng", "fn", "deps", "idx", "sig", "tok", "is_dma")

    def __init__(self, eng, fn):
        self.eng = eng
        self.fn = fn
        self.deps = []
        self.sig = False
        self.tok = None
        self.is_dma = False


class Sched:
    def __init__(self, nc):
        self.nc = nc
        self.ops = {e: [] for e in ENGS}
        self.lane_cnt = [0] * N_LANES
        self.lane_last = [None] * N_LANES
        self.next_lane = 0
        self.next_lane2 = 0

    def _track(self, op, reads, writes):
        deps = []
        rd, wr = [], []
        for b in reads:
            (wr if b.excl else rd).append(b)
        wr.extend(writes)
        for b in rd:
            if b.w is not None:
                deps.append(b.w)
        for b in wr:
            if b.w is not None:
                deps.append(b.w)
            deps.extend(b.r)
        for b in rd:
            b.r.append(op)
        for b in wr:
            b.w = op
            b.r = []
        seen = set()
        for d in deps:
            if d is op or id(d) in seen:
                continue
            seen.add(id(d))
            if d.eng == "pe" and op.eng == "pe" and not d.is_dma and not op.is_dma:
                continue
            op.deps.append(d)

    def op(self, eng, fn, reads=(), writes=()):
        o = Op(eng, fn)
        self._track(o, reads, writes)
        self.ops[eng].append(o)
        return o

    def dma(self, fn, reads=(), writes=(), eng="sp"):
        o = Op(eng, fn)
        o.is_dma = True
        lane = self.next_lane
        self.next_lane = (self.next_lane + 1) % N_LANES
        prev = self.lane_last[lane]
        self._track(o, reads, writes)
        if prev is not None and all(d is not prev for d in o.deps):
            o.deps.append(prev)
        self.lane_cnt[lane] += 16
        o.tok = ("lane", lane, self.lane_cnt[lane])
        self.lane_last[lane] = o
        self.ops[eng].append(o)
        return o

    def barrier(self):
        lasts = []
        for e in ENGS:
            for o in reversed(self.ops[e]):
                if not o.is_dma:
                    lasts.append(o)
                    break
        lasts.extend(o for o in self.lane_last if o is not None)
        for e in ENGS:
            o = Op(e, lambda eng: eng.nop())
            o.deps = [d for d in lasts if not (d.eng == e and not d.is_dma)]
            self.ops[e].append(o)

    def run(self):
        nc = self.nc
        for e in ENGS:
            for o in self.ops[e]:
                for d in o.deps:
                    if not d.is_dma:
                        d.sig = True
        for e in ENGS:
            c = 0
            for o in self.ops[e]:
                if o.is_dma:
                    continue
                if o.sig:
                    c += 1
                    o.tok = ("eng", e, c)
        with ExitStack() as es:
            esem = {e: es.enter_context(nc.semaphore("s_" + e)) for e in ENGS}
            lsem = [es.enter_context(nc.semaphore("l_%d" % i)) for i in range(N_LANES)]
            block = es.enter_context(nc.Block())

            def body(ename, final=False):
                def _f(eng):
                    waited = {}
                    for o in self.ops[ename]:
                        need = {}
                        for d in o.deps:
                            k = (d.tok[0], d.tok[1])
                            if d.tok[2] > need.get(k, 0):
                                need[k] = d.tok[2]
                        for k, v in need.items():
                            if waited.get(k, 0) >= v:
                                continue
                            eng.wait_ge(esem[k[1]] if k[0] == "eng" else lsem[k[1]], v)
                            waited[k] = v
                        ins = o.fn(eng)
                        if o.is_dma:
                            ins.then_inc(lsem[o.tok[1]], 16)
                        elif o.sig:
                            ins.then_inc(esem[ename], 1)
                    if final:
                        for lane in range(N_LANES):
                            if self.lane_cnt[lane] > 0:
                                eng.wait_ge(lsem[lane], self.lane_cnt[lane])
                return _f

            block.tensor(body("pe"))
            block.scalar(body("act"))
            block.vector(body("dve"))
            block.gpsimd(body("pool"))
            block.sync(body("sp", final=True))


class T:
    __slots__ = ("ap", "b")

    def __init__(self, ap, b):
        self.ap = ap
        self.b = b


class Arena:
    def __init__(self, tens, nf32):
        self.t = tens
        self.cap = nf32 * 4
        self.top = 0
        self.peak = 0

    def alloc(self, free_shape, dt, name, at=None):
        n = int(np.prod(free_shape))
        esz = 4 if dt == F32 else 2
        nbytes = (n * esz + 3) // 4 * 4
        if at is None:
            off = (self.top + 63) // 64 * 64
            self.top = off + nbytes
            self.peak = max(self.peak, self.top)
            assert self.top <= self.cap, ("SBUF arena overflow", name, self.top, self.cap)
        else:
            off = at
        ap = self.t[:, off // 4: off // 4 + nbytes // 4]
        if dt == BF16:
            ap = ap.bitcast(BF16)
        ap = ap[:, 0:n]
        if len(free_shape) == 2:
            ap = ap.rearrange("p (a b) -> p a b", a=free_shape[0])
        elif len(free_shape) == 3:
            ap = ap.rearrange("p (a b c) -> p a b c", a=free_shape[0], b=free_shape[1])
        return T(ap, Buf(name))


class Pool:
    def __init__(self, arena, n, free_shape, dt, name):
        self.items = [arena.alloc(free_shape, dt, "%s%d" % (name, i)) for i in range(n)]
        self.i = 0

    def get(self):
        t = self.items[self.i]
        self.i = (self.i + 1) % len(self.items)
        return t


def build_program(debug=None, stage=99, nbatch=2):
    nc = bass.Bass("TRN2", target_bir_lowering=False)
    S = Sched(nc)
    dbg_outs = {}

    def din(name, shape, dt=F32):
        return nc.dram_tensor(name, list(shape), dt, kind="ExternalInput").ap()

    x_d = din("x", [2, SEQ, D])
    ctx_d = din("ctx", [2, CTXL, D])
    cT_d = din("cT", [128, 24])
    wada_d = din("w_ada", [D, 6 * D])
    badaT_d = din("b_adaT", [128, 48])
    win_d = din("w_in", [D, P_TOTAL])
    cwq_d = din("cwq", [128, 72])
    alog_d = din("alog_bc", [128, 16])
    dtb_d = din("dtb_bc", [128, 16])
    onw_d = din("onw_col", [128, 1])
    wa_d = din("w_a_out", [D, D])
    cwb_d = din("cwb", [128, 24])
    wb_d = din("w_b_out", [D, D])
    wo_d = din("w_o", [D, D])
    ln1g_d = din("ln1gT", [128, 8])
    ln1b_d = din("ln1bT", [128, 8])
    wf1_d = din("w_ff1", [D, 4 * D])
    bf1_d = din("b_ff1T", [128, 32])
    wf2_d = din("w_ff2", [4 * D, D])
    bf2_d = din("b_ff2T", [128, 8])
    ln2g_d = din("ln2gT", [128, 8])
    ln2b_d = din("ln2bT", [128, 8])
    const_d = din("consts", [128, NCONST * 128])
    out_d = nc.dram_tensor("out", [2, SEQ, D], F32, kind="ExternalOutput").ap()

    def dscr(name, nchunk, K):
        t = nc.dram_tensor(name, [nchunk * 128, K * 128], BF16, kind="Internal").ap()
        return t, [Buf("%s_%d" % (name, i)) for i in range(nchunk)]

    sc_in, sb_in = dscr("sc_in", 72, 8)
    sc_bg_t = nc.dram_tensor("sc_bg", [128, 256], BF16, kind="Internal").ap()
    sb_bg = Buf("sc_bg")
    sc_a, sb_a = dscr("sc_a", 8, 8)
    sc_b, sb_b = dscr("sc_b", 8, 8)
    sc_o, sb_o = dscr("sc_o", 8, 8)
    sc_f1, sb_f1 = dscr("sc_f1", 32, 8)
    sc_f2, sb_f2 = dscr("sc_f2", 8, 32)

    def dbg(name, t, shape, dt=F32):
        if not debug or name not in debug:
            return
        o = nc.dram_tensor("dbg_" + name, list(shape), dt, kind="ExternalOutput").ap()
        dbg_outs[name] = o
        ap = t.ap
        S.dma(lambda e: e.dma_start(out=o, in_=ap), reads=[t.b])

    es = ExitStack()
    NF32 = 53200
    arena_t = es.enter_context(nc.sbuf_tensor("arena", [128, NF32], F32))
    AR = Arena(arena_t, NF32)
    banks = []
    for i in range(8):
        pt = es.enter_context(nc.psum_tensor("ps%d" % i, [128, 512], F32))
        banks.append(T(pt[:, :], Buf("ps%d" % i, excl=True)))
    rr = [0]
    rr_lo = [0]

    def nextbank():
        if rr[0] < rr_lo[0]:
            rr[0] = rr_lo[0]
        b = banks[rr[0]]
        rr[0] += 1
        if rr[0] >= 8:
            rr[0] = rr_lo[0]
        return b

    def MM(out, lhsT, rhs, start=True, stop=True, r=(), w=()):
        S.op("pe", lambda e: e.matmul(out, lhsT=lhsT, rhs=rhs, start=start, stop=stop), r, w)

    def ACT(out, in_, func, r=(), w=(), bias=None, scale=1.0):
        if bias is None:
            S.op("act", lambda e: e.activation(out=out, in_=in_, func=func, scale=scale), r, w)
        else:
            S.op("act", lambda e: e.activation(out=out, in_=in_, func=func, bias=bias, scale=scale), r, w)

    def TTop(eng, out, in0, in1, op, r=(), w=()):
        S.op(eng, lambda e: e.tensor_tensor(out=out, in0=in0, in1=in1, op=op), r, w)

    def STT(eng, out, in0, scalar, in1, op0, op1, r=(), w=()):
        S.op(eng, lambda e: e.scalar_tensor_tensor(out=out, in0=in0, scalar=scalar, in1=in1, op0=op0, op1=op1), r, w)

    def TS(eng, out, in0, s1, s2, op0, op1=None, r=(), w=()):
        if op1 is None:
            S.op(eng, lambda e: e.tensor_scalar(out=out, in0=in0, scalar1=s1, scalar2=None, op0=op0), r, w)
        else:
            S.op(eng, lambda e: e.tensor_scalar(out=out, in0=in0, scalar1=s1, scalar2=s2, op0=op0, op1=op1), r, w)

    def CP(eng, out, in_, r=(), w=()):
        if eng == "act":
            S.op("act", lambda e: e.copy(out=out, in_=in_), r, w)
        else:
            S.op(eng, lambda e: e.tensor_copy(out=out, in_=in_), r, w)

    def DMA(out, in_, r=(), w=()):
        S.dma(lambda e: e.dma_start(out=out, in_=in_), r, w)

    cst = AR.alloc([NCONST, 128], F32, "cst")
    DMA(cst.ap.rearrange("p a b -> p (a b)"), const_d, w=[cst.b])
    IDENT, ONES, CUM0, CUM1, MPOS0, MPOS1, MTN0, MTN1, BDM, NBDM = range(10)

    def C(i):
        return cst.ap[:, i, :]

    cstb = AR.alloc([4, 128], BF16, "cstb")
    for j, i in enumerate([IDENT, ONES, BDM, NBDM]):
        CP("dve", cstb.ap[:, j, :], C(i), r=[cst.b], w=[cstb.b])
    IDB, ONB, BDB, NBDB = [cstb.ap[:, j, :] for j in range(4)]

    kc = AR.alloc([8], F32, "kcols")
    KV = {"eps4": 4 * L2_EPS, "eps4q": 4 * L2_EPS * 128.0, "rmseps": RMS_EPS, "lneps": LN_EPS, "mhalf": -0.5,
          "one": 1.0, "zero": 0.0}
    KI = {}
    for j, (k, v) in enumerate(KV.items()):
        KI[k] = j
        S.op("pool", lambda e, j=j, v=v: e.memset(kc.ap[:, j:j + 1], v), (), [kc.b])

    def KC(name):
        j = KI[name]
        return kc.ap[:, j:j + 1]

    def small(name, shape):
        return AR.alloc(shape, F32, name)

    cT = small("cT", [8, 3])
    badaT = small("badaT", [48])
    cwq = small("cwq", [24, 3])
    cwb = small("cwb", [8, 3])
    alog = small("alog", [16])
    dtb = small("dtb", [16])
    onwc = small("onwc", [1])
    ln1g = small("ln1g", [8]); ln1b = small("ln1b", [8]); ln2g = small("ln2g", [8]); ln2b = small("ln2b", [8])
    bf1 = small("bf1", [32]); bf2 = small("bf2", [8])
    DMA(cT.ap.rearrange("p a b -> p (a b)"), cT_d, w=[cT.b])
    DMA(badaT.ap, badaT_d, w=[badaT.b])
    DMA(cwq.ap.rearrange("p a b -> p (a b)"), cwq_d, w=[cwq.b])
    DMA(cwb.ap.rearrange("p a b -> p (a b)"), cwb_d, w=[cwb.b])
    DMA(alog.ap, alog_d, w=[alog.b])
    DMA(dtb.ap, dtb_d, w=[dtb.b])
    DMA(onwc.ap, onw_d, w=[onwc.b])
    for t_, d_ in ((ln1g, ln1g_d), (ln1b, ln1b_d), (ln2g, ln2g_d), (ln2b, ln2b_d), (bf1, bf1_d), (bf2, bf2_d)):
        DMA(t_.ap, d_, w=[t_.b])
    ACT(onwc.ap, onwc.ap, AF.Copy, r=[onwc.b], w=[onwc.b], scale=0.5)
    nA = small("nA", [16])
    ACT(nA.ap, alog.ap, AF.Exp, r=[alog.b], w=[nA.b])
    TS("dve", nA.ap, nA.ap, -1.0, None, ALU.mult, r=[nA.b], w=[nA.b])
    modT = small("modT", [48, 3])
    s1c = small("s1c", [8, 3])
    hg1 = small("hg1", [8, 2])
    A1s = small("A1s", [8])
    A1b = small("A1b", [8, 2])
    U2s = small("U2s", [8, 2])
    U2b = small("U2b", [8, 2])
    wtp = Pool(AR, 6, [8, 128], BF16, "wt")
    mark_persist = AR.top

    cast_head, cast_rest = [], []
    cast_jobs = cast_head

    def mk_job(W, K0, K, c0, ncol, dst_ap, dst_buf):
        src = W.rearrange("(k p) c -> p k c", p=128)[:, K0:K0 + K, c0:c0 + ncol]
        dst = dst_ap.rearrange("p (k c) -> p k c", k=K)
        cast_jobs.append((src, dst, dst_buf))

    def in_col(c):
        return c * 128 if c < 24 else 3104 + (c - 24) * 128

    mk_job(win_d, 0, 8, 3072, 32, sc_bg_t, sb_bg)
    for h in range(NH):
        for c in (h, 8 + h, 16 + h, 24 + h):
            mk_job(win_d, 0, 8, in_col(c), 128, sc_in[c * 128:(c + 1) * 128, :], sb_in[c])
    cast_jobs = cast_rest
    for c in range(32, 72):
        mk_job(win_d, 0, 8, in_col(c), 128, sc_in[c * 128:(c + 1) * 128, :], sb_in[c])
    for W, sc, sb in ((wa_d, sc_a, sb_a), (wb_d, sc_b, sb_b), (wo_d, sc_o, sb_o)):
        for c in range(8):
            mk_job(W, 0, 8, c * 128, 128, sc[c * 128:(c + 1) * 128, :], sb[c])
    for c in range(32):
        mk_job(wf1_d, 0, 8, c * 128, 128, sc_f1[c * 128:(c + 1) * 128, :], sb_f1[c])
    for c in range(8):
        for kq in range(4):
            mk_job(wf2_d, kq * 8, 8, c * 128, 128, sc_f2[c * 128:(c + 1) * 128, kq * 1024:(kq + 1) * 1024], sb_f2[c])
    def emit_casts(lst, n):
        for _ in range(n):
            if not lst:
                return
            src, dst, dbuf = lst.pop(0)
            S.dma(lambda e, src=src, dst=dst: e.dma_start(out=dst, in_=src), reads=[], writes=[dbuf], eng="pool")

    def mod_phase():
        stg = Pool(AR, 3, [8, 128], F32, "mstg")
        sil = AR.alloc([8, 3], F32, "silc")
        th = AR.alloc([8, 3], F32, "silt")
        ACT(th.ap, cT.ap, AF.Tanh, r=[cT.b], w=[th.b], scale=0.5)
        STT("dve", sil.ap, th.ap, 1.0, cT.ap, ALU.add, ALU.mult, r=[th.b, cT.b], w=[sil.b])
        TS("dve", sil.ap, sil.ap, 0.5, None, ALU.mult, r=[sil.b], w=[sil.b])
        pb = banks[0]
        for fc in range(48):
            s = stg.get()
            DMA(s.ap, wada_d.rearrange("(k p) c -> p k c", p=128)[:, :, fc * 128:(fc + 1) * 128], w=[s.b])
            for k in range(8):
                MM(pb.ap[:, fc * 3:fc * 3 + 3], s.ap[:, k, :], sil.ap[:, k, :], start=(k == 0), stop=(k == 7),
                   r=[s.b, sil.b], w=[pb.b])
        TTop("dve", modT.ap, pb.ap[:, 0:144].rearrange("p (a b) -> p a b", a=48),
             badaT.ap.unsqueeze(2).to_broadcast([128, 48, 3]), ALU.add, r=[pb.b, badaT.b], w=[modT.b])
        m = modT.ap
        TS("dve", s1c.ap, m[:, 8:16, :], 1.0, None, ALU.add, r=[modT.b], w=[s1c.b])
        TS("dve", hg1.ap, m[:, 16:24, 0:2], 0.5, None, ALU.mult, r=[modT.b], w=[hg1.b])
        TS("dve", A1s.ap, ln1g.ap, ALPHA, None, ALU.mult, r=[ln1g.b], w=[A1s.b])
        tmp = AR.alloc([8, 2], F32, "mtmp")
        TTop("dve", tmp.ap, m[:, 40:48, 0:2], bf2.ap.unsqueeze(2).to_broadcast([128, 8, 2]), ALU.mult,
             r=[modT.b, bf2.b], w=[tmp.b])
        STT("dve", A1b.ap, ln1b.ap.unsqueeze(2).to_broadcast([128, 8, 2]), ALPHA, tmp.ap, ALU.mult, ALU.add,
            r=[ln1b.b, tmp.b], w=[A1b.b])
        tmp2 = AR.alloc([8, 2], F32, "mtmp2")
        TS("dve", tmp2.ap, m[:, 32:40, 0:2], 1.0, None, ALU.add, r=[modT.b], w=[tmp2.b])
        TTop("dve", U2s.ap, tmp2.ap, ln1g.ap.unsqueeze(2).to_broadcast([128, 8, 2]), ALU.mult,
             r=[tmp2.b, ln1g.b], w=[U2s.b])
        TTop("dve", U2b.ap, tmp2.ap, ln1b.ap.unsqueeze(2).to_broadcast([128, 8, 2]), ALU.mult,
             r=[tmp2.b, ln1b.b], w=[U2b.b])
        TTop("dve", U2b.ap, U2b.ap, m[:, 24:32, 0:2], ALU.add, r=[U2b.b, modT.b], w=[U2b.b])

    m0 = AR.top
    emit_casts(cast_head, 1 + 8)
    mod_phase()
    dbg("modT", modT, [128, 144])
    S.barrier()
    AR.top = m0

    def batch_program(b):
        mb = AR.top
        uT = [AR.alloc([8, ln], BF16, "uT%d" % ti) for ti, (st, ln) in enumerate(TILES)]
        og_scr = nc.dram_tensor("og_scr%d" % b, [NH, 128, SEQ], BF16, kind="Internal").ap()
        og_buf = [[Buf("ogs%d_%d_%d" % (b, h, t)) for t in range(4)] for h in range(NH)]
        m_phase = AR.top

        def utile(tok):
            for ti, (st, ln) in enumerate(TILES):
                if st <= tok < st + ln:
                    return ti, tok - st
            raise ValueError

        xs = Pool(AR, 6, [1024], F32, "xs")
        for ti, (st, ln) in enumerate(TILES):
            nb = ln // 128
            blks = []
            for j in range(nb):
                s = xs.get()
                if ti == 0:
                    src = ctx_d[b, j * 128:(j + 1) * 128, :]
                else:
                    src = x_d[b, st - CTXL + j * 128: st - CTXL + (j + 1) * 128, :]
                DMA(s.ap, src, w=[s.b])
                blks.append(s)
            jm = 2 if ti == 0 else b
            for fc in range(8):
                pb = nextbank()
                for j in range(nb):
                    MM(pb.ap[:, j * 128:(j + 1) * 128], blks[j].ap[:, fc * 128:(fc + 1) * 128], C(IDENT),
                       r=[blks[j].b, cst.b], w=[pb.b])
                ACT(uT[ti].ap[:, fc, :], pb.ap[:, 0:ln], AF.Identity, r=[pb.b, s1c.b, modT.b], w=[uT[ti].b],
                    bias=modT.ap[:, fc, jm:jm + 1], scale=s1c.ap[:, fc, jm:jm + 1])
        if b == 0:
            dbg("uT1", uT[1], [128, 8, 512], BF16)
        S.barrier()
        AR.top = m_phase
        if stage < 1:
            AR.top = mb
            return

        def pcol(name):
            return AR.alloc([NBLK, 16], F32, name)

        bt = pcol("bt"); hbt = pcol("hbt"); gg = pcol("gg"); Gc = pcol("Gc")
        n2eG = pcol("n2eG"); kdec = pcol("kdec"); eGl = pcol("eGl")
        m_A = AR.top
        wbg = AR.alloc([8, 32], BF16, "wbg")
        DMA(wbg.ap.rearrange("p a b -> p (a b)"), sc_bg_t, r=[sb_bg], w=[wbg.b])
        tb = AR.alloc([NBLK, 16], F32, "pc_tb")
        xg = AR.alloc([NBLK, 16], F32, "pc_xg")
        for half in range(2):
            pb = nextbank()
            for j in range(9):
                blk = half * 9 + j
                ti, off = utile(blk * 128)
                for k in range(8):
                    MM(pb.ap[:, j * 32:(j + 1) * 32], uT[ti].ap[:, k, off:off + 128], wbg.ap[:, k, :],
                       start=(k == 0), stop=(k == 7), r=[uT[ti].b, wbg.b], w=[pb.b])
            pv = pb.ap[:, 0:288].rearrange("p (a b) -> p a b", a=9)
            sl = slice(half * 9, half * 9 + 9)
            ACT(tb.ap[:, sl, :], pv[:, :, 0:16], AF.Tanh, r=[pb.b], w=[tb.b], scale=0.5)
            TTop("dve", xg.ap[:, sl, :], pv[:, :, 16:32], dtb.ap.unsqueeze(1).to_broadcast([128, 9, 16]), ALU.add,
                 r=[pb.b, dtb.b], w=[xg.b])
        TS("dve", bt.ap, tb.ap, 0.5, 0.5, ALU.mult, ALU.add, r=[tb.b], w=[bt.b])
        TS("dve", hbt.ap, tb.ap, 0.25, 0.25, ALU.mult, ALU.add, r=[tb.b], w=[hbt.b])
        TS("dve", xg.ap, xg.ap, 30.0, None, ALU.min, r=[xg.b], w=[xg.b])
        ACT(xg.ap, xg.ap, AF.Exp, r=[xg.b], w=[xg.b])
        ACT(xg.ap, xg.ap, AF.Ln, r=[xg.b, kc.b], w=[xg.b], bias=KC("one"))
        TTop("dve", gg.ap, xg.ap, nA.ap.unsqueeze(1).to_broadcast([128, NBLK, 16]), ALU.mult, r=[xg.b, nA.b], w=[gg.b])
        pb = nextbank()
        pv = pb.ap[:, 0:288].rearrange("p (a b) -> p a b", a=NBLK)
        for d in range(2):
            MM(pv[:, :, d * 8:(d + 1) * 8], C(CUM0 + d), gg.ap[:, :, d * 8:(d + 1) * 8], r=[cst.b, gg.b], w=[pb.b])
        CP("act", Gc.ap, pv, r=[pb.b], w=[Gc.b])
        pb2 = nextbank()
        pv2 = pb2.ap[:, 0:288].rearrange("p (a b) -> p a b", a=NBLK)
        MM(pb2.ap[:, 0:288], C(ONES), gg.ap.rearrange("p a b -> p (a b)"), r=[cst.b, gg.b], w=[pb2.b])
        ACT(eGl.ap, pv2, AF.Exp, r=[pb2.b], w=[eGl.b])
        TTop("dve", kdec.ap, pv2, Gc.ap, ALU.subtract, r=[pb2.b, Gc.b], w=[kdec.b])
        ACT(kdec.ap, kdec.ap, AF.Exp, r=[kdec.b], w=[kdec.b])
        ACT(n2eG.ap, Gc.ap, AF.Exp, r=[Gc.b], w=[n2eG.b])
        TS("dve", n2eG.ap, n2eG.ap, -2.0, None, ALU.mult, r=[n2eG.b], w=[n2eG.b])
        if b == 0:
            dbg("bt", bt, [128, NBLK, 16]); dbg("gg", gg, [128, NBLK, 16]); dbg("Gc", Gc, [128, NBLK, 16])
            dbg("kdec", kdec, [128, NBLK, 16]); dbg("eGl", eGl, [128, NBLK, 16])
        AR.top = m_A

        qTs = [AR.alloc([NTOK], BF16, "qT%d" % i) for i in range(2)]
        kTs = [AR.alloc([NTOK], BF16, "kT%d" % i) for i in range(2)]
        k_tms = [AR.alloc([NBLK, 128], BF16, "k_tm%d" % i) for i in range(2)]
        v_tms = [AR.alloc([NBLK, 128], BF16, "v_tm%d" % i) for i in range(2)]
        zs2Ts = [AR.alloc([SEQ], BF16, "zs2T%d" % i) for i in range(2)]
        WSn = ("Q", "AT", "Ub", "nWT", "Bb")
        WS = [{n_: AR.alloc([12, 128], BF16, "ws%d_%s" % (p_, n_)) for n_ in WSn} for p_ in range(2)]
        o_acc = AR.alloc([16, 128], F32, "o_acc")
        tf = Pool(AR, 3, [512], F32, "tf")
        tb16 = Pool(AR, 3, [512], BF16, "tb")
        chain = [[AR.alloc([4, 128], BF16, "ch%d_%d" % (g_, i)) for i in range(8)] for g_ in range(3)]
        S32 = [[AR.alloc([128], F32, "S32_%d_%d" % (d, i)) for i in range(2)] for d in range(2)]
        Sb = [[AR.alloc([128], BF16, "Sb_%d_%d" % (d, i)) for i in range(2)] for d in range(2)]
        e_ct = AR.alloc([512], F32, "e_ct")
        e_s2 = AR.alloc([512], F32, "e_s2")
        e_dg = AR.alloc([512], F32, "e_dg")
        e_sq = AR.alloc([512], BF16, "e_sq")
        e_vt = e_sq
        e_rs = AR.alloc([8], F32, "e_rs")
        on_all = AR.alloc([16, 128], BF16, "on_all")
        dec = [(AR.alloc([512], F32, "nD%d" % g_), AR.alloc([512], F32, "DT%d" % g_)) for g_ in range(3)]
        ssq = AR.alloc([16], F32, "ssq")
        rstd = AR.alloc([16], F32, "rstd")

        print("phase A top:", AR.top, "cap", AR.cap)

        def early_gen(h):
            qT, kT, k_tm, v_tm, zs2T = qTs[h % 2], kTs[h % 2], k_tms[h % 2], v_tms[h % 2], zs2Ts[h % 2]
            wts = []
            for c in (h, 8 + h, 16 + h, 24 + h):
                wt = wtp.get()
                DMA(wt.ap.rearrange("p a b -> p (a b)"), sc_in[c * 128:(c + 1) * 128, :], r=[sb_in[c]], w=[wt.b])
                wts.append(wt)
            yield
            for ti, (st, ln) in enumerate(TILES):
                R = 1 if ti == 0 else ln // 64
                L = ln // R
                nb = ln // 128
                for wi, name in enumerate(("q", "k", "v")):
                    pb = nextbank()
                    for k in range(8):
                        MM(pb.ap[:, 0:ln], wts[wi].ap[:, k, :], uT[ti].ap[:, k, :], start=(k == 0), stop=(k == 7),
                           r=[wts[wi].b, uT[ti].b], w=[pb.b])
                    fcw = wi * 8 + h
                    ct = e_ct
                    ACT(ct.ap[:, 0:ln], pb.ap[:, 0:ln], AF.Copy, r=[pb.b, cwq.b], w=[ct.b], scale=cwq.ap[:, fcw, 1:2])
                    pv = pb.ap[:, 0:ln].rearrange("p (r l) -> p r l", r=R)
                    cv = ct.ap[:, 0:ln].rearrange("p (r l) -> p r l", r=R)
                    STT("dve", cv[:, :, 1:L], pv[:, :, 0:L - 1], cwq.ap[:, fcw, 0:1], cv[:, :, 1:L], ALU.mult, ALU.add,
                        r=[pb.b, cwq.b, ct.b], w=[ct.b])
                    STT("dve", cv[:, :, 0:L - 1], pv[:, :, 1:L], cwq.ap[:, fcw, 2:3], cv[:, :, 0:L - 1], ALU.mult, ALU.add,
                        r=[pb.b, cwq.b, ct.b], w=[ct.b])
                    s2 = e_s2
                    ACT(s2.ap[:, 0:ln], ct.ap[:, 0:ln], AF.Tanh, r=[ct.b], w=[s2.b], scale=0.5)
                    if name == "v":
                        STT("dve", e_vt.ap[:, 0:ln], s2.ap[:, 0:ln], 1.0, ct.ap[:, 0:ln], ALU.add, ALU.mult,
                            r=[s2.b, ct.b], w=[e_vt.b])
                        yield
                        pbv = nextbank()
                        for j in range(nb):
                            MM(pbv.ap[:, j * 128:(j + 1) * 128], e_vt.ap[:, j * 128:(j + 1) * 128], IDB,
                               r=[e_vt.b, cstb.b], w=[pbv.b])
                        CP("act", v_tm.ap[:, st // 128: st // 128 + nb, :].rearrange("p a b -> p (a b)"),
                           pbv.ap[:, 0:ln], r=[pbv.b], w=[v_tm.b])
                        yield
                        continue
                    STT("dve", s2.ap[:, 0:ln], s2.ap[:, 0:ln], 1.0, ct.ap[:, 0:ln], ALU.add, ALU.mult,
                        r=[s2.b, ct.b], w=[s2.b])
                    TTop("pool", e_sq.ap[:, 0:ln], s2.ap[:, 0:ln], s2.ap[:, 0:ln], ALU.mult, r=[s2.b], w=[e_sq.b])
                    yield
                    pb2 = nextbank()
                    for j in range(nb):
                        MM(pb2.ap[:, j:j + 1], e_sq.ap[:, j * 128:(j + 1) * 128], ONB[:, 0:1], r=[e_sq.b, cstb.b], w=[pb2.b])
                    rs = e_rs
                    if name == "q":
                        ACT(rs.ap[:, 0:nb], pb2.ap[:, 0:nb], AF.Identity, r=[pb2.b, kc.b], w=[rs.b],
                            bias=KC("eps4q"), scale=128.0)
                    else:
                        ACT(rs.ap[:, 0:nb], pb2.ap[:, 0:nb], AF.Identity, r=[pb2.b, kc.b], w=[rs.b],
                            bias=KC("eps4"), scale=1.0)
                    TTop("pool", rs.ap[:, 0:nb], rs.ap[:, 0:nb], KC("mhalf").to_broadcast([128, nb]), ALU.pow,
                         r=[rs.b, kc.b], w=[rs.b])
                    dg = e_dg
                    dgv = dg.ap[:, 0:ln].rearrange("p (a b) -> p a b", a=nb)
                    TTop("pool", dgv, C(IDENT).unsqueeze(1).to_broadcast([128, nb, 128]),
                         rs.ap[:, 0:nb].unsqueeze(2).to_broadcast([128, nb, 128]), ALU.mult,
                         r=[cst.b, rs.b], w=[dg.b])
                    yield
                    pb3 = nextbank()
                    for j in range(nb):
                        MM(pb3.ap[:, j * 128:(j + 1) * 128], C(ONES), dgv[:, j, :], r=[cst.b, dg.b], w=[pb3.b])
                    dst = qT if name == "q" else kT
                    TTop("dve", dst.ap[:, st:st + ln], s2.ap[:, 0:ln], pb3.ap[:, 0:ln], ALU.mult,
                         r=[s2.b, pb3.b], w=[dst.b])
                    yield
                    if name == "k":
                        pbt = nextbank()
                        for j in range(nb):
                            MM(pbt.ap[:, j * 128:(j + 1) * 128], kT.ap[:, st + j * 128: st + (j + 1) * 128], IDB,
                               r=[kT.b, cstb.b], w=[pbt.b])
                        CP("act", k_tm.ap[:, st // 128: st // 128 + nb, :].rearrange("p a b -> p (a b)"),
                           pbt.ap[:, 0:ln], r=[pbt.b], w=[k_tm.b])
                        yield
            for ti, (st, ln) in enumerate(TILES):
                if ti == 0:
                    continue
                pb = nextbank()
                for k in range(8):
                    MM(pb.ap[:, 0:ln], wts[3].ap[:, k, :], uT[ti].ap[:, k, :], start=(k == 0), stop=(k == 7),
                       r=[wts[3].b, uT[ti].b], w=[pb.b])
                ACT(e_ct.ap[:, 0:ln], pb.ap[:, 0:ln], AF.Tanh, r=[pb.b], w=[e_ct.b], scale=0.5)
                STT("dve", zs2T.ap[:, st - CTXL: st - CTXL + ln], e_ct.ap[:, 0:ln], 1.0, pb.ap[:, 0:ln],
                    ALU.add, ALU.mult, r=[e_ct.b, pb.b], w=[zs2T.b])
                yield
            if b == 0 and h == 0:
                dbg("qT", qT, [128, NTOK], BF16); dbg("kT", kT, [128, NTOK], BF16)
                dbg("v_tm", v_tm, [128, NBLK, 128], BF16); dbg("zs2T", zs2T, [128, SEQ], BF16)

        def adv(gen, n):
            if gen is None:
                return
            for _ in range(n):
                try:
                    next(gen)
                except StopIteration:
                    return

        def wave_units(w):
            fw = [(0, 6 * w + s_) for s_ in range(6)]
            bw = sorted([(1, 6 * w + s_) for s_ in range(6)], key=lambda t_: ORDER[1][t_[1]])
            return fw + bw

        def head_ctx(h):
            qT, kT, k_tm, v_tm, zs2T = qTs[h % 2], kTs[h % 2], k_tms[h % 2], v_tms[h % 2], zs2Ts[h % 2]

            def init():
                for d in range(2):
                    S.op("pool", lambda e, d=d: e.memset(S32[d][0].ap, 0.0), (), [S32[d][0].b])
                    S.op("pool", lambda e, d=d: e.memset(Sb[d][0].ap, 0.0), (), [Sb[d][0].b])

            def col(t_, blk, d):
                return t_.ap[:, blk, d * 8 + h: d * 8 + h + 1]

            def cols(t_, blk0, n, d):
                return t_.ap[:, blk0:blk0 + n, d * 8 + h]

            def runs(us):
                out = []
                for j, (d, s_) in enumerate(us):
                    blk = ORDER[d][s_]
                    if out and out[-1][2] == d and out[-1][3] + out[-1][1] == blk:
                        out[-1][1] += 1
                    else:
                        out.append([j, 1, d, blk])
                return out

            def wave_prep(w):
                units = wave_units(w)
                f2 = lambda t_: t_.ap.rearrange("p a b -> p (a b)")
                ws = WS[(3 * h + w) % 2]
                G = []
                for gq in range(3):
                    ch = chain[gq]
                    G.append({"us": units[gq * 4: gq * 4 + 4], "gq": gq, "Mt": ch[0], "R": [ch[1], ch[2]],
                              "P": [ch[3], ch[4]], "Y": [ch[5], ch[6]], "Moff": ch[7], "rp": 0, "y": 0})
                gcs = []
                for g in G:
                    gcum = tf.get()
                    gcv = gcum.ap.rearrange("p (a b) -> p a b", a=4)
                    for (j0, n, d, blk0) in runs(g["us"]):
                        TTop("pool", gcv[:, j0:j0 + n, :], C(CUM0 + d).unsqueeze(1).to_broadcast([128, n, 128]),
                             cols(gg, blk0, n, d).unsqueeze(2).to_broadcast([128, n, 128]), ALU.mult,
                             r=[cst.b, gg.b], w=[gcum.b])
                    gcs.append((gcum, gcv))
                yield
                for g, (gcum, gcv) in zip(G, gcs):
                    gq = g["gq"]
                    pg = nextbank()
                    for j in range(4):
                        MM(pg.ap[:, j * 128:(j + 1) * 128], C(ONES), gcv[:, j, :], r=[cst.b, gcum.b], w=[pg.b])
                    eGr = gcum
                    ACT(eGr.ap, pg.ap, AF.Exp, r=[pg.b], w=[eGr.b])
                    nD, DT = dec[gq]
                    pg3 = pg.ap.rearrange("p (a b) -> p a b", a=4)
                    nD3 = nD.ap.rearrange("p (a b) -> p a b", a=4)
                    DT3 = DT.ap.rearrange("p (a b) -> p a b", a=4)
                    for (j0, n, d, blk0) in runs(g["us"]):
                        gcb = cols(Gc, blk0, n, d).unsqueeze(2).to_broadcast([128, n, 128])
                        TTop("dve", nD3[:, j0:j0 + n, :], pg3[:, j0:j0 + n, :], gcb, ALU.subtract,
                             r=[pg.b, Gc.b], w=[nD.b])
                        TTop("pool", DT3[:, j0:j0 + n, :], nD3[:, j0:j0 + n, :],
                             C(MTN0 + d).unsqueeze(1).to_broadcast([128, n, 128]), ALU.add, r=[nD.b, cst.b], w=[DT.b])
                        TTop("pool", nD3[:, j0:j0 + n, :], nD3[:, j0:j0 + n, :],
                             C(MPOS0 + d).unsqueeze(1).to_broadcast([128, n, 128]), ALU.add, r=[nD.b, cst.b], w=[nD.b])
                    for (j0, n, d, blk0) in runs(g["us"]):
                        TTop("pool", ws["Q"].ap[:, gq * 4 + j0:gq * 4 + j0 + n, :].rearrange("p a b -> p (a b)"),
                             qT.ap[:, blk0 * 128:(blk0 + n) * 128], eGr.ap[:, j0 * 128:(j0 + n) * 128],
                             ALU.mult, r=[qT.b, eGr.b], w=[ws["Q"].b])
                yield
                for g in G:
                    gq = g["gq"]
                    us = g["us"]
                    nD, DT = dec[gq]
                    ACT(nD.ap, nD.ap, AF.Exp, r=[nD.b], w=[nD.b], scale=-1.0)
                    ACT(DT.ap, DT.ap, AF.Exp, r=[DT.b], w=[DT.b])
                    pkk = nextbank()
                    pkq = nextbank()
                    for j, (d, s_) in enumerate(us):
                        blk = ORDER[d][s_]
                        ks = kT.ap[:, blk * 128:(blk + 1) * 128]
                        MM(pkk.ap[:, j * 128:(j + 1) * 128], ks, ks, r=[kT.b], w=[pkk.b])
                    for j, (d, s_) in enumerate(us):
                        blk = ORDER[d][s_]
                        ks = kT.ap[:, blk * 128:(blk + 1) * 128]
                        MM(pkq.ap[:, j * 128:(j + 1) * 128], ks, qT.ap[:, blk * 128:(blk + 1) * 128],
                           r=[kT.b, qT.b], w=[pkq.b])
                    Mt = g["Mt"]
                    pk3 = pkk.ap.rearrange("p (a b) -> p a b", a=4)
                    nD3 = nD.ap.rearrange("p (a b) -> p a b", a=4)
                    for (j0, n, d, blk0) in runs(us):
                        TTop("dve", Mt.ap[:, j0:j0 + n, :], pk3[:, j0:j0 + n, :],
                             cols(bt, blk0, n, d).unsqueeze(2).to_broadcast([128, n, 128]), ALU.mult,
                             r=[pkk.b, bt.b], w=[Mt.b])
                    TTop("pool", Mt.ap, Mt.ap, nD3, ALU.mult, r=[Mt.b, nD.b], w=[Mt.b])
                    TTop("dve", ws["AT"].ap[:, gq * 4:gq * 4 + 4, :].rearrange("p a b -> p (a b)"), pkq.ap, DT.ap, ALU.mult,
                         r=[pkq.b, DT.b], w=[ws["AT"].b])
                yield
                for g in G:
                    Mt = g["Mt"]
                    TTop("pool", g["R"][0].ap, Mt.ap, BDB.unsqueeze(1).to_broadcast([128, 4, 128]), ALU.mult,
                         r=[Mt.b, cstb.b], w=[g["R"][0].b])
                    TTop("pool", g["Moff"].ap, Mt.ap, NBDB.unsqueeze(1).to_broadcast([128, 4, 128]), ALU.mult,
                         r=[Mt.b, cstb.b], w=[g["Moff"].b])
                yield
                for g in G:
                    R_, P_, Y_ = g["R"], g["P"], g["Y"]
                    pb = nextbank()
                    for j in range(4):
                        MM(pb.ap[:, j * 128:(j + 1) * 128], R_[0].ap[:, j, :], IDB, r=[R_[0].b, cstb.b], w=[pb.b])
                    CP("act", f2(P_[0]), pb.ap, r=[pb.b], w=[P_[0].b])
                    TTop("dve", Y_[0].ap, IDB.unsqueeze(1).to_broadcast([128, 4, 128]),
                         pb.ap.rearrange("p (a b) -> p a b", a=4), ALU.subtract, r=[cstb.b, pb.b], w=[Y_[0].b])
                yield
                for seg in range(6):
                    for g in G:
                        R_, P_, Y_ = g["R"], g["P"], g["Y"]
                        rp, y = g["rp"], g["y"]
                        if seg >= 1:
                            pc = nextbank()
                            for j in range(4):
                                MM(pc.ap[:, j * 128:(j + 1) * 128], R_[rp].ap[:, j, :], Y_[y].ap[:, j, :],
                                   r=[R_[rp].b, Y_[y].b], w=[pc.b])
                            TTop("dve", f2(Y_[1 - y]), f2(Y_[y]), pc.ap, ALU.add, r=[Y_[y].b, pc.b], w=[Y_[1 - y].b])
                            g["y"] = 1 - y
                        if seg <= 4:
                            pa = nextbank()
                            for j in range(4):
                                MM(pa.ap[:, j * 128:(j + 1) * 128], P_[rp].ap[:, j, :], R_[rp].ap[:, j, :],
                                   r=[P_[rp].b, R_[rp].b], w=[pa.b])
                            CP("act", f2(R_[1 - rp]), pa.ap, r=[pa.b], w=[R_[1 - rp].b])
                            if seg <= 3:
                                pbk = nextbank()
                                for j in range(4):
                                    MM(pbk.ap[:, j * 128:(j + 1) * 128], R_[rp].ap[:, j, :], P_[rp].ap[:, j, :],
                                       r=[P_[rp].b, R_[rp].b], w=[pbk.b])
                                CP("act", f2(P_[1 - rp]), pbk.ap, r=[pbk.b], w=[P_[1 - rp].b])
                            g["rp"] = 1 - rp
                    yield
                for g in G:
                    rp, y = g["rp"], g["y"]
                    g["Yf"], g["YT"], g["Z1"] = g["Y"][y], g["P"][1], g["P"][0]
                    g["ek"], g["kd"], g["Wt"] = g["R"][1 - rp], g["Y"][1 - y], g["R"][rp]
                    Yf, YT, Z1, Moff = g["Yf"], g["YT"], g["Z1"], g["Moff"]
                    pb = nextbank()
                    for j in range(4):
                        MM(pb.ap[:, j * 128:(j + 1) * 128], Yf.ap[:, j, :], IDB, r=[Yf.b, cstb.b], w=[pb.b])
                    CP("act", f2(YT), pb.ap, r=[pb.b], w=[YT.b])
                    pz = nextbank()
                    for j in range(4):
                        MM(pz.ap[:, j * 128:(j + 1) * 128], Moff.ap[:, j, :], Yf.ap[:, j, :], r=[Moff.b, Yf.b], w=[pz.b])
                    CP("act", f2(Z1), pz.ap, r=[pz.b], w=[Z1.b])
                    for (j0, n, d, blk0) in runs(g["us"]):
                        TTop("pool", g["ek"].ap[:, j0:j0 + n, :], k_tm.ap[:, blk0:blk0 + n, :],
                             cols(n2eG, blk0, n, d).unsqueeze(2).to_broadcast([128, n, 128]),
                             ALU.mult, r=[k_tm.b, n2eG.b], w=[g["ek"].b])
                        TTop("pool", g["kd"].ap[:, j0:j0 + n, :], k_tm.ap[:, blk0:blk0 + n, :],
                             cols(kdec, blk0, n, d).unsqueeze(2).to_broadcast([128, n, 128]),
                             ALU.mult, r=[k_tm.b, kdec.b], w=[g["kd"].b])
                yield
                for g in G:
                    Yf, YT, Z1 = g["Yf"], g["YT"], g["Z1"]
                    pz2 = nextbank()
                    for j in range(4):
                        MM(pz2.ap[:, j * 128:(j + 1) * 128], YT.ap[:, j, :], Z1.ap[:, j, :], r=[YT.b, Z1.b], w=[pz2.b])
                    tmpT = tf.get()
                    TTop("dve", tmpT.ap, f2(Yf), pz2.ap, ALU.subtract, r=[Yf.b, pz2.b], w=[tmpT.b])
                    TTs = g["Mt"]
                    tmv = tmpT.ap.rearrange("p (a b) -> p a b", a=4)
                    for (j0, n, d, blk0) in runs(g["us"]):
                        TTop("pool", TTs.ap[:, j0:j0 + n, :], tmv[:, j0:j0 + n, :],
                             cols(hbt, blk0, n, d).unsqueeze(2).to_broadcast([128, n, 128]), ALU.mult,
                             r=[tmpT.b, hbt.b], w=[TTs.b])
                yield
                for g in G:
                    gq = g["gq"]
                    TTs, ek, Wt = g["Mt"], g["ek"], g["Wt"]
                    usl = slice(gq * 4, gq * 4 + 4)
                    pU = nextbank()
                    pW = nextbank()
                    for j, (d, s_) in enumerate(g["us"]):
                        blk = ORDER[d][s_]
                        MM(pU.ap[:, j * 128:(j + 1) * 128], TTs.ap[:, j, :], v_tm.ap[:, blk, :], r=[TTs.b, v_tm.b], w=[pU.b])
                    for j in range(4):
                        MM(pW.ap[:, j * 128:(j + 1) * 128], TTs.ap[:, j, :], ek.ap[:, j, :], r=[TTs.b, ek.b], w=[pW.b])
                    CP("act", ws["Ub"].ap[:, usl, :].rearrange("p a b -> p (a b)"), pU.ap, r=[pU.b], w=[ws["Ub"].b])
                    CP("dve", f2(Wt), pW.ap, r=[pW.b], w=[Wt.b])
                yield
                for g in G:
                    gq = g["gq"]
                    kd, Wt = g["kd"], g["Wt"]
                    usl = slice(gq * 4, gq * 4 + 4)
                    pWT = nextbank()
                    pB = nextbank()
                    pQ = nextbank()
                    for j in range(4):
                        MM(pWT.ap[:, j * 128:(j + 1) * 128], Wt.ap[:, j, :], kd.ap[:, j, :], r=[Wt.b, kd.b], w=[pWT.b])
                    for j in range(4):
                        MM(pB.ap[:, j * 128:(j + 1) * 128], kd.ap[:, j, :], ws["Ub"].ap[:, gq * 4 + j, :],
                           r=[kd.b, ws["Ub"].b], w=[pB.b])
                    for j in range(4):
                        MM(pQ.ap[:, j * 128:(j + 1) * 128], Wt.ap[:, j, :], ws["AT"].ap[:, gq * 4 + j, :],
                           r=[Wt.b, ws["AT"].b], w=[pQ.b])
                    CP("act", ws["nWT"].ap[:, usl, :].rearrange("p a b -> p (a b)"), pWT.ap, r=[pWT.b], w=[ws["nWT"].b])
                    CP("act", ws["Bb"].ap[:, usl, :].rearrange("p a b -> p (a b)"), pB.ap, r=[pB.b], w=[ws["Bb"].b])
                    qv = ws["Q"].ap[:, usl, :].rearrange("p a b -> p (a b)")
                    TTop("dve", qv, qv, pQ.ap, ALU.add, r=[ws["Q"].b, pQ.b], w=[ws["Q"].b])
                yield

            def scan_step(d, s_):
                blk = ORDER[d][s_]
                ws = WS[(3 * h + s_ // 6) % 2]
                u = wave_units(s_ // 6).index((d, s_))
                cur = s_ % 2
                nxt = 1 - cur
                pst_ = banks[d]
                MM(pst_.ap[:, 0:128], ws["nWT"].ap[:, u, :], Sb[d][cur].ap, start=True, stop=False,
                   r=[ws["nWT"].b, Sb[d][cur].b], w=[pst_.b])
                MM(pst_.ap[:, 0:128], IDB, ws["Bb"].ap[:, u, :], start=False, stop=True,
                   r=[cstb.b, ws["Bb"].b], w=[pst_.b])
                STT("dve", Sb[d][nxt].ap, S32[d][cur].ap, col(eGl, blk, d), pst_.ap[:, 0:128], ALU.mult, ALU.add,
                    r=[S32[d][cur].b, eGl.b, pst_.b], w=[Sb[d][nxt].b])
                STT("dve", S32[d][nxt].ap, S32[d][cur].ap, col(eGl, blk, d), pst_.ap[:, 0:128], ALU.mult, ALU.add,
                    r=[S32[d][cur].b, eGl.b, pst_.b], w=[S32[d][nxt].b])
                if blk >= 2:
                    li = blk - 2
                    po = banks[2]
                    MM(po.ap[:, d * 128:(d + 1) * 128], ws["Q"].ap[:, u, :], Sb[d][cur].ap, start=True, stop=False,
                       r=[ws["Q"].b, Sb[d][cur].b], w=[po.b])
                    MM(po.ap[:, d * 128:(d + 1) * 128], ws["AT"].ap[:, u, :], ws["Ub"].ap[:, u, :], start=False, stop=True,
                       r=[ws["AT"].b, ws["Ub"].b], w=[po.b])
                    first = (d == 0 and li < 8) or (d == 1 and li >= 8)
                    if first:
                        CP("act", o_acc.ap[:, li, :], po.ap[:, d * 128:(d + 1) * 128], r=[po.b], w=[o_acc.b])
                    else:
                        TTop("dve", o_acc.ap[:, li, :], o_acc.ap[:, li, :], po.ap[:, d * 128:(d + 1) * 128], ALU.add,
                             r=[po.b, o_acc.b], w=[o_acc.b])

            def drain(gen):
                for _ in gen:
                    pass

            def finish_gen():
                S.op("pool", lambda e: e.memset(ssq.ap, 0.0), (), [ssq.b])
                for li in range(16):
                    junk = tb16.get()
                    S.op("act", lambda e, li=li, junk=junk: e.activation(out=junk.ap[:, 0:128], in_=o_acc.ap[:, li, :],
                                                                          func=AF.Square, accum_out=ssq.ap[:, li:li + 1]),
                         [o_acc.b], [junk.b, ssq.b])
                ACT(e_rs.ap[:, 4:8], kc.ap[:, 0:4], AF.Copy, r=[kc.b, ssq.b], w=[e_rs.b, ssq.b])
                ACT(rstd.ap, ssq.ap, AF.Identity, r=[ssq.b, kc.b], w=[rstd.b], bias=KC("rmseps"), scale=1.0 / 128.0)
                TTop("pool", rstd.ap, rstd.ap, KC("mhalf").to_broadcast([128, 16]), ALU.pow, r=[rstd.b, kc.b], w=[rstd.b])
                for li in range(16):
                    ACT(on_all.ap[:, li, :], o_acc.ap[:, li, :], AF.Copy, r=[o_acc.b, rstd.b], w=[on_all.b],
                        scale=rstd.ap[:, li:li + 1])
                if b == 0 and h == 0:
                    dbg("o_acc", o_acc, [128, 16, 128])
                yield
                for g0 in range(0, 16, 4):
                    pb = nextbank()
                    for j in range(4):
                        MM(pb.ap[:, j * 128:(j + 1) * 128], on_all.ap[:, g0 + j, :], IDB, r=[on_all.b, cstb.b], w=[pb.b])
                    ti = g0 // 4
                    ogt = tb16.get()
                    STT("dve", ogt.ap, pb.ap, onwc.ap[:, 0:1], zs2T.ap[:, ti * 512:(ti + 1) * 512], ALU.mult, ALU.mult,
                        r=[pb.b, onwc.b, zs2T.b], w=[ogt.b])
                    DMA(og_scr[h, :, ti * 512:(ti + 1) * 512], ogt.ap, r=[ogt.b], w=[og_buf[h][ti]])
                    if b == 0 and h == 0:
                        dbg("og0_%d" % ti, ogt, [128, 512], BF16)
                    yield

            return {"prep": wave_prep, "scan": scan_step, "fin": finish_gen, "init": init}

        nheads = NH if stage >= 4 else 1
        en = early_gen(0)
        for _ in en:
            pass
        if stage >= 2:
            ctx = [head_ctx(h) for h in range(nheads)]
            seq = [(h, w) for h in range(nheads) for w in range(3)]
            st = {"en": early_gen(1) if nheads > 1 else None, "fin": None}

            def side(n=1):
                for _ in range(n):
                    if st["fin"] is not None:
                        try:
                            next(st["fin"])
                        except StopIteration:
                            st["fin"] = None
                    elif st["en"] is not None:
                        try:
                            next(st["en"])
                        except StopIteration:
                            st["en"] = None
                    if b == 0:
                        if cast_head:
                            emit_casts(cast_head, 1)
                        else:
                            emit_casts(cast_rest, 1)

            rr_lo[0] = 3
            cur = ctx[0]["prep"](0)
            for _ in cur:
                side()
            for i, (h, w) in enumerate(seq):
                if w == 0:
                    ctx[h]["init"]()
                nxt = None
                if i + 1 < len(seq):
                    h2, w2 = seq[i + 1]
                    if w2 == 0:
                        while st["fin"] is not None or st["en"] is not None:
                            side()
                        if h + 2 < nheads:
                            st["en"] = early_gen(h + 2)
                    nxt = ctx[h2]["prep"](w2)
                for s_ in range(6 * w, 6 * w + 6):
                    for d in range(2):
                        ctx[h]["scan"](d, s_)
                        side()
                        adv(nxt, 1)
                        side()
                        if (2 * s_ + d) % 3 == 2:
                            adv(nxt, 1)
                if nxt is not None:
                    for _ in nxt:
                        side()
                if w == 2:
                    st["fin"] = ctx[h]["fin"]()
                    next(st["fin"])
            while st["fin"] is not None or st["en"] is not None:
                side()
            rr_lo[0] = 0
        emit_casts(cast_head, 100)
        emit_casts(cast_rest, 1000)
        S.barrier()
        AR.top = m_phase
        if stage < 4:
            AR.top = mb
            return

        ymT = [[AR.alloc([512], BF16, "ym%d_%d" % (fc, t)) for t in range(4)] for fc in range(8)]
        m_B = AR.top
        pT = [[AR.alloc([512], BF16, "p%d_%d" % (fc, t)) for t in range(4)] for fc in range(8)]
        tf = Pool(AR, 8, [512], F32, "tfB")

        def wload(sc, sbufs, c):
            wt = wtp.get()
            DMA(wt.ap.rearrange("p a b -> p (a b)"), sc[c * 128:(c + 1) * 128, :], r=[sbufs[c]], w=[wt.b])
            return wt

        for fc in range(8):
            wxb = wload(sc_in, sb_in, 32 + fc)
            wbgt = wload(sc_in, sb_in, 40 + fc)
            wcg = wload(sc_in, sb_in, 48 + fc)
            for t in range(4):
                ti = t + 1
                pbs = []
                for wt in (wxb, wbgt, wcg):
                    pb = nextbank()
                    for k in range(8):
                        MM(pb.ap, wt.ap[:, k, :], uT[ti].ap[:, k, :], start=(k == 0), stop=(k == 7),
                           r=[wt.b, uT[ti].b], w=[pb.b])
                    pbs.append(pb)
                cgs = tf.get()
                CP("act", cgs.ap, pbs[2].ap, r=[pbs[2].b], w=[cgs.b])
                cgx = tf.get()
                TTop("dve", cgx.ap, pbs[0].ap, cgs.ap, ALU.mult, r=[pbs[0].b, cgs.b], w=[cgx.b])
                ct = tf.get()
                TTop("pool", ct.ap, cgx.ap, cwb.ap[:, fc, 1:2].to_broadcast([128, 512]), ALU.mult,
                     r=[cgx.b, cwb.b], w=[ct.b])
                xv = cgx.ap.rearrange("p (r l) -> p r l", r=8)
                cv = ct.ap.rearrange("p (r l) -> p r l", r=8)
                STT("dve", cv[:, :, 1:64], xv[:, :, 0:63], cwb.ap[:, fc, 0:1], cv[:, :, 1:64], ALU.mult, ALU.add,
                    r=[cgx.b, cwb.b, ct.b], w=[ct.b])
                STT("dve", cv[:, :, 0:63], xv[:, :, 1:64], cwb.ap[:, fc, 2:3], cv[:, :, 0:63], ALU.mult, ALU.add,
                    r=[cgx.b, cwb.b, ct.b], w=[ct.b])
                TTop("dve", pT[fc][t].ap, pbs[1].ap, ct.ap, ALU.mult, r=[pbs[1].b, ct.b], w=[pT[fc][t].b])
        if b == 0:
            dbg("pT0_0", pT[0][0], [128, 512], BF16)
        ogp = Pool(AR, 2, [8, 512], BF16, "ogp")
        for t in range(4):
            ti = t + 1
            ogt = ogp.get()
            DMA(ogt.ap, og_scr[:, :, t * 512:(t + 1) * 512].rearrange("k p c -> p k c"),
                r=[og_buf[k][t] for k in range(NH)], w=[ogt.b])
            for fc in range(8):
                wga = wload(sc_in, sb_in, 56 + fc)
                wgb = wload(sc_in, sb_in, 64 + fc)
                wa = wload(sc_a, sb_a, fc)
                wb_ = wload(sc_b, sb_b, fc)
                pga = nextbank(); pgb = nextbank(); pya = nextbank(); pyb = nextbank()
                for k in range(8):
                    MM(pga.ap, wga.ap[:, k, :], uT[ti].ap[:, k, :], start=(k == 0), stop=(k == 7),
                       r=[wga.b, uT[ti].b], w=[pga.b])
                for k in range(8):
                    MM(pgb.ap, wgb.ap[:, k, :], uT[ti].ap[:, k, :], start=(k == 0), stop=(k == 7),
                       r=[wgb.b, uT[ti].b], w=[pgb.b])
                for k in range(8):
                    MM(pya.ap, wa.ap[:, k, :], ogt.ap[:, k, :], start=(k == 0), stop=(k == 7),
                       r=[wa.b, ogt.b], w=[pya.b])
                for k in range(8):
                    MM(pyb.ap, wb_.ap[:, k, :], pT[k][t].ap, start=(k == 0), stop=(k == 7),
                       r=[wb_.b, pT[k][t].b], w=[pyb.b])
                ta = tf.get(); tb_ = tf.get()
                ACT(ta.ap, pga.ap, AF.Tanh, r=[pga.b], w=[ta.b], scale=0.5)
                ACT(tb_.ap, pgb.ap, AF.Tanh, r=[pgb.b], w=[tb_.b], scale=0.5)
                STT("dve", ta.ap, ta.ap, 1.0, pya.ap, ALU.add, ALU.mult, r=[ta.b, pya.b], w=[ta.b])
                STT("dve", tb_.ap, tb_.ap, 1.0, pyb.ap, ALU.add, ALU.mult, r=[tb_.b, pyb.b], w=[tb_.b])
                TTop("pool", ymT[fc][t].ap, ta.ap, tb_.ap, ALU.add, r=[ta.b, tb_.b], w=[ymT[fc][t].b])
        if b == 0:
            dbg("ym0_0", ymT[0][0], [128, 512], BF16)
        S.barrier()
        AR.top = m_B
        if stage < 5:
            AR.top = mb
            return

        m_main = AR.top
        AR.top = mb
        hsqT = AR.alloc([32, 512], BF16, "hsqT")
        assert AR.top <= m_phase, (AR.top, m_phase)
        AR.top = m_main
        xa = AR.alloc([8, 512], F32, "xa")
        outT = AR.alloc([8, 512], F32, "outT")
        u2T = AR.alloc([8, 512], BF16, "u2T")
        tf = Pool(AR, 8, [512], F32, "tfC")
        w2p = Pool(AR, 2, [32, 128], BF16, "w2p")
        xs = Pool(AR, 6, [1024], F32, "xsC")

        sq_all = AR.alloc([8, 512], BF16, "sq_all")
        ln_dg = AR.alloc([8, 128], F32, "ln_dg")
        ln_A = AR.alloc([512], F32, "ln_A")
        ln_B = AR.alloc([512], F32, "ln_B")
        ln_mean = AR.alloc([4], F32, "ln_mean")
        ln_msq = AR.alloc([4], F32, "ln_msq")
        ln_var = AR.alloc([4], F32, "ln_var")
        ln_nmr = AR.alloc([4], F32, "ln_nmr")

        def layer_norm(src, dst_fns):
            for fc in range(8):
                ACT(sq_all.ap[:, fc, :], src.ap[:, fc, :], AF.Square, r=[src.b], w=[sq_all.b])
            pst = nextbank()
            for j in range(4):
                for fc in range(8):
                    MM(pst.ap[:, 2 * j:2 * j + 1], src.ap[:, fc, j * 128:(j + 1) * 128], C(ONES)[:, 0:1],
                       start=(fc == 0), stop=(fc == 7), r=[src.b, cst.b], w=[pst.b])
                for fc in range(8):
                    MM(pst.ap[:, 2 * j + 1:2 * j + 2], sq_all.ap[:, fc, j * 128:(j + 1) * 128], ONB[:, 0:1],
                       start=(fc == 0), stop=(fc == 7), r=[sq_all.b, cstb.b], w=[pst.b])
            pv = pst.ap[:, 0:8].rearrange("p (a b) -> p a b", a=4)
            ACT(ln_mean.ap, pv[:, :, 0], AF.Copy, r=[pst.b], w=[ln_mean.b], scale=1.0 / D)
            TTop("dve", ln_msq.ap, ln_mean.ap, ln_mean.ap, ALU.mult, r=[ln_mean.b], w=[ln_msq.b])
            STT("dve", ln_var.ap, pv[:, :, 1], 1.0 / D, ln_msq.ap, ALU.mult, ALU.subtract,
                r=[pst.b, ln_msq.b], w=[ln_var.b])
            ACT(ln_var.ap, ln_var.ap, AF.Identity, r=[ln_var.b, kc.b], w=[ln_var.b], bias=KC("lneps"), scale=1.0)
            TTop("pool", ln_var.ap, ln_var.ap, KC("mhalf").to_broadcast([128, 4]), ALU.pow,
                 r=[ln_var.b, kc.b], w=[ln_var.b])
            STT("dve", ln_nmr.ap, ln_mean.ap, -1.0, ln_var.ap, ALU.mult, ALU.mult,
                r=[ln_mean.b, ln_var.b], w=[ln_nmr.b])
            idb = C(IDENT).unsqueeze(1).to_broadcast([128, 4, 128])
            TTop("pool", ln_dg.ap[:, 0:4, :], idb, ln_var.ap.unsqueeze(2).to_broadcast([128, 4, 128]), ALU.mult,
                 r=[cst.b, ln_var.b], w=[ln_dg.b])
            TTop("pool", ln_dg.ap[:, 4:8, :], idb, ln_nmr.ap.unsqueeze(2).to_broadcast([128, 4, 128]), ALU.mult,
                 r=[cst.b, ln_nmr.b], w=[ln_dg.b])
            pA = nextbank()
            pB = nextbank()
            for j in range(4):
                MM(pA.ap[:, j * 128:(j + 1) * 128], C(ONES), ln_dg.ap[:, j, :], r=[cst.b, ln_dg.b], w=[pA.b])
            for j in range(4):
                MM(pB.ap[:, j * 128:(j + 1) * 128], C(ONES), ln_dg.ap[:, 4 + j, :], r=[cst.b, ln_dg.b], w=[pB.b])
            CP("act", ln_A.ap, pA.ap, r=[pA.b], w=[ln_A.b])
            CP("act", ln_B.ap, pB.ap, r=[pB.b], w=[ln_B.b])
            for fc in range(8):
                xc = tf.get()
                TTop("dve", xc.ap, src.ap[:, fc, :], ln_A.ap, ALU.mult, r=[src.b, ln_A.b], w=[xc.b])
                TTop("pool", xc.ap, xc.ap, ln_B.ap, ALU.add, r=[xc.b, ln_B.b], w=[xc.b])
                for (oap, sc_, bi_, rb, wbuf) in dst_fns:
                    ACT(oap(fc), xc.ap, AF.Identity, r=[xc.b] + rb, w=[wbuf], bias=bi_(fc), scale=sc_(fc))

        wo_t = [None] * 8
        for t in range(4):
            ti = t + 1
            blks = []
            for j in range(4):
                s = xs.get()
                DMA(s.ap, x_d[b, t * 512 + j * 128: t * 512 + (j + 1) * 128, :], w=[s.b])
                blks.append(s)
            for fc in range(8):
                pb = nextbank()
                for j in range(4):
                    MM(pb.ap[:, j * 128:(j + 1) * 128], blks[j].ap[:, fc * 128:(fc + 1) * 128], C(IDENT),
                       r=[blks[j].b, cst.b], w=[pb.b])
                ACT(xa.ap[:, fc, :], pb.ap, AF.Copy, r=[pb.b], w=[xa.b], scale=ALPHA)
            for fc in range(8):
                wo = wload(sc_o, sb_o, fc)
                pb = nextbank()
                for k in range(8):
                    MM(pb.ap, wo.ap[:, k, :], ymT[k][t].ap, start=(k == 0), stop=(k == 7), r=[wo.b, ymT[k][t].b], w=[pb.b])
                STT("dve", xa.ap[:, fc, :], pb.ap, hg1.ap[:, fc, b:b + 1], xa.ap[:, fc, :], ALU.mult, ALU.add,
                    r=[pb.b, hg1.b, xa.b], w=[xa.b])
            if b == 0 and t == 0:
                dbg("pre1", xa, [128, 8, 512])
            layer_norm(xa, [
                (lambda fc: outT.ap[:, fc, :], lambda fc: A1s.ap[:, fc:fc + 1], lambda fc: A1b.ap[:, fc, b:b + 1],
                 [A1s.b, A1b.b], outT.b),
                (lambda fc: u2T.ap[:, fc, :], lambda fc: U2s.ap[:, fc, b:b + 1], lambda fc: U2b.ap[:, fc, b:b + 1],
                 [U2s.b, U2b.b], u2T.b)])
            if b == 0 and t == 0:
                dbg("x1a", outT, [128, 8, 512]); dbg("u2T", u2T, [128, 8, 512], BF16)
            for ffc in range(32):
                w1 = wload(sc_f1, sb_f1, ffc)
                pb = nextbank()
                for k in range(8):
                    MM(pb.ap, w1.ap[:, k, :], u2T.ap[:, k, :], start=(k == 0), stop=(k == 7), r=[w1.b, u2T.b], w=[pb.b])
                rl = tf.get()
                ACT(rl.ap, pb.ap, AF.Relu, r=[pb.b, bf1.b], w=[rl.b], bias=bf1.ap[:, ffc:ffc + 1], scale=1.0)
                STT("dve", hsqT.ap[:, ffc, :], pb.ap, bf1.ap[:, ffc:ffc + 1], rl.ap, ALU.add, ALU.mult,
                    r=[pb.b, bf1.b, rl.b], w=[hsqT.b])
            for fc in range(8):
                w2 = w2p.get()
                DMA(w2.ap.rearrange("p a b -> p (a b)"), sc_f2[fc * 128:(fc + 1) * 128, :], r=[sb_f2[fc]], w=[w2.b])
                pb = nextbank()
                for k in range(32):
                    MM(pb.ap, w2.ap[:, k, :], hsqT.ap[:, k, :], start=(k == 0), stop=(k == 31), r=[w2.b, hsqT.b], w=[pb.b])
                STT("dve", outT.ap[:, fc, :], pb.ap, modT.ap[:, 40 + fc, b:b + 1], outT.ap[:, fc, :], ALU.mult, ALU.add,
                    r=[pb.b, modT.b, outT.b], w=[outT.b])
            if b == 0 and t == 0:
                dbg("pre2", outT, [128, 8, 512])
            layer_norm(outT, [
                (lambda fc: xa.ap[:, fc, :], lambda fc: ln2g.ap[:, fc:fc + 1], lambda fc: ln2b.ap[:, fc:fc + 1],
                 [ln2g.b, ln2b.b], xa.b)])
            if b == 0 and t == 0:
                dbg("xo", xa, [128, 8, 512])
            for j in range(4):
                os_ = xs.get()
                for half in range(2):
                    pb = nextbank()
                    for q in range(4):
                        fc = half * 4 + q
                        MM(pb.ap[:, q * 128:(q + 1) * 128], xa.ap[:, fc, j * 128:(j + 1) * 128], C(IDENT),
                           r=[xa.b, cst.b], w=[pb.b])
                    CP("act" if half == 0 else "dve", os_.ap[:, half * 512:(half + 1) * 512], pb.ap, r=[pb.b], w=[os_.b])
                DMA(out_d[b, t * 512 + j * 128: t * 512 + (j + 1) * 128, :], os_.ap, r=[os_.b])
        S.barrier()
        AR.top = mb

    for b in range(nbatch):
        batch_program(b)

    print("arena peak bytes/partition:", AR.peak, "ops:", {e: len(S.ops[e]) for e in ENGS})
    S.run()
    es.close()
    return nc, dbg_outs


_CACHE = {}


def _consts():
    i = np.arange(128)[:, None]
    j = np.arange(128)[None, :]
    c = np.zeros((NCONST, 128, 128), np.float32)
    c[0] = (i == j)
    c[1] = 1.0
    c[2] = (i <= j)
    c[3] = (i >= j)
    c[4] = np.where(j < i, 0.0, BIG)
    c[5] = np.where(j > i, 0.0, BIG)
    c[6] = np.where(j >= i, 0.0, -BIG)
    c[7] = np.where(j <= i, 0.0, -BIG)
    bd = ((i // 64) == (j // 64)).astype(np.float32)
    c[8] = bd
    c[9] = 1.0 - bd
    return np.ascontiguousarray(c.transpose(1, 0, 2).reshape(128, NCONST * 128))


def _pf(v, n):
    return np.ascontiguousarray(np.asarray(v, np.float32).reshape(n, 128).T)


def prepare_inputs(inputs, core):
    f = lambda k: np.ascontiguousarray(np.asarray(inputs[k], np.float32))
    b0 = 2 * core
    cc = np.stack([f("c")[b0], f("c")[b0 + 1], f("c_ctx")], axis=0)
    cT = np.ascontiguousarray(cc.reshape(3, 8, 128).transpose(2, 1, 0).reshape(128, 24))
    cq = f("conv_qkv")[0]
    cwq = np.ascontiguousarray(cq.reshape(3, 24, 128).transpose(2, 1, 0).reshape(128, 72))
    cb = f("conv_b")[0]
    cwb = np.ascontiguousarray(cb.reshape(3, 8, 128).transpose(2, 1, 0).reshape(128, 24))
    m = {
        "x": np.ascontiguousarray(f("x")[b0:b0 + 2]),
        "ctx": np.ascontiguousarray(f("ctx")[b0:b0 + 2]),
        "cT": cT,
        "w_ada": f("w_ada")[0],
        "b_adaT": _pf(f("b_ada")[0], 48),
        "w_in": f("w_in")[0],
        "cwq": cwq,
        "alog_bc": np.ascontiguousarray(np.broadcast_to(f("a_log")[0].reshape(1, 16), (128, 16))),
        "dtb_bc": np.ascontiguousarray(np.broadcast_to(f("dt_bias")[0].reshape(1, 16), (128, 16))),
        "onw_col": np.ascontiguousarray(f("o_norm_w")[0].reshape(128, 1)),
        "w_a_out": f("w_a_out")[0],
        "cwb": cwb,
        "w_b_out": f("w_b_out")[0],
        "w_o": f("w_o")[0],
        "ln1gT": _pf(f("ln1_g")[0], 8),
        "ln1bT": _pf(f("ln1_b")[0], 8),
        "w_ff1": f("w_ff1")[0],
        "b_ff1T": _pf(f("b_ff1")[0], 32),
        "w_ff2": f("w_ff2")[0],
        "b_ff2T": _pf(f("b_ff2")[0], 8),
        "ln2gT": _pf(f("ln2_g")[0], 8),
        "ln2bT": _pf(f("ln2_b")[0], 8),
        "consts": _consts(),
    }
    return m


def kernel(**inputs):
    if "nc" not in _CACHE:
        _CACHE["nc"] = build_program()[0]
    nc = _CACHE["nc"]
    n = 8
    shared = None
    in_maps = []
    for c in range(n):
        m = prepare_inputs(inputs, c)
        if shared is None:
            shared = m
        else:
            for k in m:
                if k not in ("x", "ctx", "cT"):
                    m[k] = shared[k]
        in_maps.append(m)
    res = run_bass_kernel_spmd(nc, in_maps, core_ids=list(range(n)))
    out = np.concatenate([np.asarray(r["out"], np.float32) for r in res.results], axis=0)
    return out
```

```python
import numpy as np
from contextlib import ExitStack
import concourse.bass as bass
import concourse.mybir as mybir
from concourse.bass_utils import run_bass_kernel_spmd

F32 = mybir.dt.float32
BF16 = mybir.dt.bfloat16
AF = mybir.ActivationFunctionType
ALU = mybir.AluOpType
AX = mybir.AxisListType

ENGS = ("pe", "act", "dve", "pool", "sp")
N_LANES = 24

D = 1024
SEQ = 2048
CTXL = 256
NTOK = SEQ + CTXL
NBLK = NTOK // 128
NH = 8
P_TOTAL = 9248
ALPHA = 2.0 ** 0.25
LN_EPS = 1e-5
RMS_EPS = 1e-6
L2_EPS = 1e-6
TILES = [(0, 256), (256, 512), (768, 512), (1280, 512), (1792, 512)]
ORDER = [list(range(18)), [1, 0] + list(range(17, 1, -1))]
NCONST = 10
BIG = 30000.0


class Buf:
    __slots__ = ("name", "w", "r", "excl")

    def __init__(self, name, excl=False):
        self.name = name
        self.w = None
        self.r = []
        self.excl = excl


class Op:
    __slots__ = ("eng", "fn", "deps", "idx", "sig", "tok", "is_dma")

    def __init__(self, eng, fn):
        self.eng = eng
        self.fn = fn
        self.deps = []
        self.sig = False
        self.tok = None
        self.is_dma = False


class Sched:
    def __init__(self, nc):
        self.nc = nc
        self.ops = {e: [] for e in ENGS}
        self.lane_cnt = [0] * N_LANES
        self.lane_last = [None] * N_LANES
        self.next_lane = 0
        self.next_lane2 = 0

    def _track(self, op, reads, writes):
        deps = []
        rd, wr = [], []
        for b in reads:
            (wr if b.excl else rd).append(b)
        wr.extend(writes)
        for b in rd:
            if b.w is not None:
                deps.append(b.w)
        for b in wr:
            if b.w is not None:
                deps.append(b.w)
            deps.extend(b.r)
        for b in rd:
            b.r.append(op)
        for b in wr:
            b.w = op
            b.r = []
        seen = set()
        for d in deps:
            if d is op or id(d) in seen:
                continue
            seen.add(id(d))
            if d.eng == "pe" and op.eng == "pe" and not d.is_dma and not op.is_dma:
                continue
            op.deps.append(d)

    def op(self, eng, fn, reads=(), writes=()):
        o = Op(eng, fn)
        self._track(o, reads, writes)
        self.ops[eng].append(o)
        return o

    def dma(self, fn, reads=(), writes=(), eng="sp"):
        o = Op(eng, fn)
        o.is_dma = True
        lane = self.next_lane
        self.next_lane = (self.next_lane + 1) % N_LANES
        prev = self.lane_last[lane]
        self._track(o, reads, writes)
        if prev is not None and all(d is not prev for d in o.deps):
            o.deps.append(prev)
        self.lane_cnt[lane] += 16
        o.tok = ("lane", lane, self.lane_cnt[lane])
        self.lane_last[lane] = o
        self.ops[eng].append(o)
        return o

    def barrier(self):
        lasts = []
        for e in ENGS:
            for o in reversed(self.ops[e]):
                if not o.is_dma:
                    lasts.append(o)
                    break
        lasts.extend(o for o in self.lane_last if o is not None)
        for e in ENGS:
            o = Op(e, lambda eng: eng.nop())
            o.deps = [d for d in lasts if not (d.eng == e and not d.is_dma)]
            self.ops[e].append(o)

    def run(self):
        nc = self.nc
        for e in ENGS:
            for o in self.ops[e]:
                for d in o.deps:
                    if not d.is_dma:
                        d.sig = True
        for e in ENGS:
            c = 0
            for o in self.ops[e]:
                if o.is_dma:
                    continue
                if o.sig:
                    c += 1
                    o.tok = ("eng", e, c)
        with ExitStack() as es:
            esem = {e: es.enter_context(nc.semaphore("s_" + e)) for e in ENGS}
            lsem = [es.enter_context(nc.semaphore("l_%d" % i)) for i in range(N_LANES)]
            block = es.enter_context(nc.Block())

            def body(ename, final=False):
                def _f(eng):
                    waited = {}
                    for o in self.ops[ename]:
                        need = {}
                        for d in o.deps:
                            k = (d.tok[0], d.tok[1])
                            if d.tok[2] > need.get(k, 0):
                                need[k] = d.tok[2]
                        for k, v in need.items():
                            if waited.get(k, 0) >= v:
                                continue
                            eng.wait_ge(esem[k[1]] if k[0] == "eng" else lsem[k[1]], v)
                            waited[k] = v
                        ins = o.fn(eng)
                        if o.is_dma:
                            ins.then_inc(lsem[o.tok[1]], 16)
                        elif o.sig:
                            ins.then_inc(esem[ename], 1)
                    if final:
                        for lane in range(N_LANES):
                            if self.lane_cnt[lane] > 0:
                                eng.wait_ge(lsem[lane], self.lane_cnt[lane])
                return _f

            block.tensor(body("pe"))
            block.scalar(body("act"))
            block.vector(body("dve"))
            block.gpsimd(body("pool"))
            block.sync(body("sp", final=True))


class T:
    __slots__ = ("ap", "b")

    def __init__(self, ap, b):
        self.ap = ap
        self.b = b


class Arena:
    def __init__(self, tens, nf32):
        self.t = tens
        self.cap = nf32 * 4
        self.top = 0
        self.peak = 0

    def alloc(self, free_shape, dt, name, at=None):
        n = int(np.prod(free_shape))
        esz = 4 if dt == F32 else 2
        nbytes = (n * esz + 3) // 4 * 4
        if at is None:
            off = (self.top + 63) // 64 * 64
            self.top = off + nbytes
            self.peak = max(self.peak, self.top)
            assert self.top <= self.cap, ("SBUF arena overflow", name, self.top, self.cap)
        else:
            off = at
        ap = self.t[:, off // 4: off // 4 + nbytes // 4]
        if dt == BF16:
            ap = ap.bitcast(BF16)
        ap = ap[:, 0:n]
        if len(free_shape) == 2:
            ap = ap.rearrange("p (a b) -> p a b", a=free_shape[0])
        elif len(free_shape) == 3:
            ap = ap.rearrange("p (a b c) -> p a b c", a=free_shape[0], b=free_shape[1])
        return T(ap, Buf(name))


class Pool:
    def __init__(self, arena, n, free_shape, dt, name):
        self.items = [arena.alloc(free_shape, dt, "%s%d" % (name, i)) for i in range(n)]
        self.i = 0

    def get(self):
        t = self.items[self.i]
        self.i = (self.i + 1) % len(self.items)
        return t


def build_program(debug=None, stage=99, nbatch=2):
    nc = bass.Bass("TRN2", target_bir_lowering=False)
    S = Sched(nc)
    dbg_outs = {}

    def din(name, shape, dt=F32):
        return nc.dram_tensor(name, list(shape), dt, kind="ExternalInput").ap()

    x_d = din("x", [2, SEQ, D])
    ctx_d = din("ctx", [2, CTXL, D])
    cT_d = din("cT", [128, 24])
    wada_d = din("w_ada", [D, 6 * D])
    badaT_d = din("b_adaT", [128, 48])
    win_d = din("w_in", [D, P_TOTAL])
    cwq_d = din("cwq", [128, 72])
    alog_d = din("alog_bc", [128, 16])
    dtb_d = din("dtb_bc", [128, 16])
    onw_d = din("onw_col", [128, 1])
    wa_d = din("w_a_out", [D, D])
    cwb_d = din("cwb", [128, 24])
    wb_d = din("w_b_out", [D, D])
    wo_d = din("w_o", [D, D])
    ln1g_d = din("ln1gT", [128, 8])
    ln1b_d = din("ln1bT", [128, 8])
    wf1_d = din("w_ff1", [D, 4 * D])
    bf1_d = din("b_ff1T", [128, 32])
    wf2_d = din("w_ff2", [4 * D, D])
    bf2_d = din("b_ff2T", [128, 8])
    ln2g_d = din("ln2gT", [128, 8])
    ln2b_d = din("ln2bT", [128, 8])
    const_d = din("consts", [128, NCONST * 128])
    out_d = nc.dram_tensor("out", [2, SEQ, D], F32, kind="ExternalOutput").ap()

    def dscr(name, nchunk, K):
        t = nc.dram_tensor(name, [nchunk * 128, K * 128], BF16, kind="Internal").ap()
        return t, [Buf("%s_%d" % (name, i)) for i in range(nchunk)]

    sc_in, sb_in = dscr("sc_in", 72, 8)
    sc_bg_t = nc.dram_tensor("sc_bg", [128, 256], BF16, kind="Internal").ap()
    sb_bg = Buf("sc_bg")
    sc_a, sb_a = dscr("sc_a", 8, 8)
    sc_b, sb_b = dscr("sc_b", 8, 8)
    sc_o, sb_o = dscr("sc_o", 8, 8)
    sc_f1, sb_f1 = dscr("sc_f1", 32, 8)
    sc_f2, sb_f2 = dscr("sc_f2", 8, 32)

    def dbg(name, t, shape, dt=F32):
        if not debug or name not in debug:
            return
        o = nc.dram_tensor("dbg_" + name, list(shape), dt, kind="ExternalOutput").ap()
        dbg_outs[name] = o
        ap = t.ap
        S.dma(lambda e: e.dma_start(out=o, in_=ap), reads=[t.b])

    es = ExitStack()
    NF32 = 53200
    arena_t = es.enter_context(nc.sbuf_tensor("arena", [128, NF32], F32))
    AR = Arena(arena_t, NF32)
    banks = []
    for i in range(8):
        pt = es.enter_context(nc.psum_tensor("ps%d" % i, [128, 512], F32))
        banks.append(T(pt[:, :], Buf("ps%d" % i, excl=True)))
    rr = [0]
    rr_lo = [0]

    def nextbank():
        if rr[0] < rr_lo[0]:
            rr[0] = rr_lo[0]
        b = banks[rr[0]]
        rr[0] += 1
        if rr[0] >= 8:
            rr[0] = rr_lo[0]
        return b

    def MM(out, lhsT, rhs, start=True, stop=True, r=(), w=()):
        S.op("pe", lambda e: e.matmul(out, lhsT=lhsT, rhs=rhs, start=start, stop=stop), r, w)

    def ACT(out, in_, func, r=(), w=(), bias=None, scale=1.0):
        if bias is None:
            S.op("act", lambda e: e.activation(out=out, in_=in_, func=func, scale=scale), r, w)
        else:
            S.op("act", lambda e: e.activation(out=out, in_=in_, func=func, bias=bias, scale=scale), r, w)

    def TTop(eng, out, in0, in1, op, r=(), w=()):
        S.op(eng, lambda e: e.tensor_tensor(out=out, in0=in0, in1=in1, op=op), r, w)

    def STT(eng, out, in0, scalar, in1, op0, op1, r=(), w=()):
        S.op(eng, lambda e: e.scalar_tensor_tensor(out=out, in0=in0, scalar=scalar, in1=in1, op0=op0, op1=op1), r, w)

    def TS(eng, out, in0, s1, s2, op0, op1=None, r=(), w=()):
        if op1 is None:
            S.op(eng, lambda e: e.tensor_scalar(out=out, in0=in0, scalar1=s1, scalar2=None, op0=op0), r, w)
        else:
            S.op(eng, lambda e: e.tensor_scalar(out=out, in0=in0, scalar1=s1, scalar2=s2, op0=op0, op1=op1), r, w)

    def CP(eng, out, in_, r=(), w=()):
        if eng == "act":
            S.op("act", lambda e: e.copy(out=out, in_=in_), r, w)
        else:
            S.op(eng, lambda e: e.tensor_copy(out=out, in_=in_), r, w)

    def DMA(out, in_, r=(), w=()):
        S.dma(lambda e: e.dma_start(out=out, in_=in_), r, w)

    cst = AR.alloc([NCONST, 128], F32, "cst")
    DMA(cst.ap.rearrange("p a b -> p (a b)"), const_d, w=[cst.b])
    IDENT, ONES, CUM0, CUM1, MPOS0, MPOS1, MTN0, MTN1, BDM, NBDM = range(10)

    def C(i):
        return cst.ap[:, i, :]

    cstb = AR.alloc([4, 128], BF16, "cstb")
    for j, i in enumerate([IDENT, ONES, BDM, NBDM]):
        CP("dve", cstb.ap[:, j, :], C(i), r=[cst.b], w=[cstb.b])
    IDB, ONB, BDB, NBDB = [cstb.ap[:, j, :] for j in range(4)]

    kc = AR.alloc([8], F32, "kcols")
    KV = {"eps4": 4 * L2_EPS, "eps4q": 4 * L2_EPS * 128.0, "rmseps": RMS_EPS, "lneps": LN_EPS, "mhalf": -0.5,
          "one": 1.0, "zero": 0.0}
    KI = {}
    for j, (k, v) in enumerate(KV.items()):
        KI[k] = j
        S.op("pool", lambda e, j=j, v=v: e.memset(kc.ap[:, j:j + 1], v), (), [kc.b])

    def KC(name):
        j = KI[name]
        return kc.ap[:, j:j + 1]

    def small(name, shape):
        return AR.alloc(shape, F32, name)

    cT = small("cT", [8, 3])
    badaT = small("badaT", [48])
    cwq = small("cwq", [24, 3])
    cwb = small("cwb", [8, 3])
    alog = small("alog", [16])
    dtb = small("dtb", [16])
    onwc = small("onwc", [1])
    ln1g = small("ln1g", [8]); ln1b = small("ln1b", [8]); ln2g = small("ln2g", [8]); ln2b = small("ln2b", [8])
    bf1 = small("bf1", [32]); bf2 = small("bf2", [8])
    DMA(cT.ap.rearrange("p a b -> p (a b)"), cT_d, w=[cT.b])
    DMA(badaT.ap, badaT_d, w=[badaT.b])
    DMA(cwq.ap.rearrange("p a b -> p (a b)"), cwq_d, w=[cwq.b])
    DMA(cwb.ap.rearrange("p a b -> p (a b)"), cwb_d, w=[cwb.b])
    DMA(alog.ap, alog_d, w=[alog.b])
    DMA(dtb.ap, dtb_d, w=[dtb.b])
    DMA(onwc.ap, onw_d, w=[onwc.b])
    for t_, d_ in ((ln1g, ln1g_d), (ln1b, ln1b_d), (ln2g, ln2g_d), (ln2b, ln2b_d), (bf1, bf1_d), (bf2, bf2_d)):
        DMA(t_.ap, d_, w=[t_.b])
    ACT(onwc.ap, onwc.ap, AF.Copy, r=[onwc.b], w=[onwc.b], scale=0.5)
    nA = small("nA", [16])
    ACT(nA.ap, alog.ap, AF.Exp, r=[alog.b], w=[nA.b])
    TS("dve", nA.ap, nA.ap, -1.0, None, ALU.mult, r=[nA.b], w=[nA.b])
    modT = small("modT", [48, 3])
    s1c = small("s1c", [8, 3])
    hg1 = small("hg1", [8, 2])
    A1s = small("A1s", [8])
    A1b = small("A1b", [8, 2])
    U2s = small("U2s", [8, 2])
    U2b = small("U2b", [8, 2])
    wtp = Pool(AR, 6, [8, 128], BF16, "wt")
    mark_persist = AR.top

    cast_head, cast_rest = [], []
    cast_jobs = cast_head

    def mk_job(W, K0, K, c0, ncol, dst_ap, dst_buf):
        src = W.rearrange("(k p) c -> p k c", p=128)[:, K0:K0 + K, c0:c0 + ncol]
        dst = dst_ap.rearrange("p (k c) -> p k c", k=K)
        cast_jobs.append((src, dst, dst_buf))

    def in_col(c):
        return c * 128 if c < 24 else 3104 + (c - 24) * 128

    mk_job(win_d, 0, 8, 3072, 32, sc_bg_t, sb_bg)
    for h in range(NH):
        for c in (h, 8 + h, 16 + h, 24 + h):
            mk_job(win_d, 0, 8, in_col(c), 128, sc_in[c * 128:(c + 1) * 128, :], sb_in[c])
    cast_jobs = cast_rest
    for c in range(32, 72):
        mk_job(win_d, 0, 8, in_col(c), 128, sc_in[c * 128:(c + 1) * 128, :], sb_in[c])
    for W, sc, sb in ((wa_d, sc_a, sb_a), (wb_d, sc_b, sb_b), (wo_d, sc_o, sb_o)):
        for c in range(8):
            mk_job(W, 0, 8, c * 128, 128, sc[c * 128:(c + 1) * 128, :], sb[c])
    for c in range(32):
        mk_job(wf1_d, 0, 8, c * 128, 128, sc_f1[c * 128:(c + 1) * 128, :], sb_f1[c])
    for c in range(8):
        for kq in range(4):
            mk_job(wf2_d, kq * 8, 8, c * 128, 128, sc_f2[c * 128:(c + 1) * 128, kq * 1024:(kq + 1) * 1024], sb_f2[c])
    def emit_casts(lst, n):
        for _ in range(n):
            if not lst:
                return
            src, dst, dbuf = lst.pop(0)
            S.dma(lambda e, src=src, dst=dst: e.dma_start(out=dst, in_=src), reads=[], writes=[dbuf], eng="pool")

    def mod_phase():
        stg = Pool(AR, 3, [8, 128], F32, "mstg")
        sil = AR.alloc([8, 3], F32, "silc")
        th = AR.alloc([8, 3], F32, "silt")
        ACT(th.ap, cT.ap, AF.Tanh, r=[cT.b], w=[th.b], scale=0.5)
        STT("dve", sil.ap, th.ap, 1.0, cT.ap, ALU.add, ALU.mult, r=[th.b, cT.b], w=[sil.b])
        TS("dve", sil.ap, sil.ap, 0.5, None, ALU.mult, r=[sil.b], w=[sil.b])
        pb = banks[0]
        for fc in range(48):
            s = stg.get()
            DMA(s.ap, wada_d.rearrange("(k p) c -> p k c", p=128)[:, :, fc * 128:(fc + 1) * 128], w=[s.b])
            for k in range(8):
                MM(pb.ap[:, fc * 3:fc * 3 + 3], s.ap[:, k, :], sil.ap[:, k, :], start=(k == 0), stop=(k == 7),
                   r=[s.b, sil.b], w=[pb.b])
        TTop("dve", modT.ap, pb.ap[:, 0:144].rearrange("p (a b) -> p a b", a=48),
             badaT.ap.unsqueeze(2).to_broadcast([128, 48, 3]), ALU.add, r=[pb.b, badaT.b], w=[modT.b])
        m = modT.ap
        TS("dve", s1c.ap, m[:, 8:16, :], 1.0, None, ALU.add, r=[modT.b], w=[s1c.b])
        TS("dve", hg1.ap, m[:, 16:24, 0:2], 0.5, None, ALU.mult, r=[modT.b], w=[hg1.b])
        TS("dve", A1s.ap, ln1g.ap, ALPHA, None, ALU.mult, r=[ln1g.b], w=[A1s.b])
        tmp = AR.alloc([8, 2], F32, "mtmp")
        TTop("dve", tmp.ap, m[:, 40:48, 0:2], bf2.ap.unsqueeze(2).to_broadcast([128, 8, 2]), ALU.mult,
             r=[modT.b, bf2.b], w=[tmp.b])
        STT("dve", A1b.ap, ln1b.ap.unsqueeze(2).to_broadcast([128, 8, 2]), ALPHA, tmp.ap, ALU.mult, ALU.add,
            r=[ln1b.b, tmp.b], w=[A1b.b])
        tmp2 = AR.alloc([8, 2], F32, "mtmp2")
        TS("dve", tmp2.ap, m[:, 32:40, 0:2], 1.0, None, ALU.add, r=[modT.b], w=[tmp2.b])
        TTop("dve", U2s.ap, tmp2.ap, ln1g.ap.unsqueeze(2).to_broadcast([128, 8, 2]), ALU.mult,
             r=[tmp2.b, ln1g.b], w=[U2s.b])
        TTop("dve", U2b.ap, tmp2.ap, ln1b.ap.unsqueeze(2).to_broadcast([128, 8, 2]), ALU.mult,
             r=[tmp2.b, ln1b.b], w=[U2b.b])
        TTop("dve", U2b.ap, U2b.ap, m[:, 24:32, 0:2], ALU.add, r=[U2b.b, modT.b], w=[U2b.b])

    m0 = AR.top
    emit_casts(cast_head, 1 + 8)
    mod_phase()
    dbg("modT", modT, [128, 144])
    S.barrier()
    AR.top = m0

    def batch_program(b):
        mb = AR.top
        uT = [AR.alloc([8, ln], BF16, "uT%d" % ti) for ti, (st, ln) in enumerate(TILES)]
        og_scr = nc.dram_tensor("og_scr%d" % b, [NH, 128, SEQ], BF16, kind="Internal").ap()
        og_buf = [[Buf("ogs%d_%d_%d" % (b, h, t)) for t in range(4)] for h in range(NH)]
        m_phase = AR.top

        def utile(tok):
            for ti, (st, ln) in enumerate(TILES):
                if st <= tok < st + ln:
                    return ti, tok - st
            raise ValueError

        xs = Pool(AR, 6, [1024], F32, "xs")
        for ti, (st, ln) in enumerate(TILES):
            nb = ln // 128
            blks = []
            for j in range(nb):
                s = xs.get()
                if ti == 0:
                    src = ctx_d[b, j * 128:(j + 1) * 128, :]
                else:
                    src = x_d[b, st - CTXL + j * 128: st - CTXL + (j + 1) * 128, :]
                DMA(s.ap, src, w=[s.b])
                blks.append(s)
            jm = 2 if ti == 0 else b
            for fc in range(8):
                pb = nextbank()
                for j in range(nb):
                    MM(pb.ap[:, j * 128:(j + 1) * 128], blks[j].ap[:, fc * 128:(fc + 1) * 128], C(IDENT),
                       r=[blks[j].b, cst.b], w=[pb.b])
                ACT(uT[ti].ap[:, fc, :], pb.ap[:, 0:ln], AF.Identity, r=[pb.b, s1c.b, modT.b], w=[uT[ti].b],
                    bias=modT.ap[:, fc, jm:jm + 1], scale=s1c.ap[:, fc, jm:jm + 1])
        if b == 0:
            dbg("uT1", uT[1], [128, 8, 512], BF16)
        S.barrier()
        AR.top = m_phase
        if stage < 1:
            AR.top = mb
            return

        def pcol(name):
            return AR.alloc([NBLK, 16], F32, name)

        bt = pcol("bt"); hbt = pcol("hbt"); gg = pcol("gg"); Gc = pcol("Gc")
        n2eG = pcol("n2eG"); kdec = pcol("kdec"); eGl = pcol("eGl")
        m_A = AR.top
        wbg = AR.alloc([8, 32], BF16, "wbg")
        DMA(wbg.ap.rearrange("p a b -> p (a b)"), sc_bg_t, r=[sb_bg], w=[wbg.b])
        tb = AR.alloc([NBLK, 16], F32, "pc_tb")
        xg = AR.alloc([NBLK, 16], F32, "pc_xg")
        for half in range(2):
            pb = nextbank()
            for j in range(9):
                blk = half * 9 + j
                ti, off = utile(blk * 128)
                for k in range(8):
                    MM(pb.ap[:, j * 32:(j + 1) * 32], uT[ti].ap[:, k, off:off + 128], wbg.ap[:, k, :],
                       start=(k == 0), stop=(k == 7), r=[uT[ti].b, wbg.b], w=[pb.b])
            pv = pb.ap[:, 0:288].rearrange("p (a b) -> p a b", a=9)
            sl = slice(half * 9, half * 9 + 9)
            ACT(tb.ap[:, sl, :], pv[:, :, 0:16], AF.Tanh, r=[pb.b], w=[tb.b], scale=0.5)
            TTop("dve", xg.ap[:, sl, :], pv[:, :, 16:32], dtb.ap.unsqueeze(1).to_broadcast([128, 9, 16]), ALU.add,
                 r=[pb.b, dtb.b], w=[xg.b])
        TS("dve", bt.ap, tb.ap, 0.5, 0.5, ALU.mult, ALU.add, r=[tb.b], w=[bt.b])
        TS("dve", hbt.ap, tb.ap, 0.25, 0.25, ALU.mult, ALU.add, r=[tb.b], w=[hbt.b])
        TS("dve", xg.ap, xg.ap, 30.0, None, ALU.min, r=[xg.b], w=[xg.b])
        ACT(xg.ap, xg.ap, AF.Exp, r=[xg.b], w=[xg.b])
        ACT(xg.ap, xg.ap, AF.Ln, r=[xg.b, kc.b], w=[xg.b], bias=KC("one"))
        TTop("dve", gg.ap, xg.ap, nA.ap.unsqueeze(1).to_broadcast([128, NBLK, 16]), ALU.mult, r=[xg.b, nA.b], w=[gg.b])
        pb = nextbank()
        pv = pb.ap[:, 0:288].rearrange("p (a b) -> p a b", a=NBLK)
        for d in range(2):
            MM(pv[:, :, d * 8:(d + 1) * 8], C(CUM0 + d), gg.ap[:, :, d * 8:(d + 1) * 8], r=[cst.b, gg.b], w=[pb.b])
        CP("act", Gc.ap, pv, r=[pb.b], w=[Gc.b])
        pb2 = nextbank()
        pv2 = pb2.ap[:, 0:288].rearrange("p (a b) -> p a b", a=NBLK)
        MM(pb2.ap[:, 0:288], C(ONES), gg.ap.rearrange("p a b -> p (a b)"), r=[cst.b, gg.b], w=[pb2.b])
        ACT(eGl.ap, pv2, AF.Exp, r=[pb2.b], w=[eGl.b])
        TTop("dve", kdec.ap, pv2, Gc.ap, ALU.subtract, r=[pb2.b, Gc.b], w=[kdec.b])
        ACT(kdec.ap, kdec.ap, AF.Exp, r=[kdec.b], w=[kdec.b])
        ACT(n2eG.ap, Gc.ap, AF.Exp, r=[Gc.b], w=[n2eG.b])
        TS("dve", n2eG.ap, n2eG.ap, -2.0, None, ALU.mult, r=[n2eG.b], w=[n2eG.b])
        if b == 0:
            dbg("bt", bt, [128, NBLK, 16]); dbg("gg", gg, [128, NBLK, 16]); dbg("Gc", Gc, [128, NBLK, 16])
            dbg("kdec", kdec, [128, NBLK, 16]); dbg("eGl", eGl, [128, NBLK, 16])
        AR.top = m_A

        qTs = [AR.alloc([NTOK], BF16, "qT%d" % i) for i in range(2)]
        kTs = [AR.alloc([NTOK], BF16, "kT%d" % i) for i in range(2)]
        k_tms = [AR.alloc([NBLK, 128], BF16, "k_tm%d" % i) for i in range(2)]
        v_tms = [AR.alloc([NBLK, 128], BF16, "v_tm%d" % i) for i in range(2)]
        zs2Ts = [AR.alloc([SEQ], BF16, "zs2T%d" % i) for i in range(2)]
        WSn = ("Q", "AT", "Ub", "nWT", "Bb")
        WS = [{n_: AR.alloc([12, 128], BF16, "ws%d_%s" % (p_, n_)) for n_ in WSn} for p_ in range(2)]
        o_acc = AR.alloc([16, 128], F32, "o_acc")
        tf = Pool(AR, 3, [512], F32, "tf")
        tb16 = Pool(AR, 3, [512], BF16, "tb")
        chain = [[AR.alloc([4, 128], BF16, "ch%d_%d" % (g_, i)) for i in range(8)] for g_ in range(3)]
        S32 = [[AR.alloc([128], F32, "S32_%d_%d" % (d, i)) for i in range(2)] for d in range(2)]
        Sb = [[AR.alloc([128], BF16, "Sb_%d_%d" % (d, i)) for i in range(2)] for d in range(2)]
        e_ct = AR.alloc([512], F32, "e_ct")
        e_s2 = AR.alloc([512], F32, "e_s2")
        e_dg = AR.alloc([512], F32, "e_dg")
        e_sq = AR.alloc([512], BF16, "e_sq")
        e_vt = e_sq
        e_rs = AR.alloc([8], F32, "e_rs")
        on_all = AR.alloc([16, 128], BF16, "on_all")
        dec = [(AR.alloc([512], F32, "nD%d" % g_), AR.alloc([512], F32, "DT%d" % g_)) for g_ in range(3)]
        ssq = AR.alloc([16], F32, "ssq")
        rstd = AR.alloc([16], F32, "rstd")


        def early_gen(h):
            qT, kT, k_tm, v_tm, zs2T = qTs[h % 2], kTs[h % 2], k_tms[h % 2], v_tms[h % 2], zs2Ts[h % 2]
            wts = []
            for c in (h, 8 + h, 16 + h, 24 + h):
                wt = wtp.get()
                DMA(wt.ap.rearrange("p a b -> p (a b)"), sc_in[c * 128:(c + 1) * 128, :], r=[sb_in[c]], w=[wt.b])
                wts.append(wt)
            yield
            for ti, (st, ln) in enumerate(TILES):
                R = 1 if ti == 0 else ln // 64
                L = ln // R
                nb = ln // 128
                for wi, name in enumerate(("q", "k", "v")):
                    pb = nextbank()
                    for k in range(8):
                        MM(pb.ap[:, 0:ln], wts[wi].ap[:, k, :], uT[ti].ap[:, k, :], start=(k == 0), stop=(k == 7),
                           r=[wts[wi].b, uT[ti].b], w=[pb.b])
                    fcw = wi * 8 + h
                    ct = e_ct
                    ACT(ct.ap[:, 0:ln], pb.ap[:, 0:ln], AF.Copy, r=[pb.b, cwq.b], w=[ct.b], scale=cwq.ap[:, fcw, 1:2])
                    pv = pb.ap[:, 0:ln].rearrange("p (r l) -> p r l", r=R)
                    cv = ct.ap[:, 0:ln].rearrange("p (r l) -> p r l", r=R)
                    STT("dve", cv[:, :, 1:L], pv[:, :, 0:L - 1], cwq.ap[:, fcw, 0:1], cv[:, :, 1:L], ALU.mult, ALU.add,
                        r=[pb.b, cwq.b, ct.b], w=[ct.b])
                    STT("dve", cv[:, :, 0:L - 1], pv[:, :, 1:L], cwq.ap[:, fcw, 2:3], cv[:, :, 0:L - 1], ALU.mult, ALU.add,
                        r=[pb.b, cwq.b, ct.b], w=[ct.b])
                    s2 = e_s2
                    ACT(s2.ap[:, 0:ln], ct.ap[:, 0:ln], AF.Tanh, r=[ct.b], w=[s2.b], scale=0.5)
                    if name == "v":
                        STT("dve", e_vt.ap[:, 0:ln], s2.ap[:, 0:ln], 1.0, ct.ap[:, 0:ln], ALU.add, ALU.mult,
                            r=[s2.b, ct.b], w=[e_vt.b])
                        yield
                        pbv = nextbank()
                        for j in range(nb):
                            MM(pbv.ap[:, j * 128:(j + 1) * 128], e_vt.ap[:, j * 128:(j + 1) * 128], IDB,
                               r=[e_vt.b, cstb.b], w=[pbv.b])
                        CP("act", v_tm.ap[:, st // 128: st // 128 + nb, :].rearrange("p a b -> p (a b)"),
                           pbv.ap[:, 0:ln], r=[pbv.b], w=[v_tm.b])
                        yield
                        continue
                    STT("dve", s2.ap[:, 0:ln], s2.ap[:, 0:ln], 1.0, ct.ap[:, 0:ln], ALU.add, ALU.mult,
                        r=[s2.b, ct.b], w=[s2.b])
                    TTop("pool", e_sq.ap[:, 0:ln], s2.ap[:, 0:ln], s2.ap[:, 0:ln], ALU.mult, r=[s2.b], w=[e_sq.b])
                    yield
                    pb2 = nextbank()
                    for j in range(nb):
                        MM(pb2.ap[:, j:j + 1], e_sq.ap[:, j * 128:(j + 1) * 128], ONB[:, 0:1], r=[e_sq.b, cstb.b], w=[pb2.b])
                    rs = e_rs
                    if name == "q":
                        ACT(rs.ap[:, 0:nb], pb2.ap[:, 0:nb], AF.Identity, r=[pb2.b, kc.b], w=[rs.b],
                            bias=KC("eps4q"), scale=128.0)
                    else:
                        ACT(rs.ap[:, 0:nb], pb2.ap[:, 0:nb], AF.Identity, r=[pb2.b, kc.b], w=[rs.b],
                            bias=KC("eps4"), scale=1.0)
                    TTop("pool", rs.ap[:, 0:nb], rs.ap[:, 0:nb], KC("mhalf").to_broadcast([128, nb]), ALU.pow,
                         r=[rs.b, kc.b], w=[rs.b])
                    dg = e_dg
                    dgv = dg.ap[:, 0:ln].rearrange("p (a b) -> p a b", a=nb)
                    TTop("pool", dgv, C(IDENT).unsqueeze(1).to_broadcast([128, nb, 128]),
                         rs.ap[:, 0:nb].unsqueeze(2).to_broadcast([128, nb, 128]), ALU.mult,
                         r=[cst.b, rs.b], w=[dg.b])
                    yield
                    pb3 = nextbank()
                    for j in range(nb):
                        MM(pb3.ap[:, j * 128:(j + 1) * 128], C(ONES), dgv[:, j, :], r=[cst.b, dg.b], w=[pb3.b])
                    dst = qT if name == "q" else kT
                    TTop("dve", dst.ap[:, st:st + ln], s2.ap[:, 0:ln], pb3.ap[:, 0:ln], ALU.mult,
                         r=[s2.b, pb3.b], w=[dst.b])
                    yield
                    if name == "k":
                        pbt = nextbank()
                        for j in range(nb):
                            MM(pbt.ap[:, j * 128:(j + 1) * 128], kT.ap[:, st + j * 128: st + (j + 1) * 128], IDB,
                               r=[kT.b, cstb.b], w=[pbt.b])
                        CP("act", k_tm.ap[:, st // 128: st // 128 + nb, :].rearrange("p a b -> p (a b)"),
                           pbt.ap[:, 0:ln], r=[pbt.b], w=[k_tm.b])
                        yield
            for ti, (st, ln) in enumerate(TILES):
                if ti == 0:
                    continue
                pb = nextbank()
                for k in range(8):
                    MM(pb.ap[:, 0:ln], wts[3].ap[:, k, :], uT[ti].ap[:, k, :], start=(k == 0), stop=(k == 7),
                       r=[wts[3].b, uT[ti].b], w=[pb.b])
                ACT(e_ct.ap[:, 0:ln], pb.ap[:, 0:ln], AF.Tanh, r=[pb.b], w=[e_ct.b], scale=0.5)
                STT("dve", zs2T.ap[:, st - CTXL: st - CTXL + ln], e_ct.ap[:, 0:ln], 1.0, pb.ap[:, 0:ln],
                    ALU.add, ALU.mult, r=[e_ct.b, pb.b], w=[zs2T.b])
                yield
            if b == 0 and h == 0:
                dbg("qT", qT, [128, NTOK], BF16); dbg("kT", kT, [128, NTOK], BF16)
                dbg("v_tm", v_tm, [128, NBLK, 128], BF16); dbg("zs2T", zs2T, [128, SEQ], BF16)

        def adv(gen, n):
            if gen is None:
                return
            for _ in range(n):
                try:
                    next(gen)
                except StopIteration:
                    return

        def wave_units(w):
            fw = [(0, 6 * w + s_) for s_ in range(6)]
            bw = sorted([(1, 6 * w + s_) for s_ in range(6)], key=lambda t_: ORDER[1][t_[1]])
            return fw + bw

        def head_ctx(h):
            qT, kT, k_tm, v_tm, zs2T = qTs[h % 2], kTs[h % 2], k_tms[h % 2], v_tms[h % 2], zs2Ts[h % 2]

            def init():
                for d in range(2):
                    S.op("pool", lambda e, d=d: e.memset(S32[d][0].ap, 0.0), (), [S32[d][0].b])
                    S.op("pool", lambda e, d=d: e.memset(Sb[d][0].ap, 0.0), (), [Sb[d][0].b])

            def col(t_, blk, d):
                return t_.ap[:, blk, d * 8 + h: d * 8 + h + 1]

            def cols(t_, blk0, n, d):
                return t_.ap[:, blk0:blk0 + n, d * 8 + h]

            def runs(us):
                out = []
                for j, (d, s_) in enumerate(us):
                    blk = ORDER[d][s_]
                    if out and out[-1][2] == d and out[-1][3] + out[-1][1] == blk:
                        out[-1][1] += 1
                    else:
                        out.append([j, 1, d, blk])
                return out

            def wave_prep(w):
                units = wave_units(w)
                f2 = lambda t_: t_.ap.rearrange("p a b -> p (a b)")
                ws = WS[(3 * h + w) % 2]
                G = []
                for gq in range(3):
                    ch = chain[gq]
                    G.append({"us": units[gq * 4: gq * 4 + 4], "gq": gq, "Mt": ch[0], "R": [ch[1], ch[2]],
                              "P": [ch[3], ch[4]], "Y": [ch[5], ch[6]], "Moff": ch[7], "rp": 0, "y": 0})
                gcs = []
                for g in G:
                    gcum = tf.get()
                    gcv = gcum.ap.rearrange("p (a b) -> p a b", a=4)
                    for (j0, n, d, blk0) in runs(g["us"]):
                        TTop("pool", gcv[:, j0:j0 + n, :], C(CUM0 + d).unsqueeze(1).to_broadcast([128, n, 128]),
                             cols(gg, blk0, n, d).unsqueeze(2).to_broadcast([128, n, 128]), ALU.mult,
                             r=[cst.b, gg.b], w=[gcum.b])
                    gcs.append((gcum, gcv))
                yield
                for g, (gcum, gcv) in zip(G, gcs):
                    gq = g["gq"]
                    pg = nextbank()
                    for j in range(4):
                        MM(pg.ap[:, j * 128:(j + 1) * 128], C(ONES), gcv[:, j, :], r=[cst.b, gcum.b], w=[pg.b])
                    eGr = gcum
                    ACT(eGr.ap, pg.ap, AF.Exp, r=[pg.b], w=[eGr.b])
                    nD, DT = dec[gq]
                    pg3 = pg.ap.rearrange("p (a b) -> p a b", a=4)
                    nD3 = nD.ap.rearrange("p (a b) -> p a b", a=4)
                    DT3 = DT.ap.rearrange("p (a b) -> p a b", a=4)
                    for (j0, n, d, blk0) in runs(g["us"]):
                        gcb = cols(Gc, blk0, n, d).unsqueeze(2).to_broadcast([128, n, 128])
                        TTop("dve", nD3[:, j0:j0 + n, :], pg3[:, j0:j0 + n, :], gcb, ALU.subtract,
                             r=[pg.b, Gc.b], w=[nD.b])
                        TTop("pool", DT3[:, j0:j0 + n, :], nD3[:, j0:j0 + n, :],
                             C(MTN0 + d).unsqueeze(1).to_broadcast([128, n, 128]), ALU.add, r=[nD.b, cst.b], w=[DT.b])
                        TTop("pool", nD3[:, j0:j0 + n, :], nD3[:, j0:j0 + n, :],
                             C(MPOS0 + d).unsqueeze(1).to_broadcast([128, n, 128]), ALU.add, r=[nD.b, cst.b], w=[nD.b])
                    for (j0, n, d, blk0) in runs(g["us"]):
                        TTop("pool", ws["Q"].ap[:, gq * 4 + j0:gq * 4 + j0 + n, :].rearrange("p a b -> p (a b)"),
                             qT.ap[:, blk0 * 128:(blk0 + n) * 128], eGr.ap[:, j0 * 128:(j0 + n) * 128],
                             ALU.mult, r=[qT.b, eGr.b], w=[ws["Q"].b])
                yield
                for g in G:
                    gq = g["gq"]
                    us = g["us"]
                    nD, DT = dec[gq]
                    ACT(nD.ap, nD.ap, AF.Exp, r=[nD.b], w=[nD.b], scale=-1.0)
                    ACT(DT.ap, DT.ap, AF.Exp, r=[DT.b], w=[DT.b])
                    pkk = nextbank()
                    pkq = nextbank()
                    for j, (d, s_) in enumerate(us):
                        blk = ORDER[d][s_]
                        ks = kT.ap[:, blk * 128:(blk + 1) * 128]
                        MM(pkk.ap[:, j * 128:(j + 1) * 128], ks, ks, r=[kT.b], w=[pkk.b])
                    for j, (d, s_) in enumerate(us):
                        blk = ORDER[d][s_]
                        ks = kT.ap[:, blk * 128:(blk + 1) * 128]
                        MM(pkq.ap[:, j * 128:(j + 1) * 128], ks, qT.ap[:, blk * 128:(blk + 1) * 128],
                           r=[kT.b, qT.b], w=[pkq.b])
                    Mt = g["Mt"]
                    pk3 = pkk.ap.rearrange("p (a b) -> p a b", a=4)
                    nD3 = nD.ap.rearrange("p (a b) -> p a b", a=4)
                    for (j0, n, d, blk0) in runs(us):
                        TTop("dve", Mt.ap[:, j0:j0 + n, :], pk3[:, j0:j0 + n, :],
                             cols(bt, blk0, n, d).unsqueeze(2).to_broadcast([128, n, 128]), ALU.mult,
                             r=[pkk.b, bt.b], w=[Mt.b])
                    TTop("pool", Mt.ap, Mt.ap, nD3, ALU.mult, r=[Mt.b, nD.b], w=[Mt.b])
                    TTop("dve", ws["AT"].ap[:, gq * 4:gq * 4 + 4, :].rearrange("p a b -> p (a b)"), pkq.ap, DT.ap, ALU.mult,
                         r=[pkq.b, DT.b], w=[ws["AT"].b])
                yield
                for g in G:
                    Mt = g["Mt"]
                    TTop("pool", g["R"][0].ap, Mt.ap, BDB.unsqueeze(1).to_broadcast([128, 4, 128]), ALU.mult,
                         r=[Mt.b, cstb.b], w=[g["R"][0].b])
                    TTop("pool", g["Moff"].ap, Mt.ap, NBDB.unsqueeze(1).to_broadcast([128, 4, 128]), ALU.mult,
                         r=[Mt.b, cstb.b], w=[g["Moff"].b])
                yield
                for g in G:
                    R_, P_, Y_ = g["R"], g["P"], g["Y"]
                    pb = nextbank()
                    for j in range(4):
                        MM(pb.ap[:, j * 128:(j + 1) * 128], R_[0].ap[:, j, :], IDB, r=[R_[0].b, cstb.b], w=[pb.b])
                    CP("act", f2(P_[0]), pb.ap, r=[pb.b], w=[P_[0].b])
                    TTop("dve", Y_[0].ap, IDB.unsqueeze(1).to_broadcast([128, 4, 128]),
                         pb.ap.rearrange("p (a b) -> p a b", a=4), ALU.subtract, r=[cstb.b, pb.b], w=[Y_[0].b])
                yield
                for seg in range(6):
                    for g in G:
                        R_, P_, Y_ = g["R"], g["P"], g["Y"]
                        rp, y = g["rp"], g["y"]
                        if seg >= 1:
                            pc = nextbank()
                            for j in range(4):
                                MM(pc.ap[:, j * 128:(j + 1) * 128], R_[rp].ap[:, j, :], Y_[y].ap[:, j, :],
                                   r=[R_[rp].b, Y_[y].b], w=[pc.b])
                            TTop("dve", f2(Y_[1 - y]), f2(Y_[y]), pc.ap, ALU.add, r=[Y_[y].b, pc.b], w=[Y_[1 - y].b])
                            g["y"] = 1 - y
                        if seg <= 4:
                            pa = nextbank()
                            for j in range(4):
                                MM(pa.ap[:, j * 128:(j + 1) * 128], P_[rp].ap[:, j, :], R_[rp].ap[:, j, :],
                                   r=[P_[rp].b, R_[rp].b], w=[pa.b])
                            CP("act", f2(R_[1 - rp]), pa.ap, r=[pa.b], w=[R_[1 - rp].b])
                            if seg <= 3:
                                pbk = nextbank()
                                for j in range(4):
                                    MM(pbk.ap[:, j * 128:(j + 1) * 128], R_[rp].ap[:, j, :], P_[rp].ap[:, j, :],
                                       r=[P_[rp].b, R_[rp].b], w=[pbk.b])
                                CP("act", f2(P_[1 - rp]), pbk.ap, r=[pbk.b], w=[P_[1 - rp].b])
                            g["rp"] = 1 - rp
                    yield
                for g in G:
                    rp, y = g["rp"], g["y"]
                    g["Yf"], g["YT"], g["Z1"] = g["Y"][y], g["P"][1], g["P"][0]
                    g["ek"], g["kd"], g["Wt"] = g["R"][1 - rp], g["Y"][1 - y], g["R"][rp]
                    Yf, YT, Z1, Moff = g["Yf"], g["YT"], g["Z1"], g["Moff"]
                    pb = nextbank()
                    for j in range(4):
                        MM(pb.ap[:, j * 128:(j + 1) * 128], Yf.ap[:, j, :], IDB, r=[Yf.b, cstb.b], w=[pb.b])
                    CP("act", f2(YT), pb.ap, r=[pb.b], w=[YT.b])
                    pz = nextbank()
                    for j in range(4):
                        MM(pz.ap[:, j * 128:(j + 1) * 128], Moff.ap[:, j, :], Yf.ap[:, j, :], r=[Moff.b, Yf.b], w=[pz.b])
                    CP("act", f2(Z1), pz.ap, r=[pz.b], w=[Z1.b])
                    for (j0, n, d, blk0) in runs(g["us"]):
                        TTop("pool", g["ek"].ap[:, j0:j0 + n, :], k_tm.ap[:, blk0:blk0 + n, :],
                             cols(n2eG, blk0, n, d).unsqueeze(2).to_broadcast([128, n, 128]),
                             ALU.mult, r=[k_tm.b, n2eG.b], w=[g["ek"].b])
                        TTop("pool", g["kd"].ap[:, j0:j0 + n, :], k_tm.ap[:, blk0:blk0 + n, :],
                             cols(kdec, blk0, n, d).unsqueeze(2).to_broadcast([128, n, 128]),
                             ALU.mult, r=[k_tm.b, kdec.b], w=[g["kd"].b])
                yield
                for g in G:
                    Yf, YT, Z1 = g["Yf"], g["YT"], g["Z1"]
                    pz2 = nextbank()
                    for j in range(4):
                        MM(pz2.ap[:, j * 128:(j + 1) * 128], YT.ap[:, j, :], Z1.ap[:, j, :], r=[YT.b, Z1.b], w=[pz2.b])
                    tmpT = tf.get()
                    TTop("dve", tmpT.ap, f2(Yf), pz2.ap, ALU.subtract, r=[Yf.b, pz2.b], w=[tmpT.b])
                    TTs = g["Mt"]
                    tmv = tmpT.ap.rearrange("p (a b) -> p a b", a=4)
                    for (j0, n, d, blk0) in runs(g["us"]):
                        TTop("pool", TTs.ap[:, j0:j0 + n, :], tmv[:, j0:j0 + n, :],
                             cols(hbt, blk0, n, d).unsqueeze(2).to_broadcast([128, n, 128]), ALU.mult,
                             r=[tmpT.b, hbt.b], w=[TTs.b])
                yield
                for g in G:
                    gq = g["gq"]
                    TTs, ek, Wt = g["Mt"], g["ek"], g["Wt"]
                    usl = slice(gq * 4, gq * 4 + 4)
                    pU = nextbank()
                    pW = nextbank()
                    for j, (d, s_) in enumerate(g["us"]):
                        blk = ORDER[d][s_]
                        MM(pU.ap[:, j * 128:(j + 1) * 128], TTs.ap[:, j, :], v_tm.ap[:, blk, :], r=[TTs.b, v_tm.b], w=[pU.b])
                    for j in range(4):
                        MM(pW.ap[:, j * 128:(j + 1) * 128], TTs.ap[:, j, :], ek.ap[:, j, :], r=[TTs.b, ek.b], w=[pW.b])
                    CP("act", ws["Ub"].ap[:, usl, :].rearrange("p a b -> p (a b)"), pU.ap, r=[pU.b], w=[ws["Ub"].b])
                    CP("dve", f2(Wt), pW.ap, r=[pW.b], w=[Wt.b])
                yield
                for g in G:
                    gq = g["gq"]
                    kd, Wt = g["kd"], g["Wt"]
                    usl = slice(gq * 4, gq * 4 + 4)
                    pWT = nextbank()
                    pB = nextbank()
                    pQ = nextbank()
                    for j in range(4):
                        MM(pWT.ap[:, j * 128:(j + 1) * 128], Wt.ap[:, j, :], kd.ap[:, j, :], r=[Wt.b, kd.b], w=[pWT.b])
                    for j in range(4):
                        MM(pB.ap[:, j * 128:(j + 1) * 128], kd.ap[:, j, :], ws["Ub"].ap[:, gq * 4 + j, :],
                           r=[kd.b, ws["Ub"].b], w=[pB.b])
                    for j in range(4):
                        MM(pQ.ap[:, j * 128:(j + 1) * 128], Wt.ap[:, j, :], ws["AT"].ap[:, gq * 4 + j, :],
                           r=[Wt.b, ws["AT"].b], w=[pQ.b])
                    CP("act", ws["nWT"].ap[:, usl, :].rearrange("p a b -> p (a b)"), pWT.ap, r=[pWT.b], w=[ws["nWT"].b])
                    CP("act", ws["Bb"].ap[:, usl, :].rearrange("p a b -> p (a b)"), pB.ap, r=[pB.b], w=[ws["Bb"].b])
                    qv = ws["Q"].ap[:, usl, :].rearrange("p a b -> p (a b)")
                    TTop("dve", qv, qv, pQ.ap, ALU.add, r=[ws["Q"].b, pQ.b], w=[ws["Q"].b])
                yield

            def scan_step(d, s_):
                blk = ORDER[d][s_]
                ws = WS[(3 * h + s_ // 6) % 2]
                u = wave_units(s_ // 6).index((d, s_))
                cur = s_ % 2
                nxt = 1 - cur
                pst_ = banks[d]
                MM(pst_.ap[:, 0:128], ws["nWT"].ap[:, u, :], Sb[d][cur].ap, start=True, stop=False,
                   r=[ws["nWT"].b, Sb[d][cur].b], w=[pst_.b])
                MM(pst_.ap[:, 0:128], IDB, ws["Bb"].ap[:, u, :], start=False, stop=True,
                   r=[cstb.b, ws["Bb"].b], w=[pst_.b])
                STT("dve", Sb[d][nxt].ap, S32[d][cur].ap, col(eGl, blk, d), pst_.ap[:, 0:128], ALU.mult, ALU.add,
                    r=[S32[d][cur].b, eGl.b, pst_.b], w=[Sb[d][nxt].b])
                STT("dve", S32[d][nxt].ap, S32[d][cur].ap, col(eGl, blk, d), pst_.ap[:, 0:128], ALU.mult, ALU.add,
                    r=[S32[d][cur].b, eGl.b, pst_.b], w=[S32[d][nxt].b])
                if blk >= 2:
                    li = blk - 2
                    po = banks[2]
                    MM(po.ap[:, d * 128:(d + 1) * 128], ws["Q"].ap[:, u, :], Sb[d][cur].ap, start=True, stop=False,
                       r=[ws["Q"].b, Sb[d][cur].b], w=[po.b])
                    MM(po.ap[:, d * 128:(d + 1) * 128], ws["AT"].ap[:, u, :], ws["Ub"].ap[:, u, :], start=False, stop=True,
                       r=[ws["AT"].b, ws["Ub"].b], w=[po.b])
                    first = (d == 0 and li < 8) or (d == 1 and li >= 8)
                    if first:
                        CP("act", o_acc.ap[:, li, :], po.ap[:, d * 128:(d + 1) * 128], r=[po.b], w=[o_acc.b])
                    else:
                        TTop("dve", o_acc.ap[:, li, :], o_acc.ap[:, li, :], po.ap[:, d * 128:(d + 1) * 128], ALU.add,
                             r=[po.b, o_acc.b], w=[o_acc.b])

            def drain(gen):
                for _ in gen:
                    pass

            def finish_gen():
                S.op("pool", lambda e: e.memset(ssq.ap, 0.0), (), [ssq.b])
                for li in range(16):
                    junk = tb16.get()
                    S.op("act", lambda e, li=li, junk=junk: e.activation(out=junk.ap[:, 0:128], in_=o_acc.ap[:, li, :],
                                                                          func=AF.Square, accum_out=ssq.ap[:, li:li + 1]),
                         [o_acc.b], [junk.b, ssq.b])
                ACT(e_rs.ap[:, 4:8], kc.ap[:, 0:4], AF.Copy, r=[kc.b, ssq.b], w=[e_rs.b, ssq.b])
                ACT(rstd.ap, ssq.ap, AF.Identity, r=[ssq.b, kc.b], w=[rstd.b], bias=KC("rmseps"), scale=1.0 / 128.0)
                TTop("pool", rstd.ap, rstd.ap, KC("mhalf").to_broadcast([128, 16]), ALU.pow, r=[rstd.b, kc.b], w=[rstd.b])
                for li in range(16):
                    ACT(on_all.ap[:, li, :], o_acc.ap[:, li, :], AF.Copy, r=[o_acc.b, rstd.b], w=[on_all.b],
                        scale=rstd.ap[:, li:li + 1])
                if b == 0 and h == 0:
                    dbg("o_acc", o_acc, [128, 16, 128])
                yield
                for g0 in range(0, 16, 4):
                    pb = nextbank()
                    for j in range(4):
                        MM(pb.ap[:, j * 128:(j + 1) * 128], on_all.ap[:, g0 + j, :], IDB, r=[on_all.b, cstb.b], w=[pb.b])
                    ti = g0 // 4
                    ogt = tb16.get()
                    STT("dve", ogt.ap, pb.ap, onwc.ap[:, 0:1], zs2T.ap[:, ti * 512:(ti + 1) * 512], ALU.mult, ALU.mult,
                        r=[pb.b, onwc.b, zs2T.b], w=[ogt.b])
                    DMA(og_scr[h, :, ti * 512:(ti + 1) * 512], ogt.ap, r=[ogt.b], w=[og_buf[h][ti]])
                    if b == 0 and h == 0:
                        dbg("og0_%d" % ti, ogt, [128, 512], BF16)
                    yield

            return {"prep": wave_prep, "scan": scan_step, "fin": finish_gen, "init": init}

        nheads = NH if stage >= 4 else 1
        en = early_gen(0)
        for _ in en:
            pass
        if stage >= 2:
            ctx = [head_ctx(h) for h in range(nheads)]
            seq = [(h, w) for h in range(nheads) for w in range(3)]
            st = {"en": early_gen(1) if nheads > 1 else None, "fin": None}

            def side(n=1):
                for _ in range(n):
                    if st["fin"] is not None:
                        try:
                            next(st["fin"])
                        except StopIteration:
                            st["fin"] = None
                    elif st["en"] is not None:
                        try:
                            next(st["en"])
                        except StopIteration:
                            st["en"] = None
                    st["n"] = st.get("n", 0) + 1
                    if b == 0 and st["n"] % 2 == 0:
                        if cast_head:
                            emit_casts(cast_head, 1)
                        else:
                            emit_casts(cast_rest, 1)

            rr_lo[0] = 3
            cur = ctx[0]["prep"](0)
            for _ in cur:
                side()
            for i, (h, w) in enumerate(seq):
                if w == 0:
                    ctx[h]["init"]()
                nxt = None
                if i + 1 < len(seq):
                    h2, w2 = seq[i + 1]
                    if w2 == 0:
                        while st["fin"] is not None or st["en"] is not None:
                            side()
                        if h + 2 < nheads:
                            st["en"] = early_gen(h + 2)
                    nxt = ctx[h2]["prep"](w2)
                for s_ in range(6 * w, 6 * w + 6):
                    for d in range(2):
                        ctx[h]["scan"](d, s_)
                        side()
                        adv(nxt, 1)
                        side()
                        if (2 * s_ + d) % 3 == 2:
                            adv(nxt, 1)
                if nxt is not None:
                    for _ in nxt:
                        side()
                if w == 2:
                    st["fin"] = ctx[h]["fin"]()
                    next(st["fin"])
            while st["fin"] is not None or st["en"] is not None:
                side()
            rr_lo[0] = 0
        emit_casts(cast_head, 100)
        emit_casts(cast_rest, 1000)
        S.barrier()
        AR.top = m_phase
        if stage < 4:
            AR.top = mb
            return

        ymT = [[AR.alloc([512], BF16, "ym%d_%d" % (fc, t)) for t in range(4)] for fc in range(8)]
        m_B = AR.top
        pT = [[AR.alloc([512], BF16, "p%d_%d" % (fc, t)) for t in range(4)] for fc in range(8)]
        tf = Pool(AR, 8, [512], F32, "tfB")

        def wload(sc, sbufs, c):
            wt = wtp.get()
            DMA(wt.ap.rearrange("p a b -> p (a b)"), sc[c * 128:(c + 1) * 128, :], r=[sbufs[c]], w=[wt.b])
            return wt

        for fc in range(8):
            wxb = wload(sc_in, sb_in, 32 + fc)
            wbgt = wload(sc_in, sb_in, 40 + fc)
            wcg = wload(sc_in, sb_in, 48 + fc)
            for t in range(4):
                ti = t + 1
                pbs = []
                for wt in (wxb, wbgt, wcg):
                    pb = nextbank()
                    for k in range(8):
                        MM(pb.ap, wt.ap[:, k, :], uT[ti].ap[:, k, :], start=(k == 0), stop=(k == 7),
                           r=[wt.b, uT[ti].b], w=[pb.b])
                    pbs.append(pb)
                cgs = tf.get()
                CP("act", cgs.ap, pbs[2].ap, r=[pbs[2].b], w=[cgs.b])
                cgx = tf.get()
                TTop("dve", cgx.ap, pbs[0].ap, cgs.ap, ALU.mult, r=[pbs[0].b, cgs.b], w=[cgx.b])
                ct = tf.get()
                TTop("pool", ct.ap, cgx.ap, cwb.ap[:, fc, 1:2].to_broadcast([128, 512]), ALU.mult,
                     r=[cgx.b, cwb.b], w=[ct.b])
                xv = cgx.ap.rearrange("p (r l) -> p r l", r=8)
                cv = ct.ap.rearrange("p (r l) -> p r l", r=8)
                STT("dve", cv[:, :, 1:64], xv[:, :, 0:63], cwb.ap[:, fc, 0:1], cv[:, :, 1:64], ALU.mult, ALU.add,
                    r=[cgx.b, cwb.b, ct.b], w=[ct.b])
                STT("dve", cv[:, :, 0:63], xv[:, :, 1:64], cwb.ap[:, fc, 2:3], cv[:, :, 0:63], ALU.mult, ALU.add,
                    r=[cgx.b, cwb.b, ct.b], w=[ct.b])
                TTop("dve", pT[fc][t].ap, pbs[1].ap, ct.ap, ALU.mult, r=[pbs[1].b, ct.b], w=[pT[fc][t].b])
        if b == 0:
            dbg("pT0_0", pT[0][0], [128, 512], BF16)
        ogp = Pool(AR, 2, [8, 512], BF16, "ogp")
        for t in range(4):
            ti = t + 1
            ogt = ogp.get()
            DMA(ogt.ap, og_scr[:, :, t * 512:(t + 1) * 512].rearrange("k p c -> p k c"),
                r=[og_buf[k][t] for k in range(NH)], w=[ogt.b])
            for fc in range(8):
                wga = wload(sc_in, sb_in, 56 + fc)
                wgb = wload(sc_in, sb_in, 64 + fc)
                wa = wload(sc_a, sb_a, fc)
                wb_ = wload(sc_b, sb_b, fc)
                pga = nextbank(); pgb = nextbank(); pya = nextbank(); pyb = nextbank()
                for k in range(8):
                    MM(pga.ap, wga.ap[:, k, :], uT[ti].ap[:, k, :], start=(k == 0), stop=(k == 7),
                       r=[wga.b, uT[ti].b], w=[pga.b])
                for k in range(8):
                    MM(pgb.ap, wgb.ap[:, k, :], uT[ti].ap[:, k, :], start=(k == 0), stop=(k == 7),
                       r=[wgb.b, uT[ti].b], w=[pgb.b])
                for k in range(8):
                    MM(pya.ap, wa.ap[:, k, :], ogt.ap[:, k, :], start=(k == 0), stop=(k == 7),
                       r=[wa.b, ogt.b], w=[pya.b])
                for k in range(8):
                    MM(pyb.ap, wb_.ap[:, k, :], pT[k][t].ap, start=(k == 0), stop=(k == 7),
                       r=[wb_.b, pT[k][t].b], w=[pyb.b])
                ta = tf.get(); tb_ = tf.get()
                ACT(ta.ap, pga.ap, AF.Tanh, r=[pga.b], w=[ta.b], scale=0.5)
                ACT(tb_.ap, pgb.ap, AF.Tanh, r=[pgb.b], w=[tb_.b], scale=0.5)
                STT("dve", ta.ap, ta.ap, 1.0, pya.ap, ALU.add, ALU.mult, r=[ta.b, pya.b], w=[ta.b])
                STT("dve", tb_.ap, tb_.ap, 1.0, pyb.ap, ALU.add, ALU.mult, r=[tb_.b, pyb.b], w=[tb_.b])
                TTop("pool", ymT[fc][t].ap, ta.ap, tb_.ap, ALU.add, r=[ta.b, tb_.b], w=[ymT[fc][t].b])
        if b == 0:
            dbg("ym0_0", ymT[0][0], [128, 512], BF16)
        S.barrier()
        AR.top = m_B
        if stage < 5:
            AR.top = mb
            return

        m_main = AR.top
        AR.top = mb
        hsqT = AR.alloc([32, 512], BF16, "hsqT")
        assert AR.top <= m_phase, (AR.top, m_phase)
        AR.top = m_main
        xa = AR.alloc([8, 512], F32, "xa")
        outT = AR.alloc([8, 512], F32, "outT")
        u2T = AR.alloc([8, 512], BF16, "u2T")
        tf = Pool(AR, 8, [512], F32, "tfC")
        w2p = Pool(AR, 2, [32, 128], BF16, "w2p")
        xs = Pool(AR, 6, [1024], F32, "xsC")

        sq_all = AR.alloc([8, 512], BF16, "sq_all")
        ln_dg = AR.alloc([8, 128], F32, "ln_dg")
        ln_A = AR.alloc([512], F32, "ln_A")
        ln_B = AR.alloc([512], F32, "ln_B")
        ln_mean = AR.alloc([4], F32, "ln_mean")
        ln_msq = AR.alloc([4], F32, "ln_msq")
        ln_var = AR.alloc([4], F32, "ln_var")
        ln_nmr = AR.alloc([4], F32, "ln_nmr")

        def layer_norm(src, dst_fns):
            for fc in range(8):
                ACT(sq_all.ap[:, fc, :], src.ap[:, fc, :], AF.Square, r=[src.b], w=[sq_all.b])
            pst = nextbank()
            for j in range(4):
                for fc in range(8):
                    MM(pst.ap[:, 2 * j:2 * j + 1], src.ap[:, fc, j * 128:(j + 1) * 128], C(ONES)[:, 0:1],
                       start=(fc == 0), stop=(fc == 7), r=[src.b, cst.b], w=[pst.b])
                for fc in range(8):
                    MM(pst.ap[:, 2 * j + 1:2 * j + 2], sq_all.ap[:, fc, j * 128:(j + 1) * 128], ONB[:, 0:1],
                       start=(fc == 0), stop=(fc == 7), r=[sq_all.b, cstb.b], w=[pst.b])
            pv = pst.ap[:, 0:8].rearrange("p (a b) -> p a b", a=4)
            ACT(ln_mean.ap, pv[:, :, 0], AF.Copy, r=[pst.b], w=[ln_mean.b], scale=1.0 / D)
            TTop("dve", ln_msq.ap, ln_mean.ap, ln_mean.ap, ALU.mult, r=[ln_mean.b], w=[ln_msq.b])
            STT("dve", ln_var.ap, pv[:, :, 1], 1.0 / D, ln_msq.ap, ALU.mult, ALU.subtract,
                r=[pst.b, ln_msq.b], w=[ln_var.b])
            ACT(ln_var.ap, ln_var.ap, AF.Identity, r=[ln_var.b, kc.b], w=[ln_var.b], bias=KC("lneps"), scale=1.0)
            TTop("pool", ln_var.ap, ln_var.ap, KC("mhalf").to_broadcast([128, 4]), ALU.pow,
                 r=[ln_var.b, kc.b], w=[ln_var.b])
            STT("dve", ln_nmr.ap, ln_mean.ap, -1.0, ln_var.ap, ALU.mult, ALU.mult,
                r=[ln_mean.b, ln_var.b], w=[ln_nmr.b])
            idb = C(IDENT).unsqueeze(1).to_broadcast([128, 4, 128])
            TTop("pool", ln_dg.ap[:, 0:4, :], idb, ln_var.ap.unsqueeze(2).to_broadcast([128, 4, 128]), ALU.mult,
                 r=[cst.b, ln_var.b], w=[ln_dg.b])
            TTop("pool", ln_dg.ap[:, 4:8, :], idb, ln_nmr.ap.unsqueeze(2).to_broadcast([128, 4, 128]), ALU.mult,
                 r=[cst.b, ln_nmr.b], w=[ln_dg.b])
            pA = nextbank()
            pB = nextbank()
            for j in range(4):
                MM(pA.ap[:, j * 128:(j + 1) * 128], C(ONES), ln_dg.ap[:, j, :], r=[cst.b, ln_dg.b], w=[pA.b])
            for j in range(4):
                MM(pB.ap[:, j * 128:(j + 1) * 128], C(ONES), ln_dg.ap[:, 4 + j, :], r=[cst.b, ln_dg.b], w=[pB.b])
            CP("act", ln_A.ap, pA.ap, r=[pA.b], w=[ln_A.b])
            CP("act", ln_B.ap, pB.ap, r=[pB.b], w=[ln_B.b])
            for fc in range(8):
                xc = tf.get()
                TTop("dve", xc.ap, src.ap[:, fc, :], ln_A.ap, ALU.mult, r=[src.b, ln_A.b], w=[xc.b])
                TTop("pool", xc.ap, xc.ap, ln_B.ap, ALU.add, r=[xc.b, ln_B.b], w=[xc.b])
                for (oap, sc_, bi_, rb, wbuf) in dst_fns:
                    ACT(oap(fc), xc.ap, AF.Identity, r=[xc.b] + rb, w=[wbuf], bias=bi_(fc), scale=sc_(fc))

        wo_t = [None] * 8
        for t in range(4):
            ti = t + 1
            blks = []
            for j in range(4):
                s = xs.get()
                DMA(s.ap, x_d[b, t * 512 + j * 128: t * 512 + (j + 1) * 128, :], w=[s.b])
                blks.append(s)
            for fc in range(8):
                pb = nextbank()
                for j in range(4):
                    MM(pb.ap[:, j * 128:(j + 1) * 128], blks[j].ap[:, fc * 128:(fc + 1) * 128], C(IDENT),
                       r=[blks[j].b, cst.b], w=[pb.b])
                ACT(xa.ap[:, fc, :], pb.ap, AF.Copy, r=[pb.b], w=[xa.b], scale=ALPHA)
            for fc in range(8):
                wo = wload(sc_o, sb_o, fc)
                pb = nextbank()
                for k in range(8):
                    MM(pb.ap, wo.ap[:, k, :], ymT[k][t].ap, start=(k == 0), stop=(k == 7), r=[wo.b, ymT[k][t].b], w=[pb.b])
                STT("dve", xa.ap[:, fc, :], pb.ap, hg1.ap[:, fc, b:b + 1], xa.ap[:, fc, :], ALU.mult, ALU.add,
                    r=[pb.b, hg1.b, xa.b], w=[xa.b])
            if b == 0 and t == 0:
                dbg("pre1", xa, [128, 8, 512])
            layer_norm(xa, [
                (lambda fc: outT.ap[:, fc, :], lambda fc: A1s.ap[:, fc:fc + 1], lambda fc: A1b.ap[:, fc, b:b + 1],
                 [A1s.b, A1b.b], outT.b),
                (lambda fc: u2T.ap[:, fc, :], lambda fc: U2s.ap[:, fc, b:b + 1], lambda fc: U2b.ap[:, fc, b:b + 1],
                 [U2s.b, U2b.b], u2T.b)])
            if b == 0 and t == 0:
                dbg("x1a", outT, [128, 8, 512]); dbg("u2T", u2T, [128, 8, 512], BF16)
            for ffc in range(32):
                w1 = wload(sc_f1, sb_f1, ffc)
                pb = nextbank()
                for k in range(8):
                    MM(pb.ap, w1.ap[:, k, :], u2T.ap[:, k, :], start=(k == 0), stop=(k == 7), r=[w1.b, u2T.b], w=[pb.b])
                rl = tf.get()
                ACT(rl.ap, pb.ap, AF.Relu, r=[pb.b, bf1.b], w=[rl.b], bias=bf1.ap[:, ffc:ffc + 1], scale=1.0)
                STT("dve", hsqT.ap[:, ffc, :], pb.ap, bf1.ap[:, ffc:ffc + 1], rl.ap, ALU.add, ALU.mult,
                    r=[pb.b, bf1.b, rl.b], w=[hsqT.b])
            for fc in range(8):
                w2 = w2p.get()
                DMA(w2.ap.rearrange("p a b -> p (a b)"), sc_f2[fc * 128:(fc + 1) * 128, :], r=[sb_f2[fc]], w=[w2.b])
                pb = nextbank()
                for k in range(32):
                    MM(pb.ap, w2.ap[:, k, :], hsqT.ap[:, k, :], start=(k == 0), stop=(k == 31), r=[w2.b, hsqT.b], w=[pb.b])
                STT("dve", outT.ap[:, fc, :], pb.ap, modT.ap[:, 40 + fc, b:b + 1], outT.ap[:, fc, :], ALU.mult, ALU.add,
                    r=[pb.b, modT.b, outT.b], w=[outT.b])
            if b == 0 and t == 0:
                dbg("pre2", outT, [128, 8, 512])
            layer_norm(outT, [
                (lambda fc: xa.ap[:, fc, :], lambda fc: ln2g.ap[:, fc:fc + 1], lambda fc: ln2b.ap[:, fc:fc + 1],
                 [ln2g.b, ln2b.b], xa.b)])
            if b == 0 and t == 0:
                dbg("xo", xa, [128, 8, 512])
            for j in range(4):
                os_ = xs.get()
                for half in range(2):
                    pb = nextbank()
                    for q in range(4):
                        fc = half * 4 + q
                        MM(pb.ap[:, q * 128:(q + 1) * 128], xa.ap[:, fc, j * 128:(j + 1) * 128], C(IDENT),
                           r=[xa.b, cst.b], w=[pb.b])
                    CP("act" if half == 0 else "dve", os_.ap[:, half * 512:(half + 1) * 512], pb.ap, r=[pb.b], w=[os_.b])
                DMA(out_d[b, t * 512 + j * 128: t * 512 + (j + 1) * 128, :], os_.ap, r=[os_.b])
        S.barrier()
        AR.top = mb

    for b in range(nbatch):
        batch_program(b)

    S.run()
    es.close()
    return nc, dbg_outs


_CACHE = {}


def _consts():
    i = np.arange(128)[:, None]
    j = np.arange(128)[None, :]
    c = np.zeros((NCONST, 128, 128), np.float32)
    c[0] = (i == j)
    c[1] = 1.0
    c[2] = (i <= j)
    c[3] = (i >= j)
    c[4] = np.where(j < i, 0.0, BIG)
    c[5] = np.where(j > i, 0.0, BIG)
    c[6] = np.where(j >= i, 0.0, -BIG)
    c[7] = np.where(j <= i, 0.0, -BIG)
    bd = ((i // 64) == (j // 64)).astype(np.float32)
    c[8] = bd
    c[9] = 1.0 - bd
    return np.ascontiguousarray(c.transpose(1, 0, 2).reshape(128, NCONST * 128))


def _pf(v, n):
    return np.ascontiguousarray(np.asarray(v, np.float32).reshape(n, 128).T)


def prepare_inputs(inputs, core):
    f = lambda k: np.ascontiguousarray(np.asarray(inputs[k], np.float32))
    b0 = 2 * core
    cc = np.stack([f("c")[b0], f("c")[b0 + 1], f("c_ctx")], axis=0)
    cT = np.ascontiguousarray(cc.reshape(3, 8, 128).transpose(2, 1, 0).reshape(128, 24))
    cq = f("conv_qkv")[0]
    cwq = np.ascontiguousarray(cq.reshape(3, 24, 128).transpose(2, 1, 0).reshape(128, 72))
    cb = f("conv_b")[0]
    cwb = np.ascontiguousarray(cb.reshape(3, 8, 128).transpose(2, 1, 0).reshape(128, 24))
    m = {
        "x": np.ascontiguousarray(f("x")[b0:b0 + 2]),
        "ctx": np.ascontiguousarray(f("ctx")[b0:b0 + 2]),
        "cT": cT,
        "w_ada": f("w_ada")[0],
        "b_adaT": _pf(f("b_ada")[0], 48),
        "w_in": f("w_in")[0],
        "cwq": cwq,
        "alog_bc": np.ascontiguousarray(np.broadcast_to(f("a_log")[0].reshape(1, 16), (128, 16))),
        "dtb_bc": np.ascontiguousarray(np.broadcast_to(f("dt_bias")[0].reshape(1, 16), (128, 16))),
        "onw_col": np.ascontiguousarray(f("o_norm_w")[0].reshape(128, 1)),
        "w_a_out": f("w_a_out")[0],
        "cwb": cwb,
        "w_b_out": f("w_b_out")[0],
        "w_o": f("w_o")[0],
        "ln1gT": _pf(f("ln1_g")[0], 8),
        "ln1bT": _pf(f("ln1_b")[0], 8),
        "w_ff1": f("w_ff1")[0],
        "b_ff1T": _pf(f("b_ff1")[0], 32),
        "w_ff2": f("w_ff2")[0],
        "b_ff2T": _pf(f("b_ff2")[0], 8),
        "ln2gT": _pf(f("ln2_g")[0], 8),
        "ln2bT": _pf(f("ln2_b")[0], 8),
        "consts": _consts(),
    }
    return m


def kernel(**inputs):
    if "nc" not in _CACHE:
        _CACHE["nc"] = build_program()[0]
    nc = _CACHE["nc"]
    n = 8
    shared = None
    in_maps = []
    for c in range(n):
        m = prepare_inputs(inputs, c)
        if shared is None:
            shared = m
        else:
            for k in m:
                if k not in ("x", "ctx", "cT"):
                    m[k] = shared[k]
        in_maps.append(m)
    res = run_bass_kernel_spmd(nc, in_maps, core_ids=list(range(n)))
    out = np.concatenate([np.asarray(r["out"], np.float32) for r in res.results], axis=0)
    return out
```
